# Optimizing a Trainium2 kernel written in Bass

```python
import math
import jax, jax.numpy as jnp
from jax import lax
import numpy as np

D_MODEL = 1024
BATCH = 4
SEQ = 8192
DEPTH = 1

D_SSM = 512
SSM_GROUP = 16
N_SSM_GROUPS = D_SSM // SSM_GROUP
SSM_STATE = 64
N_ATT_HEADS = 8
ATT_HEAD_DIM = 64
D_ATT = N_ATT_HEADS * ATT_HEAD_DIM
D_MIX = D_SSM + D_ATT
D_IN_PROJ = D_SSM + 3 * D_ATT
MOBA_BLOCK = 256
MOBA_TOPK = 3
Q_CHUNK = 32
PEER_HEADS = 8
PEER_KEY_DIM = 256
PEER_HALF = PEER_KEY_DIM // 2
PEER_N_KEYS = 128
PEER_N_EXPERTS = PEER_N_KEYS * PEER_N_KEYS
PEER_TOPK = 16
TOKEN_CHUNK = 128
LN_EPS = 1e-5
DN_ALPHA = (2.0 * DEPTH) ** 0.25
DN_BETA = (8.0 * DEPTH) ** -0.25
NEG = -1e30

kernel_name = "hymba_s5_moba_peer_deepnorm"


def _layernorm(x, g, b):
    xf = x.astype(jnp.float32)
    mu = jnp.mean(xf, axis=-1, keepdims=True)
    var = jnp.mean(jnp.square(xf - mu), axis=-1, keepdims=True)
    y = (xf - mu) * lax.rsqrt(var + LN_EPS) * g.astype(jnp.float32) + b.astype(jnp.float32)
    return y.astype(x.dtype)


def _diag_op(e1, e2):
    a1, b1 = e1
    a2, b2 = e2
    return a1 * a2, a2 * b1 + b2


def _s5(u, a_re, a_im, log_dt, b_re, b_im, c_re, c_im, d_skip, w_glu):
    bsz, L, _ = u.shape
    uf = u.astype(jnp.float32).reshape(bsz, L, N_SSM_GROUPS, SSM_GROUP)
    lam = lax.complex(a_re.astype(jnp.float32), a_im.astype(jnp.float32))
    dt = jnp.exp(log_dt.astype(jnp.float32))[:, None]
    lam_bar = jnp.exp(lam * dt)
    b_c = lax.complex(b_re.astype(jnp.float32), b_im.astype(jnp.float32))
    b_bar = ((lam_bar - 1.0) / lam)[:, :, None] * b_c
    bu = jnp.einsum('blgh,gph->blgp', uf.astype(jnp.complex64), b_bar)
    a = jnp.broadcast_to(lam_bar, (1, L) + lam_bar.shape)
    _, states = lax.associative_scan(_diag_op, (a, bu), axis=1)
    c_c = lax.complex(c_re.astype(jnp.float32), c_im.astype(jnp.float32))
    y = jnp.real(jnp.einsum('blgp,ghp->blgh', states, c_c)) + d_skip.astype(jnp.float32) * uf
    y = jax.nn.gelu(y.reshape(bsz, L, D_SSM))
    y = y * jax.nn.sigmoid(y @ w_glu.astype(jnp.float32))
    return y.astype(u.dtype)


def _moba(q, k, v):
    bsz, L, _ = q.shape
    n_blk = -(-L // MOBA_BLOCK)
    lp = n_blk * MOBA_BLOCK
    pad = lp - L

    def heads(t):
        t = jnp.pad(t, ((0, 0), (0, pad), (0, 0)))
        return t.reshape(bsz, lp, N_ATT_HEADS, ATT_HEAD_DIM).transpose(0, 2, 1, 3)

    qh, kh, vh = heads(q), heads(k), heads(v)
    kb = kh.reshape(bsz, N_ATT_HEADS, n_blk, MOBA_BLOCK, ATT_HEAD_DIM)
    vb = vh.reshape(bsz, N_ATT_HEADS, n_blk, MOBA_BLOCK, ATT_HEAD_DIM)
    kmean = jnp.mean(kb.astype(jnp.float32), axis=3)
    n_sel = min(MOBA_TOPK, n_blk)
    n_chunks = lp // Q_CHUNK
    qc = qh.reshape(bsz, N_ATT_HEADS, n_chunks, Q_CHUNK, ATT_HEAD_DIM).transpose(2, 0, 1, 3, 4)
    bi = jnp.arange(bsz)[:, None, None, None]
    hi = jnp.arange(N_ATT_HEADS)[None, :, None, None]
    blk_ids = jnp.arange(n_blk)
    scale = ATT_HEAD_DIM ** -0.5

    def chunk(args):
        c, qq = args
        start = c * Q_CHUNK
        own = start // MOBA_BLOCK
        q_pos = start + jnp.arange(Q_CHUNK)
        qf = qq.astype(jnp.float32)
        gate = jnp.einsum('bhqd,bhnd->bhqn', qf, kmean)
        gate = jnp.where(blk_ids < own, gate, NEG)
        _, sel = lax.top_k(gate, n_sel)
        valid = sel < own
        ks = kb[bi, hi, sel].astype(jnp.float32)
        vs = vb[bi, hi, sel].astype(jnp.float32)
        s_sel = jnp.einsum('bhqd,bhqskd->bhqsk', qf, ks) * scale
        s_sel = jnp.where(valid[..., None], s_sel, NEG)
        s_sel = s_sel.reshape(bsz, N_ATT_HEADS, Q_CHUNK, n_sel * MOBA_BLOCK)
        ko = lax.dynamic_index_in_dim(kb, own, axis=2, keepdims=False).astype(jnp.float32)
        vo = lax.dynamic_index_in_dim(vb, own, axis=2, keepdims=False).astype(jnp.float32)
        k_pos = own * MOBA_BLOCK + jnp.arange(MOBA_BLOCK)
        s_own = jnp.einsum('bhqd,bhkd->bhqk', qf, ko) * scale
        s_own = jnp.where(k_pos[None, :] <= q_pos[:, None], s_own, NEG)
        p = jax.nn.softmax(jnp.concatenate([s_sel, s_own], axis=-1), axis=-1)
        p_sel = p[..., :n_sel * MOBA_BLOCK].reshape(bsz, N_ATT_HEADS, Q_CHUNK, n_sel, MOBA_BLOCK)
        p_own = p[..., n_sel * MOBA_BLOCK:]
        o = jnp.einsum('bhqsk,bhqskd->bhqd', p_sel, vs) + jnp.einsum('bhqk,bhkd->bhqd', p_own, vo)
        return o.astype(qq.dtype)

    out = lax.map(chunk, (jnp.arange(n_chunks), qc))
    out = out.transpose(1, 2, 0, 3, 4).reshape(bsz, N_ATT_HEADS, lp, ATT_HEAD_DIM)[:, :, :L]
    return out.transpose(0, 2, 1, 3).reshape(bsz, L, D_ATT)


def _hybrid_mixer(h, w_in, a_re, a_im, log_dt, b_re, b_im, c_re, c_im, d_skip, w_glu, w_out):
    proj = h @ w_in
    u = proj[..., :D_SSM]
    q = proj[..., D_SSM:D_SSM + D_ATT]
    k = proj[..., D_SSM + D_ATT:D_SSM + 2 * D_ATT]
    v = proj[..., D_SSM + 2 * D_ATT:]
    y_ssm = _s5(u, a_re, a_im, log_dt, b_re, b_im, c_re, c_im, d_skip, w_glu)
    y_att = _moba(q, k, v)
    return jnp.concatenate([y_ssm, y_att], axis=-1) @ w_out


def _peer(h, w_q, sub_keys, expert_u, expert_v):
    bsz, L, d = h.shape
    n_tok = bsz * L
    xt = h.reshape(n_tok // TOKEN_CHUNK, TOKEN_CHUNK, d)

    def chunk(xc):
        q = (xc @ w_q).astype(jnp.float32).reshape(TOKEN_CHUNK, PEER_HEADS, 2, PEER_HALF)
        s = jnp.einsum('thcd,hcnd->thcn', q, sub_keys.astype(jnp.float32))
        s1, i1 = lax.top_k(s[:, :, 0], PEER_TOPK)
        s2, i2 = lax.top_k(s[:, :, 1], PEER_TOPK)
        cand = (s1[..., :, None] + s2[..., None, :]).reshape(TOKEN_CHUNK, PEER_HEADS, PEER_TOPK * PEER_TOPK)
        cidx = (i1[..., :, None] * PEER_N_KEYS + i2[..., None, :]).reshape(TOKEN_CHUNK, PEER_HEADS, PEER_TOPK * PEER_TOPK)
        top, pos = lax.top_k(cand, PEER_TOPK)
        eidx = jnp.take_along_axis(cidx, pos, axis=-1)
        g = jax.nn.softmax(top, axis=-1)
        u = expert_u[eidx].astype(jnp.float32)
        act = jax.nn.gelu(jnp.einsum('td,thkd->thk', xc.astype(jnp.float32), u))
        out = jnp.einsum('thk,thkd->td', g * act, expert_v[eidx].astype(jnp.float32))
        return out.astype(xc.dtype)

    return lax.map(chunk, xt).reshape(bsz, L, d)


def setup_inputs(seed: int = 0) -> dict:
    key = jax.random.key(seed)
    ks = jax.random.split(key, 20)
    f32 = jnp.float32
    G, P, H = N_SSM_GROUPS, SSM_STATE, SSM_GROUP
    x = jax.random.normal(ks[0], (BATCH, SEQ, D_MODEL), f32)
    w_in = jax.random.normal(ks[1], (DEPTH, D_MODEL, D_IN_PROJ), f32) * D_MODEL ** -0.5
    ssm_a_re = -0.5 + 0.01 * jax.random.normal(ks[2], (DEPTH, G, P), f32)
    ssm_a_im = jnp.pi * jnp.arange(P, dtype=f32)[None, None, :] + 0.01 * jax.random.normal(ks[3], (DEPTH, G, P), f32)
    ssm_log_dt = jax.random.uniform(ks[4], (DEPTH, G), f32, math.log(1e-3), math.log(1e-1))
    ssm_b_re = jax.random.normal(ks[5], (DEPTH, G, P, H), f32) * (2.0 * H) ** -0.5
    ssm_b_im = jax.random.normal(ks[6], (DEPTH, G, P, H), f32) * (2.0 * H) ** -0.5
    ssm_c_re = jax.random.normal(ks[7], (DEPTH, G, H, P), f32) * (2.0 * P) ** -0.5
    ssm_c_im = jax.random.normal(ks[8], (DEPTH, G, H, P), f32) * (2.0 * P) ** -0.5
    ssm_d = jax.random.normal(ks[9], (DEPTH, G, H), f32)
    ssm_w_glu = jax.random.normal(ks[10], (DEPTH, D_SSM, D_SSM), f32) * D_SSM ** -0.5
    w_out = jax.random.normal(ks[11], (DEPTH, D_MIX, D_MODEL), f32) * (D_MIX ** -0.5) * DN_BETA
    ln1_g = 1.0 + 0.02 * jax.random.normal(ks[12], (DEPTH, D_MODEL), f32)
    ln1_b = 0.02 * jax.random.normal(ks[13], (DEPTH, D_MODEL), f32)
    peer_w_q = jax.random.normal(ks[14], (DEPTH, D_MODEL, PEER_HEADS * PEER_KEY_DIM), f32) * D_MODEL ** -0.5
    peer_sub_keys = jax.random.normal(ks[15], (DEPTH, PEER_HEADS, 2, PEER_N_KEYS, PEER_HALF), f32) * PEER_HALF ** -0.5
    peer_u = jax.random.normal(ks[16], (DEPTH, PEER_N_EXPERTS, D_MODEL), f32) * D_MODEL ** -0.5
    peer_v = jax.random.normal(ks[17], (DEPTH, PEER_N_EXPERTS, D_MODEL), f32) * DN_BETA * PEER_HEADS ** -0.5
    ln2_g = 1.0 + 0.02 * jax.random.normal(ks[18], (DEPTH, D_MODEL), f32)
    ln2_b = 0.02 * jax.random.normal(ks[19], (DEPTH, D_MODEL), f32)
    return {"x": x, "w_in": w_in, "ssm_a_re": ssm_a_re, "ssm_a_im": ssm_a_im,
            "ssm_log_dt": ssm_log_dt, "ssm_b_re": ssm_b_re, "ssm_b_im": ssm_b_im,
            "ssm_c_re": ssm_c_re, "ssm_c_im": ssm_c_im, "ssm_d": ssm_d,
            "ssm_w_glu": ssm_w_glu, "w_out": w_out, "ln1_g": ln1_g, "ln1_b": ln1_b,
            "peer_w_q": peer_w_q, "peer_sub_keys": peer_sub_keys, "peer_u": peer_u,
            "peer_v": peer_v, "ln2_g": ln2_g, "ln2_b": ln2_b}


def reference(x, w_in, ssm_a_re, ssm_a_im, ssm_log_dt, ssm_b_re, ssm_b_im, ssm_c_re, ssm_c_im,
              ssm_d, ssm_w_glu, w_out, ln1_g, ln1_b, peer_w_q, peer_sub_keys, peer_u, peer_v,
              ln2_g, ln2_b):
    h = x
    for l in range(DEPTH):
        mix = _hybrid_mixer(h, w_in[l], ssm_a_re[l], ssm_a_im[l], ssm_log_dt[l], ssm_b_re[l],
                            ssm_b_im[l], ssm_c_re[l], ssm_c_im[l], ssm_d[l], ssm_w_glu[l], w_out[l])
        h = _layernorm(DN_ALPHA * h + mix, ln1_g[l], ln1_b[l])
        ffn = _peer(h, peer_w_q[l], peer_sub_keys[l], peer_u[l], peer_v[l])
        h = _layernorm(DN_ALPHA * h + ffn, ln2_g[l], ln2_b[l])
    return h
```

```python
import numpy as np
from contextlib import ExitStack
import ml_dtypes

import concourse.bass as bass
import concourse.mybir as mybir
from concourse.bass_utils import run_bass_kernel_spmd

F32 = mybir.dt.float32
BF16 = mybir.dt.bfloat16
U32 = mybir.dt.uint32
I32 = mybir.dt.int32
AF = mybir.ActivationFunctionType
ALU = mybir.AluOpType
AX = mybir.AxisListType

D = 1024
SEQ = 8192
NB = 4
HALF = 4096
ALPHA = 2.0 ** 0.25
LN_EPS = 1e-5
NEG = -1e30
GELU_C = 1.5957691216057308


class Sched:
    def __init__(self, nc, es, n_lanes=64):
        self.nc = nc
        self.engs = ["pe", "act", "dve", "pool", "sp"]
        self.esem = {e: es.enter_context(nc.semaphore(f"s_{e}")) for e in self.engs}
        self.lanes = [es.enter_context(nc.semaphore(f"l_{i}")) for i in range(n_lanes)]
        self.n_lanes = n_lanes
        self.lane_val = [0] * n_lanes
        n_hw = 24
        self.lane_pool = {"pool": list(range(n_hw, n_lanes))}
        self.lane_pool_default = list(range(0, n_hw))
        self.lane_next = {}
        self.cnt = {e: 0 for e in self.engs}
        self.ops = {e: [] for e in self.engs}
        self.waited = {e: {} for e in self.engs}
        self.res = {}

    def _deps(self, reads, writes):
        deps = []
        for r in reads:
            st = self.res.get(r)
            if st is not None and st[0] is not None:
                deps.append(st[0])
        for w in writes:
            st = self.res.get(w)
            if st is not None:
                if st[0] is not None:
                    deps.append(st[0])
                deps.extend(st[1].items())
        return deps

    def _commit(self, tok, reads, writes):
        key, val = tok
        for r in reads:
            st = self.res.setdefault(r, [None, {}])
            if st[1].get(key, 0) < val:
                st[1][key] = val
        for w in writes:
            self.res[w] = [tok, {}]

    def _filter(self, eng, deps):
        need = {}
        for key, val in deps:
            if eng == "pe" and key == ("e", "pe"):
                continue
            if self.waited[eng].get(key, 0) >= val:
                continue
            if need.get(key, 0) < val:
                need[key] = val
        out = []
        for key, val in need.items():
            self.waited[eng][key] = val
            sem = self.esem[key[1]] if key[0] == "e" else self.lanes[key[1]]
            out.append((sem, val))
        return out

    def op(self, eng, fn, reads=(), writes=()):
        if eng != "pe":
            locks = {r.split("_")[0] + "#lock" for r in list(reads) + list(writes) if r.startswith("ps")}
            if locks:
                writes = list(writes) + sorted(locks)
        deps = self._deps(reads, writes)
        waits = self._filter(eng, deps)
        self.cnt[eng] += 1
        tok = (("e", eng), self.cnt[eng])
        self.ops[eng].append((waits, fn, self.esem[eng], 1))
        self._commit(tok, reads, writes)
        return tok

    def dma(self, eng, fn, reads=(), writes=()):
        deps = self._deps(reads, writes)
        pool_ = self.lane_pool.get(eng, self.lane_pool_default)
        k = self.lane_next.get(eng if eng in self.lane_pool else "hw", 0)
        self.lane_next[eng if eng in self.lane_pool else "hw"] = (k + 1) % len(pool_)
        lane = pool_[k]
        prev = self.lane_val[lane]
        if prev > 0:
            deps.append((("l", lane), prev))
        waits = self._filter(eng, deps)
        self.lane_val[lane] = prev + 16
        tok = (("l", lane), prev + 16)
        self.ops[eng].append((waits, fn, self.lanes[lane], 16))
        self._commit(tok, reads, writes)
        return tok

    def barrier(self):
        snap_e = dict(self.cnt)
        snap_l = list(self.lane_val)
        for eng in self.engs:
            deps = [(("e", e2), v) for e2, v in snap_e.items() if v > 0 and e2 != eng]
            deps += [(("l", i), v) for i, v in enumerate(snap_l) if v > 0]
            if eng != "pe" and snap_e[eng] > 0:
                deps.append((("e", eng), snap_e[eng]))
            waits = self._filter(eng, deps)
            self.cnt[eng] += 1
            self.ops[eng].append((waits, lambda e: e.nop(), self.esem[eng], 1))

    def emit(self, block, final=True):
        sch = self
        ops = self.ops
        self.ops = {e: [] for e in self.engs}

        def run(e, h):
            for waits, fn, sem, inc in ops[e]:
                for s, v in waits:
                    h.wait_ge(s, v)
                fn(h).then_inc(sem, inc)

        @block.tensor
        def _(h):
            run("pe", h)

        @block.scalar
        def _(h):
            run("act", h)

        @block.vector
        def _(h):
            run("dve", h)

        @block.gpsimd
        def _(h):
            run("pool", h)

        @block.sync
        def _(h):
            run("sp", h)
            if not final:
                return
            for i in range(sch.n_lanes):
                if sch.lane_val[i] > 0:
                    h.wait_ge(sch.lanes[i], sch.lane_val[i])
            for e in sch.engs:
                if e != "sp" and sch.cnt[e] > 0:
                    h.wait_ge(sch.esem[e], sch.cnt[e])


class Builder:
    def __init__(self, n_own_tiles=32, dev_tail=False):
        self.n_own_tiles = n_own_tiles
        self.dev_tail = dev_tail
        self.nc = bass.Bass("TRN2", target_bir_lowering=False)
        self.es = ExitStack()
        self.s = Sched(self.nc, self.es)
        self._uid = 0
        self.cur = self.es

    def begin_phase(self):
        self._uid += 1
        self.cur = ExitStack()

    def end_phase(self):
        self.s.barrier()
        with self.nc.Block() as block:
            self.s.emit(block, final=False)
        self.cur.close()
        self.cur = self.es

    def bc_reg(self, e):
        if getattr(self, "_bc", None) is None:
            self._bc = e.to_reg(16383)
        return self._bc

    def sb(self, name, shape, dt):
        return self.cur.enter_context(self.nc.sbuf_tensor(f"{name}_p{self._uid}", list(shape), dt))

    def psum(self, name):
        return self.es.enter_context(self.nc.psum_tensor(name, [128, 512], F32))

    def dram_in(self, name, shape, dt=F32):
        return self.nc.dram_tensor(name, list(shape), dt, kind="ExternalInput").ap()

    def dram_out(self, name, shape, dt=F32):
        return self.nc.dram_tensor(name, list(shape), dt, kind="ExternalOutput").ap()

    def setup_consts(self):
        s = self.s
        self.colidx = self.sb("colidx", [128, 128], F32)
        self.rowidx = self.sb("rowidx", [128, 1], F32)
        self.ident = self.sb("ident", [128, 128], F32)
        self.ident_bf = self.sb("ident_bf", [128, 128], BF16)
        s.op("pool", lambda e: e.iota(self.colidx[:], pattern=[[1, 128]], base=0, channel_multiplier=0,
                                      allow_small_or_imprecise_dtypes=True), writes=["colidx"])
        s.op("pool", lambda e: e.iota(self.rowidx[:], pattern=[[0, 1]], base=0, channel_multiplier=1,
                                      allow_small_or_imprecise_dtypes=True), writes=["rowidx"])
        s.op("dve", lambda e: e.tensor_scalar(out=self.ident[:], in0=self.colidx[:], scalar1=self.rowidx[:, 0:1],
                                              scalar2=None, op0=ALU.is_equal),
             reads=["colidx", "rowidx"], writes=["ident"])
        s.op("dve", lambda e: e.tensor_copy(out=self.ident_bf[:], in_=self.ident[:]), reads=["ident"],
             writes=["ident_bf"])

    def transpose_to_bf(self, src, src_res, dst, dst_res, ps, ps_res, nblk=8, evac=("act", "dve")):
        s = self.s
        for g in range(0, nblk, 4):
            n = min(4, nblk - g)
            bank, bres = ps[(g // 4) % len(ps)], ps_res[(g // 4) % len(ps)]
            for j in range(n):
                jj = g + j
                s.op("pe", lambda e, jj=jj, j=j, bank=bank: e.transpose(
                    out=bank[:, j * 128:(j + 1) * 128], in_=src[:, jj * 128:(jj + 1) * 128], identity=self.ident[:]),
                     reads=[src_res, "ident"], writes=[bres])
            eng = evac[(g // 4) % len(evac)]
            if eng == "act":
                s.op("act", lambda e, g=g, n=n, bank=bank: e.copy(
                    out=dst[:, g:g + n, :], in_=bank[:, 0:n * 128].rearrange("p (j t) -> p j t", j=n)),
                     reads=[bres], writes=[dst_res])
            else:
                s.op(eng, lambda e, g=g, n=n, bank=bank: e.tensor_copy(
                    out=dst[:, g:g + n, :], in_=bank[:, 0:n * 128].rearrange("p (j t) -> p j t", j=n)),
                     reads=[bres], writes=[dst_res])

    def layernorm(self, r, r_res, gain, bias, out, out_res, tag):
        s = self.s
        st = self.ln_stats
        mv = self.ln_mv
        for c in range(2):
            s.op("dve", lambda e, c=c: e.bn_stats(out=st[:, c, :], in_=r[:, c * 512:(c + 1) * 512]),
                 reads=[r_res], writes=["ln_stats"])
        s.op("dve", lambda e: e.bn_aggr(out=mv[:, 0:2], in_=st[:].rearrange("p c k -> p (c k)")),
             reads=["ln_stats"], writes=["ln_mv"])
        s.op("act", lambda e: e.activation(out=mv[:, 2:3], in_=mv[:, 1:2], func=AF.Sqrt, bias=self.eps_t[:, 0:1],
                                           scale=1.0),
             reads=["ln_mv", "eps_t"], writes=["ln_mv2"])
        s.op("dve", lambda e: e.reciprocal(out=mv[:, 3:4], in_=mv[:, 2:3]), reads=["ln_mv2"], writes=["ln_mv3"])
        s.op("dve", lambda e: e.tensor_scalar(out=out[:], in0=r[:], scalar1=mv[:, 0:1], scalar2=mv[:, 3:4],
                                              op0=ALU.subtract, op1=ALU.mult),
             reads=[r_res, "ln_mv", "ln_mv3"], writes=[out_res])
        eng = getattr(self, "ln_affine_eng", "pool")
        s.op(eng, lambda e: e.tensor_tensor(out=out[:], in0=out[:], in1=gain[:], op=ALU.mult),
             reads=[out_res, "lnw"], writes=[out_res])
        s.op(eng, lambda e: e.tensor_tensor(out=out[:], in0=out[:], in1=bias[:], op=ALU.add),
             reads=[out_res, "lnw"], writes=[out_res])

    def setup_tail(self, w_out, ln1_g, ln1_b, ln2_g, ln2_b, w_q, sub_keys):
        s = self.s
        self.wout_bf = self.sb("wout_bf", [128, 8, 1024], BF16)
        self.wq_bf = self.sb("wq_bf", [128, 8, 2048], BF16)
        self.keysT = self.sb("keysT", [128, 16, 128], F32)
        self.S = self.sb("S", [128, 16, 128], F32)
        self.keys_nat = self.S[:].rearrange("p j d -> p (j d)")
        self.lnw = self.sb("lnw", [128, 4, 1024], F32)
        self.eps_t = self.sb("eps_t", [128, 1], F32)
        self.ln_stats = self.sb("ln_stats", [128, 2, 6], F32)
        self.ln_mv = self.sb("ln_mv", [128, 4], F32)
        s.op("dve", lambda e: e.memset(self.eps_t[:], LN_EPS), writes=["eps_t"])
        for kc in range(8):
            s.dma("pool", lambda e, kc=kc: e.dma_start(out=self.wout_bf[:, kc, :], in_=w_out[kc * 128:(kc + 1) * 128, :]),
                  writes=["wout_bf"])
            for hh in range(2):
                s.dma("pool", lambda e, kc=kc, hh=hh: e.dma_start(
                    out=self.wq_bf[:, kc, hh * 1024:(hh + 1) * 1024],
                    in_=w_q[kc * 128:(kc + 1) * 128, hh * 1024:(hh + 1) * 1024]), writes=["wq_bf"])
        for i, v in enumerate([ln1_g, ln1_b, ln2_g, ln2_b]):
            s.dma("sp", lambda e, i=i, v=v: e.dma_start(out=self.lnw[:, i, :], in_=v.partition_broadcast(128)),
                  writes=["lnw"])
        s.dma("sp", lambda e: e.dma_start(out=self.S[:], in_=sub_keys.rearrange("h c n d -> n (h c) d")),
              writes=["keys_nat"])
        self.transpose_to_bf(self.keys_nat, "keys_nat", self.keysT, "keysT", [self.ps[4], self.ps[5]],
                             ["ps4", "ps5"], nblk=16)

    def alloc_tail_bufs(self):
        self.xt = [self.sb(f"xt{i}", [128, 1024], F32) for i in range(2)]
        self.r1 = self.sb("r1", [128, 1024], F32)
        self.h1 = self.sb("h1", [128, 1024], F32)
        self.h1_bf = self.sb("h1_bf", [128, 1024], BF16)
        self.hT = self.sb("hT", [128, 8, 128], BF16)
        self.qT = self.sb("qT", [128, 16, 128], F32)
        self.S2 = self.sb("S2", [128, 1, 128], F32)
        self.m16 = self.sb("m16", [128, 16, 16], F32)
        self.i16 = self.sb("i16", [128, 16, 16], U32)
        self.i16f = self.sb("i16f", [128, 16, 16], F32)
        self.cand = self.sb("cand", [128, 8, 256], F32)
        self.cand2 = self.sb("cand2", [128, 1, 256], F32)
        self.tv = self.sb("tv", [128, 8, 16], F32)
        self.pos = self.sb("pos", [128, 8, 16], U32)
        self.pa = self.sb("pa", [128, 8, 16], U32)
        self.pb = self.sb("pb", [128, 8, 16], U32)
        self.paf = self.sb("paf", [128, 8, 16], F32)
        self.pbf = self.sb("pbf", [128, 8, 16], F32)
        self.oh = self.sb("oh", [128, 8, 16, 16], F32)
        self.i1s = self.sb("i1s", [128, 8, 16], F32)
        self.i2s = self.sb("i2s", [128, 8, 16], F32)
        self.eidx = self.sb("eidx", [128, 128], U32)
        self.eidf = self.sb("eidf", [128, 128], F32)
        self.nmax = self.sb("nmax", [128, 8], F32)
        self.gex = self.sb("gex", [128, 8, 16], F32)
        self.gsum = self.sb("gsum", [128, 8], F32)
        self.act_t = self.sb("act_t", [128, 128], F32)
        self.gl = self.sb("gl", [128, 128], F32)
        self.coef = self.sb("coef", [128, 128], F32)
        self.iota16 = self.sb("iota16", [128, 16], F32)
        NG = 6
        self.NG = NG
        self.ug = [self.sb(f"ug{i}", [128, 1024], BF16) for i in range(NG)]
        self.vg = [self.sb(f"vg{i}", [128, 1024], BF16) for i in range(NG)]
        self.dg = [self.sb(f"dg{i}", [128, 128], BF16) for i in range(NG)]
        self.junk = self.sb("junk", [128, 1024], BF16)
        self.r2 = self.sb("r2", [128, 1024], F32)
        self.o_t = [self.sb(f"o_t{i}", [128, 1024], F32) for i in range(2)]
        self.s.op("pool", lambda e: e.iota(self.iota16[:], pattern=[[1, 16]], base=0, channel_multiplier=0,
                                           allow_small_or_imprecise_dtypes=True), writes=["iota16"])

    def tail_tile(self, it, x_src, ycatT, ycatT_res, peer_u, peer_v, out_dst, dbg=99):
        s = self.s
        if dbg == 0:
            s.dma("sp", lambda e: e.dma_start(out=out_dst[:, 0:128], in_=self.keysT[:, 3, :]), reads=["keysT"])
            s.dma("sp", lambda e: e.dma_start(out=out_dst[:, 128:256], in_=self.lnw[:, 1, 0:128]), reads=["lnw"])
            return
        ps = self.ps
        xt = self.xt[it % 2]
        xres = f"xt{it % 2}"
        s.dma("sp", lambda e: e.dma_start(out=xt[:], in_=x_src), writes=[xres])
        for hh in range(2):
            for kc in range(8):
                s.op("pe", lambda e, hh=hh, kc=kc: e.matmul(
                    ps[hh][:, :], lhsT=ycatT[kc], rhs=self.wout_bf[:, kc, hh * 512:(hh + 1) * 512],
                    start=(kc == 0), stop=(kc == 7)), reads=[ycatT_res[kc], "wout_bf"], writes=[f"ps{hh}"])
        for hh in range(2):
            s.op("dve", lambda e, hh=hh: e.scalar_tensor_tensor(
                out=self.r1[:, hh * 512:(hh + 1) * 512], in0=xt[:, hh * 512:(hh + 1) * 512], scalar=ALPHA,
                in1=ps[hh][:, :], op0=ALU.mult, op1=ALU.add), reads=[xres, f"ps{hh}"], writes=["r1"])
        self.layernorm(self.r1, "r1", self.lnw[:, 0, :], self.lnw[:, 1, :], self.h1, "h1", "ln1")
        s.op("act", lambda e: e.copy(out=self.h1_bf[:], in_=self.h1[:]), reads=["h1"], writes=["h1_bf"])
        if dbg == 1:
            s.dma("sp", lambda e: e.dma_start(out=out_dst, in_=self.h1[:]), reads=["h1"])
            return
        self.transpose_to_bf(self.h1, "h1", self.hT, "hT", [ps[4], ps[5]], ["ps4", "ps5"], nblk=8)
        for j in range(16):
            bank, bres = (ps[6], "ps6") if (j // 4) % 2 == 0 else (ps[7], "ps7")
            for kc in range(8):
                s.op("pe", lambda e, j=j, kc=kc, bank=bank: e.matmul(
                    bank[:, (j % 4) * 128:(j % 4 + 1) * 128], lhsT=self.wq_bf[:, kc, j * 128:(j + 1) * 128],
                    rhs=self.hT[:, kc, :], start=(kc == 0), stop=(kc == 7)),
                     reads=["wq_bf", "hT"], writes=[bres])
            if j % 4 == 3:
                g = j - 3
                eng = "act" if (j // 4) % 2 == 0 else "dve"
                if eng == "act":
                    s.op("act", lambda e, g=g, bank=bank: e.copy(
                        out=self.qT[:, g:g + 4, :], in_=bank[:, :].rearrange("p (j t) -> p j t", j=4)),
                         reads=[bres], writes=["qT"])
                else:
                    s.op("dve", lambda e, g=g, bank=bank: e.tensor_copy(
                        out=self.qT[:, g:g + 4, :], in_=bank[:, :].rearrange("p (j t) -> p j t", j=4)),
                         reads=[bres], writes=["qT"])
        for j in range(16):
            bank, bres = (ps[4], "ps4") if (j // 4) % 2 == 0 else (ps[5], "ps5")
            s.op("pe", lambda e, j=j, bank=bank: e.matmul(
                bank[:, (j % 4) * 128:(j % 4 + 1) * 128], lhsT=self.qT[:, j, :], rhs=self.keysT[:, j, :],
                start=True, stop=True), reads=["qT", "keysT"], writes=[bres])
            if j % 4 == 3:
                g = j - 3
                s.op("act", lambda e, g=g, bank=bank: e.copy(
                    out=self.S[:, g:g + 4, :], in_=bank[:, :].rearrange("p (j t) -> p j t", j=4)),
                     reads=[bres], writes=[f"S{g // 4}"])
        for j in range(16):
            sr = f"S{j // 4}"
            s.op("dve", lambda e, j=j: e.max(out=self.m16[:, j, 0:8], in_=self.S[:, j, :]), reads=[sr], writes=["m16"])
            s.op("dve", lambda e, j=j: e.max_index(out=self.i16[:, j, 0:8], in_max=self.m16[:, j, 0:8],
                                                   in_values=self.S[:, j, :]), reads=[sr, "m16"], writes=["i16"])
            s.op("dve", lambda e, j=j: e.match_replace(out=self.S2[:, 0, :], in_to_replace=self.m16[:, j, 0:8],
                                                       in_values=self.S[:, j, :], imm_value=NEG),
                 reads=[sr, "m16"], writes=["S2"])
            s.op("dve", lambda e, j=j: e.max(out=self.m16[:, j, 8:16], in_=self.S2[:, 0, :]), reads=["S2"],
                 writes=["m16"])
            s.op("dve", lambda e, j=j: e.max_index(out=self.i16[:, j, 8:16], in_max=self.m16[:, j, 8:16],
                                                   in_values=self.S2[:, 0, :]), reads=["S2", "m16"], writes=["i16"])
        s.op("dve", lambda e: e.tensor_copy(out=self.i16f[:], in_=self.i16[:]), reads=["i16"], writes=["i16f"])
        m16v = self.m16[:].rearrange("p (h c) k -> p h c k", c=2)
        s.op("dve", lambda e: e.tensor_tensor(
            out=self.cand[:].rearrange("p h (a b) -> p h a b", a=16),
            in0=m16v[:, :, 0, :].unsqueeze(3).to_broadcast([128, 8, 16, 16]),
            in1=m16v[:, :, 1, :].unsqueeze(2).to_broadcast([128, 8, 16, 16]), op=ALU.add),
             reads=["m16"], writes=["cand"])
        for h in range(8):
            s.op("dve", lambda e, h=h: e.max(out=self.tv[:, h, 0:8], in_=self.cand[:, h, :]), reads=["cand"],
                 writes=["tv"])
            s.op("dve", lambda e, h=h: e.max_index(out=self.pos[:, h, 0:8], in_max=self.tv[:, h, 0:8],
                                                   in_values=self.cand[:, h, :]), reads=["cand", "tv"], writes=["pos"])
            s.op("dve", lambda e, h=h: e.match_replace(out=self.cand2[:, 0, :], in_to_replace=self.tv[:, h, 0:8],
                                                       in_values=self.cand[:, h, :], imm_value=NEG),
                 reads=["cand", "tv"], writes=["cand2"])
            s.op("dve", lambda e, h=h: e.max(out=self.tv[:, h, 8:16], in_=self.cand2[:, 0, :]), reads=["cand2"],
                 writes=["tv"])
            s.op("dve", lambda e, h=h: e.max_index(out=self.pos[:, h, 8:16], in_max=self.tv[:, h, 8:16],
                                                   in_values=self.cand2[:, 0, :]), reads=["cand2", "tv"],
                 writes=["pos"])
        s.op("dve", lambda e: e.tensor_single_scalar(out=self.pa[:], in_=self.pos[:], scalar=4,
                                                     op=ALU.logical_shift_right), reads=["pos"], writes=["pa"])
        s.op("dve", lambda e: e.tensor_single_scalar(out=self.pb[:], in_=self.pos[:], scalar=15,
                                                     op=ALU.bitwise_and), reads=["pos"], writes=["pb"])
        s.op("dve", lambda e: e.tensor_copy(out=self.paf[:], in_=self.pa[:]), reads=["pa"], writes=["paf"])
        s.op("dve", lambda e: e.tensor_copy(out=self.pbf[:], in_=self.pb[:]), reads=["pb"], writes=["pbf"])
        i16v = self.i16f[:].rearrange("p (h c) k -> p h c k", c=2)
        iota_b = self.iota16[:].unsqueeze(1).unsqueeze(1).to_broadcast([128, 8, 16, 16])
        for (pf, pres, c, dst, dres) in ((self.paf, "paf", 0, self.i1s, "i1s"), (self.pbf, "pbf", 1, self.i2s, "i2s")):
            s.op("dve", lambda e, pf=pf: e.tensor_tensor(
                out=self.oh[:], in0=pf[:].unsqueeze(3).to_broadcast([128, 8, 16, 16]), in1=iota_b, op=ALU.is_equal),
                 reads=[pres, "iota16"], writes=["oh"])
            s.op("dve", lambda e, c=c: e.tensor_tensor(
                out=self.oh[:], in0=self.oh[:], in1=i16v[:, :, c, :].unsqueeze(2).to_broadcast([128, 8, 16, 16]),
                op=ALU.mult), reads=["oh", "i16f"], writes=["oh"])
            s.op("dve", lambda e, dst=dst: e.tensor_reduce(out=dst[:], in_=self.oh[:], axis=AX.X, op=ALU.add),
                 reads=["oh"], writes=[dres])
        s.op("dve", lambda e: e.scalar_tensor_tensor(
            out=self.eidf[:].rearrange("p (h k) -> p h k", h=8), in0=self.i1s[:], scalar=128.0, in1=self.i2s[:],
            op0=ALU.mult, op1=ALU.add), reads=["i1s", "i2s"], writes=["eidf"])
        s.op("dve", lambda e: e.tensor_copy(out=self.eidx[:], in_=self.eidf[:]), reads=["eidf"], writes=["eidx"])
        if dbg == 2:
            s.dma("sp", lambda e: e.dma_start(out=out_dst[:, 0:128], in_=self.eidf[:]), reads=["eidf"])
            s.dma("sp", lambda e: e.dma_start(out=out_dst[:, 128:256], in_=self.tv[:].rearrange("p h k -> p (h k)")),
                  reads=["tv"])
            return
        s.op("dve", lambda e: e.tensor_tensor(
            out=self.gex[:], in0=self.tv[:], in1=self.tv[:, :, 0:1].to_broadcast([128, 8, 16]), op=ALU.subtract),
             reads=["tv"], writes=["gex"])
        s.op("act", lambda e: e.activation(out=self.gex[:], in_=self.gex[:], func=AF.Exp), reads=["gex"],
             writes=["gex"])
        s.op("dve", lambda e: e.tensor_reduce(out=self.gsum[:], in_=self.gex[:], axis=AX.X, op=ALU.add),
             reads=["gex"], writes=["gsum"])
        s.op("dve", lambda e: e.reciprocal(out=self.gsum[:], in_=self.gsum[:]), reads=["gsum"], writes=["gsum"])
        s.op("dve", lambda e: e.tensor_tensor(
            out=self.gex[:], in0=self.gex[:], in1=self.gsum[:].unsqueeze(2).to_broadcast([128, 8, 16]), op=ALU.mult),
             reads=["gex", "gsum"], writes=["gex"])
        NG = self.NG
        for sl in range(128):
            b = sl % NG
            s.dma("pool", lambda e, sl=sl, b=b: e.indirect_dma_start(
                out=self.ug[b][:], out_offset=None, in_=peer_u,
                in_offset=bass.IndirectOffsetOnAxis(ap=self.eidx[:, sl:sl + 1], axis=0),
                bounds_check=self.bc_reg(e), oob_is_err=False),
                  reads=["eidx"], writes=[f"ug{b}"])
            s.op("dve", lambda e, sl=sl, b=b: e.scalar_tensor_tensor(
                out=self.junk[:], in0=self.ug[b][:], scalar=1.0, in1=self.h1_bf[:], op0=ALU.mult, op1=ALU.mult,
                accum_out=self.act_t[:, sl:sl + 1]), reads=[f"ug{b}", "h1_bf"], writes=["junk", "act_t"])
        s.op("dve", lambda e: e.tensor_tensor(out=self.gl[:], in0=self.act_t[:], in1=self.act_t[:], op=ALU.mult),
             reads=["act_t"], writes=["gl"])
        s.op("dve", lambda e: e.tensor_scalar(out=self.gl[:], in0=self.gl[:], scalar1=0.044715, scalar2=1.0,
                                              op0=ALU.mult, op1=ALU.add), reads=["gl"], writes=["gl"])
        s.op("dve", lambda e: e.tensor_tensor(out=self.gl[:], in0=self.gl[:], in1=self.act_t[:], op=ALU.mult),
             reads=["gl", "act_t"], writes=["gl"])
        s.op("act", lambda e: e.activation(out=self.gl[:], in_=self.gl[:], func=AF.Sigmoid, scale=GELU_C),
             reads=["gl"], writes=["gl"])
        s.op("dve", lambda e: e.tensor_tensor(out=self.gl[:], in0=self.gl[:], in1=self.act_t[:], op=ALU.mult),
             reads=["gl", "act_t"], writes=["gl"])
        s.op("dve", lambda e: e.tensor_tensor(out=self.coef[:], in0=self.gl[:],
                                              in1=self.gex[:].rearrange("p h k -> p (h k)"), op=ALU.mult),
             reads=["gl", "gex"], writes=["coef"])
        if dbg == 3:
            s.dma("sp", lambda e: e.dma_start(out=out_dst[:, 0:128], in_=self.act_t[:]), reads=["act_t"])
            s.dma("sp", lambda e: e.dma_start(out=out_dst[:, 128:256], in_=self.coef[:]), reads=["coef"])
            return
        for sl in range(128):
            b = sl % NG
            s.dma("pool", lambda e, sl=sl, b=b: e.indirect_dma_start(
                out=self.vg[b][:], out_offset=None, in_=peer_v,
                in_offset=bass.IndirectOffsetOnAxis(ap=self.eidx[:, sl:sl + 1], axis=0),
                bounds_check=self.bc_reg(e), oob_is_err=False),
                  reads=["eidx"], writes=[f"vg{b}"])
            s.op("act", lambda e, sl=sl, b=b: e.activation(
                out=self.dg[b][:], in_=self.ident_bf[:], func=AF.Copy, scale=self.coef[:, sl:sl + 1]),
                 reads=["ident_bf", "coef"], writes=[f"dg{b}"])
            for hh in range(2):
                s.op("pe", lambda e, sl=sl, b=b, hh=hh: e.matmul(
                    ps[2 + hh][:, :], lhsT=self.dg[b][:], rhs=self.vg[b][:, hh * 512:(hh + 1) * 512],
                    start=(sl == 0), stop=(sl == 127)), reads=[f"dg{b}", f"vg{b}"], writes=[f"ps{2 + hh}"])
        for hh in range(2):
            s.op("dve", lambda e, hh=hh: e.scalar_tensor_tensor(
                out=self.r2[:, hh * 512:(hh + 1) * 512], in0=self.h1[:, hh * 512:(hh + 1) * 512], scalar=ALPHA,
                in1=ps[2 + hh][:, :], op0=ALU.mult, op1=ALU.add), reads=["h1", f"ps{2 + hh}"], writes=["r2"])
        ot = self.o_t[it % 2]
        ores = f"o_t{it % 2}"
        self.layernorm(self.r2, "r2", self.lnw[:, 2, :], self.lnw[:, 3, :], ot, ores, "ln2")
        s.dma("sp", lambda e: e.dma_start(out=out_dst, in_=ot[:]), reads=[ores])


    def prepass_uv(self, peer_u, peer_v, uv_d):
        s = self.s
        self.begin_phase()
        NBUF = 8
        bufs = [self.sb(f"uvp{i}", [128, 2048], BF16) for i in range(NBUF)]
        for c in range(128):
            k = c % NBUF
            b = bufs[k]
            s.dma("pool", lambda e, b=b, c=c: e.dma_start(out=b[:, 0:1024], in_=peer_u[c * 128:(c + 1) * 128, :]),
                  writes=[f"uvp{k}_u"])
            s.dma("pool", lambda e, b=b, c=c: e.dma_start(out=b[:, 1024:2048], in_=peer_v[c * 128:(c + 1) * 128, :]),
                  writes=[f"uvp{k}_v"])
            s.dma("sp", lambda e, b=b, c=c: e.dma_start(out=uv_d[c * 128:(c + 1) * 128, :], in_=b[:]),
                  reads=[f"uvp{k}_u", f"uvp{k}_v"], writes=["uv_d"])
        self.end_phase()

    def prepass_bg(self, peer_u, peer_v, uv_d):
        s = self.s
        NBUF = 8
        bufs = [self.sb(f"uvp{i}", [128, 2048], BF16) for i in range(NBUF)]
        items = []
        for c in range(128):
            def f(c=c):
                k = c % NBUF
                b = bufs[k]
                s.dma("pool", lambda e: e.dma_start(out=b[:, 0:1024], in_=peer_u[c * 128:(c + 1) * 128, :]),
                      writes=[f"uvp{k}_u"])
                s.dma("pool", lambda e: e.dma_start(out=b[:, 1024:2048], in_=peer_v[c * 128:(c + 1) * 128, :]),
                      writes=[f"uvp{k}_v"])
                s.dma("sp", lambda e: e.dma_start(out=uv_d[c * 128:(c + 1) * 128, :], in_=b[:]),
                      reads=[f"uvp{k}_u", f"uvp{k}_v"], writes=["uv_d"])
            items.append(f)
        return items

    def alloc_tail2(self):
        s = self.s
        self.xt = [self.sb("xt0", [128, 1024], F32)] * 2
        self.r1 = self.sb("r1", [128, 1024], F32)
        self.h1 = [self.sb(f"h1_{i}", [128, 1024], F32) for i in range(2)]
        self.h1b = [self.sb(f"h1b_{i}", [128, 1024], BF16) for i in range(2)]
        self.hT = self.sb("hT", [128, 8, 128], BF16)
        self.m16 = self.sb("m16", [128, 16, 16], F32)
        self.i16 = self.sb("i16", [128, 16, 16], U32)
        self.i16f = self.sb("i16f", [128, 16, 16], F32)
        self.cand = self.sb("cand", [128, 8, 256], F32)
        self.qT = self.cand[:].rearrange("p h (a b) -> p (h a) b", a=2)
        self.cand2 = self.sb("cand2", [128, 1, 256], F32)
        self.S2 = self.cand2[:, :, 0:128]
        self.tv = self.sb("tv", [128, 8, 16], F32)
        self.pos = self.sb("pos", [128, 8, 16], U32)
        self.pa = self.sb("pa", [128, 8, 16], U32)
        self.pb = self.sb("pb", [128, 8, 16], U32)
        self.paf = self.sb("paf", [128, 8, 16], F32)
        self.pbf = self.sb("pbf", [128, 8, 16], F32)
        self.oh = self.cand[:].rearrange("p h (a b) -> p h a b", a=16)
        self.i1s = self.sb("i1s", [128, 8, 16], F32)
        self.i2s = self.sb("i2s", [128, 8, 16], F32)
        self.eidx2 = [self.sb(f"eidx{i}", [128, 128], U32) for i in range(2)]
        self.eidf = self.sb("eidf", [128, 128], F32)
        self.gex2 = [self.sb(f"gex{i}", [128, 8, 16], F32) for i in range(2)]
        self.gsum = self.sb("gsum", [128, 8], F32)
        self.act_t = self.sb("act_t", [128, 128], F32)
        self.gl = self.sb("gl", [128, 128], F32)
        self.coef = self.sb("coef", [128, 128], F32)
        self.iota16 = self.sb("iota16", [128, 16], F32)
        self.GS = 4
        self.NUV = 20
        self.ln_affine_eng = "dve"
        self.pool_dots = False
        self.r2 = self.r1
        self.uvg = [self.sb(f"uvg{i}", [128, 2048], BF16) for i in range(self.NUV)]
        self.dg = [self.sb(f"dg{i}", [128, 128], BF16) for i in range(8)]
        self.ycT2 = [self.sb("ycT0", [128, 8, 128], BF16)] * 2
        s.op("pool", lambda e: e.iota(self.iota16[:], pattern=[[1, 16]], base=0, channel_multiplier=0,
                                      allow_small_or_imprecise_dtypes=True), writes=["iota16"])

    def tail2_prologue(self, it, x_src, ycT_d):
        s = self.s
        ps = self.ps
        b2 = it % 2
        xt, xres = self.xt[0], "xt0"
        ycT, ycres = self.ycT2[0], "ycT0"
        h1, h1res = self.h1[b2], f"h1_{b2}"
        h1b, h1bres = self.h1b[b2], f"h1b_{b2}"
        eidx, eres = self.eidx2[b2], f"eidx{b2}"
        gex, gres = self.gex2[b2], f"gex{b2}"
        S = self.S

        def a1():
            s.dma("sp", lambda e: e.dma_start(out=xt[:], in_=x_src), writes=[xres])
            s.dma("sp", lambda e: e.dma_start(
                out=ycT[:], in_=ycT_d[:, it * 128:(it + 1) * 128].rearrange("(kc p) t -> p kc t", p=128)),
                  reads=["ycT_d"], writes=[ycres])
            for hh in range(2):
                for kc in range(8):
                    s.op("pe", lambda e, hh=hh, kc=kc: e.matmul(
                        ps[6 + hh][:, :], lhsT=ycT[:, kc, :], rhs=self.wout_bf[:, kc, hh * 512:(hh + 1) * 512],
                        start=(kc == 0), stop=(kc == 7)), reads=[ycres, "wout_bf"], writes=[f"ps{6 + hh}"])
            for hh in range(2):
                s.op("dve", lambda e, hh=hh: e.scalar_tensor_tensor(
                    out=self.r1[:, hh * 512:(hh + 1) * 512], in0=xt[:, hh * 512:(hh + 1) * 512], scalar=ALPHA,
                    in1=ps[6 + hh][:, :], op0=ALU.mult, op1=ALU.add), reads=[xres, f"ps{6 + hh}"], writes=["r1"])

        def a2():
            self.layernorm(self.r1, "r1", self.lnw[:, 0, :], self.lnw[:, 1, :], h1, h1res, "ln1")
            s.op("act", lambda e: e.copy(out=h1b[:], in_=h1[:]), reads=[h1res], writes=[h1bres])

        def b1():
            self.transpose_to_bf(h1, h1res, self.hT, "hT", [ps[4], ps[5]], ["ps4", "ps5"], nblk=8)

        def b2_(j0):
            def f():
                for j in range(j0, j0 + 4):
                    bank, bres = (ps[6], "ps6") if (j // 4) % 2 == 0 else (ps[7], "ps7")
                    for kc in range(8):
                        s.op("pe", lambda e, j=j, kc=kc, bank=bank: e.matmul(
                            bank[:, (j % 4) * 128:(j % 4 + 1) * 128], lhsT=self.wq_bf[:, kc, j * 128:(j + 1) * 128],
                            rhs=self.hT[:, kc, :], start=(kc == 0), stop=(kc == 7)),
                             reads=["wq_bf", "hT"], writes=[bres])
                s.op("act", lambda e, g=j0, bank=bank: e.copy(
                    out=self.qT[:, g:g + 4, :], in_=bank[:, :].rearrange("p (j t) -> p j t", j=4)),
                     reads=[bres], writes=["qT"])
            return f

        def b3_(j0):
            def f():
                for j in range(j0, j0 + 4):
                    bank, bres = (ps[4], "ps4") if (j // 4) % 2 == 0 else (ps[5], "ps5")
                    s.op("pe", lambda e, j=j, bank=bank: e.matmul(
                        bank[:, (j % 4) * 128:(j % 4 + 1) * 128], lhsT=self.qT[:, j, :], rhs=self.keysT[:, j, :],
                        start=True, stop=True), reads=["qT", "keysT"], writes=[bres])
                s.op("act", lambda e, g=j0, bank=bank: e.copy(
                    out=S[:, g:g + 4, :], in_=bank[:, :].rearrange("p (j t) -> p j t", j=4)),
                     reads=[bres], writes=[f"S{j0 // 4}"])
            return f

        def c_(j0, j1):
            def f():
                for j in range(j0, j1):
                    sr = f"S{j // 4}"
                    s.op("dve", lambda e, j=j: e.max(out=self.m16[:, j, 0:8], in_=S[:, j, :]), reads=[sr], writes=["m16"])
                    s.op("dve", lambda e, j=j: e.max_index(out=self.i16[:, j, 0:8], in_max=self.m16[:, j, 0:8],
                                                           in_values=S[:, j, :]), reads=[sr, "m16"], writes=["i16"])
                    s.op("dve", lambda e, j=j: e.match_replace(out=self.S2[:, 0, :], in_to_replace=self.m16[:, j, 0:8],
                                                               in_values=S[:, j, :], imm_value=NEG),
                         reads=[sr, "m16"], writes=["S2"])
                    s.op("dve", lambda e, j=j: e.max(out=self.m16[:, j, 8:16], in_=self.S2[:, 0, :]), reads=["S2"],
                         writes=["m16"])
                    s.op("dve", lambda e, j=j: e.max_index(out=self.i16[:, j, 8:16], in_max=self.m16[:, j, 8:16],
                                                           in_values=self.S2[:, 0, :]), reads=["S2", "m16"], writes=["i16"])
            return f

        def d0():
            s.op("dve", lambda e: e.tensor_copy(out=self.i16f[:], in_=self.i16[:]), reads=["i16"], writes=["i16f"])
            m16v = self.m16[:].rearrange("p (h c) k -> p h c k", c=2)
            s.op("dve", lambda e: e.tensor_tensor(
                out=self.cand[:].rearrange("p h (a b) -> p h a b", a=16),
                in0=m16v[:, :, 0, :].unsqueeze(3).to_broadcast([128, 8, 16, 16]),
                in1=m16v[:, :, 1, :].unsqueeze(2).to_broadcast([128, 8, 16, 16]), op=ALU.add),
                 reads=["m16", "qT"], writes=["cand"])

        def d_(h0, h1_):
            def f():
                for h in range(h0, h1_):
                    s.op("dve", lambda e, h=h: e.max(out=self.tv[:, h, 0:8], in_=self.cand[:, h, :]), reads=["cand"],
                         writes=["tv"])
                    s.op("dve", lambda e, h=h: e.max_index(out=self.pos[:, h, 0:8], in_max=self.tv[:, h, 0:8],
                                                           in_values=self.cand[:, h, :]), reads=["cand", "tv"], writes=["pos"])
                    s.op("dve", lambda e, h=h: e.match_replace(out=self.cand2[:, 0, :], in_to_replace=self.tv[:, h, 0:8],
                                                               in_values=self.cand[:, h, :], imm_value=NEG),
                         reads=["cand", "tv"], writes=["cand2"])
                    s.op("dve", lambda e, h=h: e.max(out=self.tv[:, h, 8:16], in_=self.cand2[:, 0, :]), reads=["cand2"],
                         writes=["tv"])
                    s.op("dve", lambda e, h=h: e.max_index(out=self.pos[:, h, 8:16], in_max=self.tv[:, h, 8:16],
                                                           in_values=self.cand2[:, 0, :]), reads=["cand2", "tv"],
                         writes=["pos"])
            return f

        def e1():
            s.op("dve", lambda e: e.tensor_single_scalar(out=self.pa[:], in_=self.pos[:], scalar=4,
                                                         op=ALU.logical_shift_right), reads=["pos"], writes=["pa"])
            s.op("dve", lambda e: e.tensor_single_scalar(out=self.pb[:], in_=self.pos[:], scalar=15,
                                                         op=ALU.bitwise_and), reads=["pos"], writes=["pb"])
            s.op("dve", lambda e: e.tensor_copy(out=self.paf[:], in_=self.pa[:]), reads=["pa"], writes=["paf"])
            s.op("dve", lambda e: e.tensor_copy(out=self.pbf[:], in_=self.pb[:]), reads=["pb"], writes=["pbf"])
            s.op("dve", lambda e: e.tensor_tensor(
                out=gex[:], in0=self.tv[:], in1=self.tv[:, :, 0:1].to_broadcast([128, 8, 16]), op=ALU.subtract),
                 reads=["tv"], writes=[gres])
            s.op("act", lambda e: e.activation(out=gex[:], in_=gex[:], func=AF.Exp), reads=[gres], writes=[gres])

        def e2_(which):
            def f():
                i16v = self.i16f[:].rearrange("p (h c) k -> p h c k", c=2)
                iota_b = self.iota16[:].unsqueeze(1).unsqueeze(1).to_broadcast([128, 8, 16, 16])
                pf, pres, c, dst, dres = ((self.paf, "paf", 0, self.i1s, "i1s"), (self.pbf, "pbf", 1, self.i2s, "i2s"))[which]
                s.op("dve", lambda e: e.tensor_tensor(
                    out=self.oh[:], in0=pf[:].unsqueeze(3).to_broadcast([128, 8, 16, 16]), in1=iota_b, op=ALU.is_equal),
                     reads=[pres, "iota16", "cand"], writes=["oh"])
                s.op("dve", lambda e: e.tensor_tensor(
                    out=self.oh[:], in0=self.oh[:], in1=i16v[:, :, c, :].unsqueeze(2).to_broadcast([128, 8, 16, 16]),
                    op=ALU.mult), reads=["oh", "i16f"], writes=["oh"])
                s.op("dve", lambda e: e.tensor_reduce(out=dst[:], in_=self.oh[:], axis=AX.X, op=ALU.add),
                     reads=["oh"], writes=[dres])
            return f

        def e3():
            s.op("dve", lambda e: e.scalar_tensor_tensor(
                out=self.eidf[:].rearrange("p (h k) -> p h k", h=8), in0=self.i1s[:], scalar=128.0, in1=self.i2s[:],
                op0=ALU.mult, op1=ALU.add), reads=["i1s", "i2s"], writes=["eidf"])
            s.op("dve", lambda e: e.tensor_copy(out=eidx[:], in_=self.eidf[:]), reads=["eidf"], writes=[eres])
            s.op("dve", lambda e: e.tensor_reduce(out=self.gsum[:], in_=gex[:], axis=AX.X, op=ALU.add),
                 reads=[gres], writes=["gsum"])
            s.op("dve", lambda e: e.reciprocal(out=self.gsum[:], in_=self.gsum[:]), reads=["gsum"], writes=["gsum"])
            s.op("dve", lambda e: e.tensor_tensor(
                out=gex[:], in0=gex[:], in1=self.gsum[:].unsqueeze(2).to_broadcast([128, 8, 16]), op=ALU.mult),
                 reads=[gres, "gsum"], writes=[gres])

        return ([a1, a2, b1] + [b2_(j0) for j0 in (0, 4, 8, 12)] + [b3_(j0) for j0 in (0, 4, 8, 12)]
                + [c_(j0, j0 + 2) for j0 in range(0, 16, 2)] + [d0] + [d_(h0, h0 + 2) for h0 in range(0, 8, 2)]
                + [e1, e2_(0), e2_(1), e3])

    def tail2_gather(self, it, g, uv_d):
        s = self.s
        eidx, eres = self.eidx2[it % 2], f"eidx{it % 2}"
        for k in range(self.GS):
            sl = g * self.GS + k
            bi = (it * 128 + sl) % self.NUV
            s.dma("pool", lambda e, sl=sl, bi=bi: e.indirect_dma_start(
                out=self.uvg[bi][:], out_offset=None, in_=uv_d,
                in_offset=bass.IndirectOffsetOnAxis(ap=eidx[:, sl:sl + 1], axis=0),
                bounds_check=self.bc_reg(e), oob_is_err=False),
                  reads=[eres, "uv_d"], writes=[f"uvg{bi}"])

    def tail2_dots(self, it, g):
        s = self.s
        b2 = it % 2
        h1b, h1bres = self.h1b[b2], f"h1b_{b2}"
        GS = self.GS
        sl0 = g * GS
        for k in range(GS):
            sl = sl0 + k
            bi = (it * 128 + sl) % self.NUV
            if k < GS - 1:
                s.op("dve", lambda e, bi=bi: e.tensor_tensor(out=self.uvg[bi][:, 0:1024], in0=self.uvg[bi][:, 0:1024],
                                                             in1=h1b[:], op=ALU.mult),
                     reads=[f"uvg{bi}", h1bres], writes=[f"uvgp{bi}"])
                s.op("act", lambda e, sl=sl, bi=bi: e.activation(out=self.uvg[bi][:, 0:1024], in_=self.uvg[bi][:, 0:1024],
                                                                 func=AF.Copy, accum_out=self.act_t[:, sl:sl + 1]),
                     reads=[f"uvgp{bi}"], writes=[f"act_{sl}", f"uvgp{bi}"])
            else:
                s.op("dve", lambda e, sl=sl, bi=bi: e.scalar_tensor_tensor(
                    out=self.uvg[bi][:, 0:1024], in0=self.uvg[bi][:, 0:1024], scalar=1.0, in1=h1b[:], op0=ALU.mult,
                    op1=ALU.mult, accum_out=self.act_t[:, sl:sl + 1]),
                     reads=[f"uvg{bi}", h1bres], writes=[f"act_{sl}", f"uvgp{bi}"])

    def tail2_gelu_a(self, it, g):
        s = self.s
        GS = self.GS
        sl0 = g * GS
        cs = slice(sl0, sl0 + GS)
        glr = f"gl{g % 4}"
        s.op("act", lambda e: e.activation(out=self.gl[:, cs], in_=self.act_t[:, cs], func=AF.Gelu_apprx_tanh),
             reads=[f"act_{sl0 + k}" for k in range(GS)], writes=[glr])

    def tail2_finish(self, it, g):
        s = self.s
        ps = self.ps
        b2 = it % 2
        gex, gres = self.gex2[b2], f"gex{b2}"
        fb0 = 2 if b2 == 0 else 0
        GS = self.GS
        sl0 = g * GS
        cs = slice(sl0, sl0 + GS)
        glr = f"gl{g % 4}"
        cr = f"coef{g % 4}"
        s.op("dve", lambda e: e.tensor_tensor(out=self.coef[:, cs], in0=self.gl[:, cs],
                                              in1=gex[:].rearrange("p h k -> p (h k)")[:, cs], op=ALU.mult),
             reads=[glr, gres], writes=[cr])
        for k in range(GS):
            sl = sl0 + k
            bi = (it * 128 + sl) % self.NUV
            di = (it * 128 + sl) % 8
            s.op("act", lambda e, sl=sl, di=di: e.activation(
                out=self.dg[di][:], in_=self.ident_bf[:], func=AF.Copy, scale=self.coef[:, sl:sl + 1]),
                 reads=["ident_bf", cr], writes=[f"dg{di}"])
            for hh in range(2):
                s.op("pe", lambda e, sl=sl, bi=bi, hh=hh, di=di: e.matmul(
                    ps[fb0 + hh][:, :], lhsT=self.dg[di][:], rhs=self.uvg[bi][:, 1024 + hh * 512:1024 + (hh + 1) * 512],
                    start=(sl == 0), stop=(sl == 127)), reads=[f"dg{di}", f"uvg{bi}"], writes=[f"ps{fb0 + hh}"])

    def tail2_epilogue(self, it, out_dst):
        s = self.s
        ps = self.ps
        b2 = it % 2
        h1, h1res = self.h1[b2], f"h1_{b2}"
        fb0 = 2 if b2 == 0 else 0
        for hh in range(2):
            s.op("dve", lambda e, hh=hh: e.scalar_tensor_tensor(
                out=self.r2[:, hh * 512:(hh + 1) * 512], in0=h1[:, hh * 512:(hh + 1) * 512], scalar=ALPHA,
                in1=ps[fb0 + hh][:, :], op0=ALU.mult, op1=ALU.add), reads=[h1res, f"ps{fb0 + hh}"], writes=["r1"])
        ot = self.r2
        ores = "r1"
        self.layernorm(self.r2, "r1", self.lnw[:, 2, :], self.lnw[:, 3, :], ot, ores, "ln2")
        s.dma("sp", lambda e: e.dma_start(out=out_dst, in_=ot[:]), reads=[ores])

    def tail2_all(self, n_tiles, x_rows, ycT_d, uv_d, out_rows):
        NGRP = 128 // self.GS
        for st in self.tail2_prologue(0, x_rows(0), ycT_d):
            st()
        steps = [(it, g) for it in range(n_tiles) for g in range(NGRP)]
        n = len(steps)
        pending = []
        for k in range(-2, n + 1):
            if 0 <= k < n:
                it, g = steps[k]
                if g == 0 and it + 1 < n_tiles:
                    pending = self.tail2_prologue(it + 1, x_rows(it + 1), ycT_d)
            if 0 <= k + 2 < n:
                self.tail2_gather(*steps[k + 2], uv_d)
            if 0 <= k - 1 < n:
                itf, gf = steps[k - 1]
                self.tail2_finish(itf, gf)
                if gf == NGRP - 1:
                    self.tail2_epilogue(itf, out_rows(itf))
            if 0 <= k < n:
                self.tail2_gelu_a(*steps[k])
            if 0 <= k + 1 < n:
                self.tail2_dots(*steps[k + 1])
            if 0 <= k < n:
                it, g = steps[k]
                if pending and g <= NGRP - 4:
                    pending.pop(0)()
                    if g == NGRP - 4:
                        while pending:
                            pending.pop(0)()
        assert not pending

    def attn_pass(self, hg, xp, xo, w_in, gbias, ycT_d, n_qb=16, dbg=99, bg_args=None):
        s = self.s
        ps = self.ps
        self.begin_phase()
        KA = [self.sb(f"KA{i}", [96, 8192], BF16) for i in range(4)]
        QA = [self.sb(f"QA{i}", [96, 4096], BF16) for i in range(4)]
        VA = self.sb("VA", [128, 64, 4, 65], BF16)
        win = self.sb("win_a", [128, 8, 768], BF16)
        xt = [self.sb(f"axt{i}", [128, 1024], F32) for i in range(2)]
        xT4 = [self.sb(f"axT{i}", [128, 8, 512], BF16) for i in range(2)]
        kms = self.sb("kms", [64, 4, 32], F32)
        kmb = self.sb("kmb", [64, 4, 32], BF16)
        GB = self.sb("GB", [128, 16, 32], F32)
        gb_in = self.sb("gb_in", [128, 16], F32)
        Gt = self.sb("Gt", [128, 2, 32], F32)
        m8 = self.sb("m8", [128, 2, 8], F32)
        Gm = [self.sb(f"Gm{i}", [128, 96], F32) for i in range(2)]
        PT = [self.sb(f"PT{i}", [128, 256], BF16) for i in range(4)]
        rs = self.sb("rs", [65, 256], F32)
        rr = self.sb("rr", [64, 256], F32)
        oT = [self.sb(f"oT{i}", [64, 256], BF16) for i in range(2)]
        ones_t = self.sb("ones_t", [65, 64], F32)
        bg = self.prepass_bg(*bg_args) if bg_args is not None else []
        for part, c0 in ((0, 512), (1, 1024), (2, 1536)):
            for kc in range(8):
                s.dma("pool", lambda e, part=part, c0=c0, kc=kc: e.dma_start(
                    out=win[:, kc, part * 256:(part + 1) * 256],
                    in_=w_in[kc * 128:(kc + 1) * 128, c0 + hg * 256:c0 + (hg + 1) * 256]), writes=["win_a"])
        s.dma("sp", lambda e: e.dma_start(out=gb_in[:], in_=gbias), writes=["gb_in"])
        for h in range(4):
            s.op("pool", lambda e, h=h: e.tensor_copy(
                out=KA[h][64:96, :].rearrange("p (n k) -> p n k", k=256),
                in_=self.ident[64:96, 64:96].unsqueeze(2).to_broadcast([32, 32, 256])),
                 reads=["ident"], writes=[f"KAe{h}"])
        s.op("pool", lambda e: e.memset(VA[:, :, :, 64:65], 1.0), writes=["VAone"])
        s.op("pool", lambda e: e.memset(ones_t[:], 1.0), writes=["ones_t"])
        for i in range(2):
            s.op("pool", lambda e, i=i: e.memset(Gm[i][:], 0.0), writes=[f"Gm{i}"])
        s.op("dve", lambda e: e.memset(kms[:], 0.0), writes=["kms"])
        s.op("pool", lambda e: e.memset(GB[:], 0.0), writes=["GB"])
        for j in range(16):
            s.op("pool", lambda e, j=j: e.tensor_copy(out=GB[:, j, 0:16], in_=gb_in[:]), reads=["gb_in"], writes=["GB"])
            s.op("pool", lambda e, j=j: e.memset(GB[:, j, 16 + j:32], NEG), writes=["GB"])
        if dbg == 0:
            s.dma("sp", lambda e: e.dma_start(out=ycT_d[0:32, 0:4096], in_=KA[1][64:96, 0:4096]), reads=["KAe1"])
            s.dma("sp", lambda e: e.dma_start(out=ycT_d[128:256, 0:768], in_=win[:, 3, :]), reads=["win_a"])
            self.end_phase()
            return
        n_groups = 8 + (n_qb * 2 + 3) // 4
        if dbg == 2:
            n_groups = 1
        def T(g):
            own = g >= 8
            xsrc = xo if own else xp
            t0 = (g - 8) * 512 if own else g * 512
            xb = xT4[g % 2]
            xbres = f"axT{g % 2}"
            for ti in range(4):
                tile_i = g * 4 + ti
                xtb = xt[tile_i % 2]
                xres = f"axt{tile_i % 2}"
                s.dma("sp", lambda e, xtb=xtb, r0=t0 + ti * 128: e.dma_start(
                    out=xtb[:], in_=xsrc[r0:r0 + 128, :]), writes=[xres])
                if bg:
                    bg.pop(0)()
                for half in range(2):
                    bank, bres = (ps[4], "ps4") if half == 0 else (ps[5], "ps5")
                    for jj in range(4):
                        kc = half * 4 + jj
                        s.op("pe", lambda e, kc=kc, jj=jj, bank=bank, xtb=xtb: e.transpose(
                            out=bank[:, jj * 128:(jj + 1) * 128], in_=xtb[:, kc * 128:(kc + 1) * 128],
                            identity=self.ident[:]), reads=[xres, "ident"], writes=[bres])
                    if half == 0:
                        s.op("act", lambda e, bank=bank, ti=ti: e.copy(
                            out=xb[:, 0:4, ti * 128:(ti + 1) * 128],
                            in_=bank[:, :].rearrange("p (j t) -> p j t", j=4)), reads=[bres], writes=[xbres])
                    else:
                        s.op("dve", lambda e, bank=bank, ti=ti: e.tensor_copy(
                            out=xb[:, 4:8, ti * 128:(ti + 1) * 128],
                            in_=bank[:, :].rearrange("p (j t) -> p j t", j=4)), reads=[bres], writes=[xbres])

        def M(g):
            own = g >= 8
            t0 = (g - 8) * 512 if own else g * 512
            xb = xT4[g % 2]
            xbres = f"axT{g % 2}"
            kcol0 = g * 512
            for h in range(4):
                bank, bres = (ps[6], "ps6") if h % 2 == 0 else (ps[7], "ps7")
                for kc in range(8):
                    s.op("pe", lambda e, h=h, kc=kc, bank=bank: e.matmul(
                        bank[0:64, :], lhsT=win[:, kc, 256 + h * 64:256 + (h + 1) * 64], rhs=xb[:, kc, :],
                        start=(kc == 0), stop=(kc == 7)), reads=["win_a", xbres], writes=[bres])
                for n in range(2):
                    s.op("act", lambda e, h=h, bank=bank, n=n: e.activation(
                        out=KA[h][0:64, kcol0 + n * 256:kcol0 + (n + 1) * 256], in_=bank[0:64, n * 256:(n + 1) * 256],
                        func=AF.Copy, accum_out=kms[:, h, g * 2 + n:g * 2 + n + 1]),
                         reads=[bres, "kms"], writes=[f"KA{h}", f"kms_{h}_{g * 2 + n}"])
                if own:
                    bank, bres = (ps[2], "ps2") if h % 2 == 0 else (ps[3], "ps3")
                    for kc in range(8):
                        s.op("pe", lambda e, h=h, kc=kc, bank=bank: e.matmul(
                            bank[0:64, :], lhsT=win[:, kc, h * 64:(h + 1) * 64], rhs=xb[:, kc, :],
                            start=(kc == 0), stop=(kc == 7)), reads=["win_a", xbres], writes=[bres])
                    s.op("dve", lambda e, h=h, bank=bank: e.tensor_copy(
                        out=QA[h][0:64, t0:t0 + 512], in_=bank[0:64, :]), reads=[bres], writes=[f"QA{h}"])
            for ti in range(4):
                bank, bres = (ps[0], "ps0") if ti % 2 == 0 else (ps[1], "ps1")
                for kc in range(8):
                    s.op("pe", lambda e, kc=kc, ti=ti, bank=bank: e.matmul(
                        bank[:, 0:256], lhsT=xb[:, kc, ti * 128:(ti + 1) * 128], rhs=win[:, kc, 512:768],
                        start=(kc == 0), stop=(kc == 7)), reads=["win_a", xbres], writes=[bres])
                s.op("act", lambda e, ti=ti, bank=bank: e.copy(
                    out=VA[:, g * 4 + ti, :, 0:64], in_=bank[:, 0:256].rearrange("p (h d) -> p h d", h=4)),
                     reads=[bres], writes=["VA"])

        T(0)
        for g in range(n_groups):
            if g + 1 < n_groups:
                T(g + 1)
            M(g)
        kms_all = [f"kms_{h}_{i}" for h in range(4) for i in range(2 * n_groups)]
        s.op("act", lambda e: e.activation(out=kmb[:], in_=kms[:], func=AF.Copy, scale=1.0 / 256.0),
             reads=["kms"] + kms_all, writes=["kmb"])
        if dbg in (1, 2):
            for h in range(4):
                s.dma("sp", lambda e, h=h: e.dma_start(out=ycT_d[h * 64:(h + 1) * 64, :], in_=KA[h][0:64, 0:4096]),
                      reads=[f"KA{h}"])
                s.dma("sp", lambda e, h=h: e.dma_start(out=ycT_d[256 + h * 64:256 + (h + 1) * 64, :], in_=QA[h][0:64, :]),
                      reads=[f"QA{h}"])
            self.end_phase()
            return
        pairs = [(j, h) for j in range(n_qb) for h in range(4)]
        SB = ((ps[2], "ps2"), (ps[6], "ps6"), (ps[7], "ps7"))

        def mask(p):
            j, h = pairs[p]
            q0 = j * 256
            for qh in range(2):
                s.op("pe", lambda e, qh=qh: e.matmul(
                    ps[4][:, qh * 32:(qh + 1) * 32], lhsT=QA[h][0:64, q0 + qh * 128:q0 + (qh + 1) * 128],
                    rhs=kmb[:, h, :], start=True, stop=True), reads=[f"QA{h}", "kmb"], writes=["ps4"])
            s.op("dve", lambda e: e.tensor_tensor(
                out=Gt[:], in0=ps[4][:, 0:64].rearrange("p (a n) -> p a n", a=2),
                in1=GB[:, j, :].unsqueeze(1).to_broadcast([128, 2, 32]), op=ALU.add),
                 reads=["ps4", "GB"], writes=["Gt"])
            for qh in range(2):
                s.op("dve", lambda e, qh=qh: e.max(out=m8[:, qh, :], in_=Gt[:, qh, :]), reads=["Gt"], writes=["m8"])
            s.op("dve", lambda e: e.tensor_scalar(out=m8[:, :, 2:3], in0=m8[:, :, 2:3], scalar1=-1e29, scalar2=None,
                                                  op0=ALU.max), reads=["m8"], writes=["m8"])
            for qh in range(2):
                s.op("dve", lambda e, qh=qh: e.tensor_scalar(
                    out=Gm[qh][:, 64:96], in0=Gt[:, qh, :], scalar1=m8[:, qh, 2:3], scalar2=1.0,
                    op0=ALU.is_ge, op1=ALU.subtract), reads=["Gt", "m8"], writes=[f"Gm{qh}"])
                s.op("dve", lambda e, qh=qh: e.memset(Gm[qh][:, 64 + 16 + j:64 + 17 + j], 0.0),
                     writes=[f"Gm{qh}"])
                s.op("pe", lambda e, qh=qh: e.transpose(
                    out=ps[5][0:96, qh * 128:(qh + 1) * 128], in_=Gm[qh][:, 0:96], identity=self.ident[:]),
                     reads=[f"Gm{qh}", "ident"], writes=["ps5"])
            s.op("act", lambda e: e.activation(
                out=QA[h][64:96, q0:q0 + 256], in_=ps[5][64:96, 0:256], func=AF.Copy, scale=30000.0),
                 reads=["ps5"], writes=[f"QA{h}"])

        def norm(p):
            j, h = pairs[p]
            hglob = hg * 4 + h
            q0 = j * 256
            obank, ores_ = (ps[0], "ps0") if p % 2 == 0 else (ps[1], "ps1")
            s.op("dve", lambda e: e.tensor_copy(out=rs[64:65, :], in_=obank[64:65, 0:256]), reads=[ores_], writes=["rs"])
            s.op("pe", lambda e: e.matmul(ps[3][0:64, 0:256], lhsT=ones_t[64:65, :], rhs=rs[64:65, :], start=True,
                                          stop=True), reads=["ones_t", "rs"], writes=["ps3"])
            s.op("dve", lambda e: e.reciprocal(out=rr[:], in_=ps[3][0:64, 0:256]), reads=["ps3"], writes=["rr"])
            ob = oT[p % 2]
            obres = f"oT{p % 2}"
            s.op("dve", lambda e: e.tensor_tensor(out=ob[:], in0=obank[0:64, 0:256], in1=rr[:], op=ALU.mult),
                 reads=[ores_, "rr"], writes=[obres])
            s.dma("sp", lambda e: e.dma_start(
                out=ycT_d[512 + hglob * 64:512 + (hglob + 1) * 64, q0:q0 + 256], in_=ob[:]),
                  reads=[obres], writes=["ycT_d"])

        def inner(p):
            j, h = pairs[p]
            q0 = j * 256
            n_kt = 32 + 2 * j + 2
            obank, ores_ = (ps[0], "ps0") if p % 2 == 0 else (ps[1], "ps1")

            def issue_S(kt):
                sbank, sres = SB[kt % 3]
                s.op("pe", lambda e: e.matmul(
                    sbank[:, 0:256], lhsT=KA[h][0:96, kt * 128:(kt + 1) * 128], rhs=QA[h][0:96, q0:q0 + 256],
                    start=True, stop=True), reads=[f"KA{h}", f"KAe{h}", f"QA{h}"], writes=[sres])

            def issue_E(kt):
                sb_i = kt % 4
                sbank, sres = SB[kt % 3]
                s.op("act", lambda e: e.activation(out=PT[sb_i][:], in_=sbank[:, 0:256], func=AF.Exp, scale=0.125),
                     reads=[sres], writes=[f"PT{sb_i}"])
                if kt >= 32 + 2 * j:
                    ktl = kt - (32 + 2 * j)
                    s.op("pool", lambda e: e.affine_select(
                        out=PT[sb_i][:], in_=PT[sb_i][:], pattern=[[1, 256]], compare_op=ALU.is_ge, fill=0.0,
                        base=-ktl * 128, channel_multiplier=-1), reads=[f"PT{sb_i}"], writes=[f"PT{sb_i}"])

            def issue_PV(kt):
                sb_i = kt % 4
                s.op("pe", lambda e: e.matmul(
                    obank[0:65, 0:256], lhsT=VA[:, kt, h, :], rhs=PT[sb_i][:], start=(kt == 0),
                    stop=(kt == n_kt - 1)), reads=["VA", "VAone", f"PT{sb_i}"], writes=[ores_])

            issue_S(0)
            issue_S(1)
            if p > 0:
                norm(p - 1)
            for kt in range(n_kt):
                issue_E(kt)
                if kt + 2 < n_kt:
                    issue_S(kt + 2)
                issue_PV(kt)
                if kt == 12 and p + 1 < len(pairs):
                    mask(p + 1)

        mask(0)
        for p in range(len(pairs)):
            if bg:
                bg.pop(0)()
            inner(p)
        norm(len(pairs) - 1)
        while bg:
            bg.pop(0)()
        self.end_phase()

    def build_dev_attn(self, n_qb=2, hgs=(0,), dbg=99):
        xp = self.dram_in("xp", [HALF, D])
        xo = self.dram_in("xo", [HALF, D])
        w_in = self.dram_in("w_in", [D, 2048])
        gbias = self.dram_in("gbias", [128, 16])
        ycT_d = self.dram_out("ycT", [1024, HALF], BF16)
        self.ps = [self.psum(f"ps{i}") for i in range(8)]
        self.setup_consts()
        for hg in hgs:
            self.attn_pass(hg, xp, xo, w_in, gbias, ycT_d, n_qb=n_qb, dbg=dbg)
        self.finish()
        return self.nc


    def ssm_phase(self, xp, xo, w_in, ssm, ycT_d, n_pre=32, n_own=32, dbg_out=None):
        s = self.s
        ps = self.ps
        self.begin_phase()
        TWO_PI = 6.283185307179586
        MAGIC = 12582912.0
        C1 = 6.28125
        C2 = TWO_PI - C1
        PI_LO = 3.1415925
        LTr = self.sb("LTr", [128, 16, 128], F32)
        LTi = self.sb("LTi", [128, 16, 128], F32)
        LIr = self.sb("LIr", [128, 2048], F32)
        LIi = self.sb("LIi", [128, 2048], F32)
        Bblk = [self.sb(f"Bblk{i}", [128, 1024], BF16) for i in range(4)]
        Cblk = self.sb("Cblk", [128, 4, 2, 4, 128], F32)
        wglu = self.sb("wglu", [128, 4, 512], BF16)
        win = self.sb("win_u", [128, 8, 512], BF16)
        Tri = self.sb("Tri", [128, 128], BF16)
        dcol = self.sb("dcol", [128, 4], F32)
        car_r = self.sb("car_r", [128, 16], F32)
        car_i = self.sb("car_i", [128, 16], F32)
        lc_r = self.sb("lc_r", [128, 16], F32)
        lc_i = self.sb("lc_i", [128, 16], F32)
        prm16 = self.sb("prm16", [16, 3 * 128], F32)
        ldt16 = self.sb("ldt16", [16, 2], F32)
        prm = self.sb("prm", [128, 3, 16], F32)
        dtt = self.sb("dtt", [128, 16], F32)
        adr = self.sb("adr", [128, 16], F32)
        adi = self.sb("adi", [128, 16], F32)
        iot = self.sb("iot", [128, 128], F32)
        big = [self.sb(f"big{i}", [128, 16, 128], F32) for i in range(6)]
        kap = self.sb("kap", [128, 6, 16], F32)
        bc = [self.sb(f"bc{i}", [128, 16, 16], F32) for i in range(2)]
        bb = [self.sb(f"bb{i}", [128, 16, 16], F32) for i in range(2)]
        bt = [self.sb(f"bt{i}", [128, 16, 16], F32) for i in range(2)]
        bexp = [self.sb(f"bexp{i}", [128, 128], F32) for i in range(2)]
        ct2 = [self.sb(f"ct2_{i}", [128, 2, 64], F32) for i in range(2)]

        def dve(fn, r, w):
            s.op("dve", fn, reads=r, writes=w)

        def stop_here(k, tile_ap, res_name, n=128):
            if dbg_out is not None and dbg_out[0] == "stop" and dbg_out[1] == k:
                s.dma("sp", lambda e: e.dma_start(out=dbg_out[2][:, 0:n], in_=tile_ap), reads=[res_name])
                self.end_phase()
                return True
            return False

        s.dma("sp", lambda e: e.dma_start(out=prm16[:, 0:128], in_=ssm["ssm_a_re"].rearrange("(cb two) p -> cb (two p)", two=2)),
              writes=["prm16"])
        s.dma("sp", lambda e: e.dma_start(out=prm16[:, 128:256], in_=ssm["ssm_a_im"].rearrange("(cb two) p -> cb (two p)", two=2)),
              writes=["prm16"])
        s.dma("sp", lambda e: e.dma_start(out=ldt16[:], in_=ssm["ssm_log_dt"].rearrange("o (cb two) -> (o cb) two", two=2)),
              writes=["ldt16"])
        dve(lambda e: e.tensor_copy(out=prm16[:, 256:384].rearrange("p (two q) -> p two q", two=2),
                                    in_=ldt16[:].unsqueeze(2).to_broadcast([16, 2, 64])), ["ldt16"], ["prm16"])
        for i in range(3):
            s.op("pe", lambda e, i=i: e.transpose(out=ps[4][:, i * 16:(i + 1) * 16], in_=prm16[:, i * 128:(i + 1) * 128],
                                                  identity=self.ident[0:16, 0:16]), reads=["prm16", "ident"], writes=["ps4"])
        dve(lambda e: e.tensor_copy(out=prm[:], in_=ps[4][:, 0:48].rearrange("p (i c) -> p i c", i=3)), ["ps4"], ["prm"])
        s.op("act", lambda e: e.activation(out=dtt[:], in_=prm[:, 2, :], func=AF.Exp), reads=["prm"], writes=["dtt"])
        dve(lambda e: e.tensor_tensor(out=adr[:], in0=prm[:, 0, :], in1=dtt[:], op=ALU.mult), ["prm", "dtt"], ["adr"])
        dve(lambda e: e.tensor_tensor(out=adi[:], in0=prm[:, 1, :], in1=dtt[:], op=ALU.mult), ["prm", "dtt"], ["adi"])
        s.op("pool", lambda e: e.iota(iot[:], pattern=[[1, 128]], base=0, channel_multiplier=0,
                                      allow_small_or_imprecise_dtypes=True), writes=["iot"])
        dve(lambda e: e.tensor_scalar(out=Tri[:], in0=self.colidx[:], scalar1=self.rowidx[:, 0:1], scalar2=None,
                                      op0=ALU.is_ge), ["colidx", "rowidx"], ["Tri"])
        iot_b = iot[:].unsqueeze(1).to_broadcast([128, 16, 128])
        ANG, RED, TMP, SN, CS, EX = big
        dve(lambda e: e.tensor_tensor(out=ANG[:], in0=adi[:].unsqueeze(2).to_broadcast([128, 16, 128]), in1=iot_b,
                                      op=ALU.mult), ["adi", "iot"], ["ANG"])

        def reduce_sin(shift, dst, dres):
            if shift != 0.0:
                dve(lambda e: e.tensor_scalar(out=RED[:], in0=ANG[:], scalar1=shift, scalar2=None, op0=ALU.add), ["ANG"], ["RED"])
                src_, sres = RED, "RED"
            else:
                src_, sres = ANG, "ANG"
            dve(lambda e: e.tensor_scalar(out=TMP[:], in0=src_[:], scalar1=1.0 / TWO_PI, scalar2=MAGIC, op0=ALU.mult,
                                          op1=ALU.add), [sres], ["TMP"])
            dve(lambda e: e.tensor_scalar(out=TMP[:], in0=TMP[:], scalar1=-MAGIC, scalar2=None, op0=ALU.add), ["TMP"], ["TMP"])
            dve(lambda e: e.scalar_tensor_tensor(out=RED[:], in0=TMP[:], scalar=-C1, in1=src_[:], op0=ALU.mult, op1=ALU.add),
                ["TMP", sres], ["RED"])
            dve(lambda e: e.scalar_tensor_tensor(out=RED[:], in0=TMP[:], scalar=-C2, in1=RED[:], op0=ALU.mult, op1=ALU.add),
                ["TMP", "RED"], ["RED"])
            dve(lambda e: e.tensor_single_scalar(out=TMP[:], in_=RED[:], scalar=PI_LO, op=ALU.is_gt), ["RED"], ["TMP"])
            dve(lambda e: e.scalar_tensor_tensor(out=RED[:], in0=TMP[:], scalar=-TWO_PI, in1=RED[:], op0=ALU.mult, op1=ALU.add),
                ["TMP", "RED"], ["RED"])
            dve(lambda e: e.tensor_single_scalar(out=TMP[:], in_=RED[:], scalar=-PI_LO, op=ALU.is_lt), ["RED"], ["TMP"])
            dve(lambda e: e.scalar_tensor_tensor(out=RED[:], in0=TMP[:], scalar=TWO_PI, in1=RED[:], op0=ALU.mult, op1=ALU.add),
                ["TMP", "RED"], ["RED"])
            dve(lambda e: e.tensor_scalar(out=RED[:], in0=RED[:], scalar1=-PI_LO, scalar2=PI_LO, op0=ALU.max, op1=ALU.min),
                ["RED"], ["RED"])
            s.op("act", lambda e: e.activation(out=dst[:], in_=RED[:], func=AF.Sin), reads=["RED"], writes=[dres])

        reduce_sin(0.0, SN, "SN")
        reduce_sin(1.5707963267948966, CS, "CS")
        dve(lambda e: e.tensor_tensor(out=TMP[:], in0=adr[:].unsqueeze(2).to_broadcast([128, 16, 128]), in1=iot_b,
                                      op=ALU.mult), ["adr", "iot"], ["TMP"])
        s.op("act", lambda e: e.activation(out=EX[:], in_=TMP[:], func=AF.Exp), reads=["TMP"], writes=["EX"])
        dve(lambda e: e.tensor_tensor(out=LTr[:], in0=EX[:], in1=CS[:], op=ALU.mult), ["EX", "CS"], ["LTr"])
        dve(lambda e: e.tensor_tensor(out=LTi[:], in0=EX[:], in1=SN[:], op=ALU.mult), ["EX", "SN"], ["LTi"])
        s.op("act", lambda e: e.activation(out=EX[:], in_=TMP[:], func=AF.Exp, scale=-1.0), reads=["TMP", "LTr", "LTi"],
             writes=["EX"])
        dve(lambda e: e.tensor_tensor(out=CS[:], in0=EX[:], in1=CS[:], op=ALU.mult), ["EX", "CS"], ["CS"])
        dve(lambda e: e.scalar_tensor_tensor(out=SN[:], in0=SN[:], scalar=-1.0, in1=EX[:], op0=ALU.mult, op1=ALU.mult),
            ["EX", "SN"], ["SN"])
        for (src_, sres, dst, dres) in ((CS, "CS", LIr, "LIr"), (SN, "SN", LIi, "LIi")):
            for g4 in range(4):
                bank, bres = (ps[4], "ps4") if g4 % 2 == 0 else (ps[5], "ps5")
                for j in range(4):
                    cb = g4 * 4 + j
                    s.op("pe", lambda e, src_=src_, cb=cb, j=j, bank=bank: e.transpose(
                        out=bank[:, j * 128:(j + 1) * 128], in_=src_[:, cb, :], identity=self.ident[:]),
                         reads=[sres, "ident"], writes=[bres])
                s.op("act", lambda e, dst=dst, g4=g4, bank=bank: e.copy(out=dst[:, g4 * 512:(g4 + 1) * 512], in_=bank[:, :]),
                     reads=[bres], writes=[dres])
        K_X, K_Y, K_DEN, K_R, K_I, K_T = range(6)
        dve(lambda e: e.tensor_scalar(out=kap[:, K_X, :], in0=LTr[:, :, 1], scalar1=-1.0, scalar2=None, op0=ALU.add), ["LTr"], ["kap"])
        dve(lambda e: e.tensor_tensor(out=kap[:, K_DEN, :], in0=prm[:, 0, :], in1=prm[:, 0, :], op=ALU.mult), ["prm"], ["kap"])
        dve(lambda e: e.tensor_tensor(out=kap[:, K_T, :], in0=prm[:, 1, :], in1=prm[:, 1, :], op=ALU.mult), ["prm"], ["kap"])
        dve(lambda e: e.tensor_tensor(out=kap[:, K_DEN, :], in0=kap[:, K_DEN, :], in1=kap[:, K_T, :], op=ALU.add), ["kap"], ["kap"])
        dve(lambda e: e.reciprocal(out=kap[:, K_DEN, :], in_=kap[:, K_DEN, :]), ["kap"], ["kap"])
        dve(lambda e: e.tensor_tensor(out=kap[:, K_R, :], in0=kap[:, K_X, :], in1=prm[:, 0, :], op=ALU.mult), ["kap", "prm"], ["kap"])
        dve(lambda e: e.tensor_tensor(out=kap[:, K_T, :], in0=LTi[:, :, 1], in1=prm[:, 1, :], op=ALU.mult), ["LTi", "prm"], ["kap"])
        dve(lambda e: e.tensor_tensor(out=kap[:, K_R, :], in0=kap[:, K_R, :], in1=kap[:, K_T, :], op=ALU.add), ["kap"], ["kap"])
        dve(lambda e: e.tensor_tensor(out=kap[:, K_R, :], in0=kap[:, K_R, :], in1=kap[:, K_DEN, :], op=ALU.mult), ["kap"], ["kap"])
        dve(lambda e: e.tensor_tensor(out=kap[:, K_I, :], in0=LTi[:, :, 1], in1=prm[:, 0, :], op=ALU.mult), ["LTi", "prm"], ["kap"])
        dve(lambda e: e.tensor_tensor(out=kap[:, K_T, :], in0=kap[:, K_X, :], in1=prm[:, 1, :], op=ALU.mult), ["kap", "prm"], ["kap"])
        dve(lambda e: e.tensor_tensor(out=kap[:, K_I, :], in0=kap[:, K_I, :], in1=kap[:, K_T, :], op=ALU.subtract), ["kap"], ["kap"])
        dve(lambda e: e.tensor_tensor(out=kap[:, K_I, :], in0=kap[:, K_I, :], in1=kap[:, K_DEN, :], op=ALU.mult), ["kap"], ["kap"])
        for i, nm in enumerate(("ssm_b_re", "ssm_b_im")):
            s.dma("sp", lambda e, i=i, nm=nm: e.dma_start(
                out=bc[i][:], in_=ssm[nm].rearrange("(cb two) p h -> (two p) cb h", two=2)), writes=[f"bc{i}"])
        kr_b = kap[:, K_R, :].unsqueeze(2).to_broadcast([128, 16, 16])
        ki_b = kap[:, K_I, :].unsqueeze(2).to_broadcast([128, 16, 16])
        dve(lambda e: e.tensor_tensor(out=bt[0][:], in0=bc[0][:], in1=kr_b, op=ALU.mult), ["bc0", "kap"], ["bt0"])
        dve(lambda e: e.tensor_tensor(out=bt[1][:], in0=bc[1][:], in1=ki_b, op=ALU.mult), ["bc1", "kap"], ["bt1"])
        dve(lambda e: e.tensor_tensor(out=bb[0][:], in0=bt[0][:], in1=bt[1][:], op=ALU.subtract), ["bt0", "bt1"], ["bb0"])
        dve(lambda e: e.tensor_tensor(out=bt[0][:], in0=bc[1][:], in1=kr_b, op=ALU.mult), ["bc1", "kap", "bb0"], ["bt0"])
        dve(lambda e: e.tensor_tensor(out=bt[1][:], in0=bc[0][:], in1=ki_b, op=ALU.mult), ["bc0", "kap", "bb0"], ["bt1"])
        dve(lambda e: e.tensor_tensor(out=bb[1][:], in0=bt[0][:], in1=bt[1][:], op=ALU.add), ["bt0", "bt1"], ["bb1"])
        k = 0
        for fb in range(4):
            for ri in range(2):
                for cbl in range(4):
                    cb = fb * 4 + cbl
                    be = bexp[k % 2]
                    beres = f"bexp{k % 2}"
                    s.op("pool", lambda e, be=be: e.memset(be[:], 0.0), writes=[beres])
                    for two in range(2):
                        gl = 2 * cbl + two
                        s.op("pool", lambda e, be=be, two=two, gl=gl, ri=ri, cb=cb: e.tensor_copy(
                            out=be[two * 64:(two + 1) * 64, gl * 16:(gl + 1) * 16],
                            in_=bb[ri][two * 64:(two + 1) * 64, cb, :]), reads=[f"bb{ri}"], writes=[beres])
                    bank, bres = (ps[6], "ps6") if k % 2 == 0 else (ps[7], "ps7")
                    s.op("pe", lambda e, be=be, bank=bank: e.matmul(bank[:, 0:128], lhsT=be[:], rhs=self.ident[:], start=True,
                                                                    stop=True), reads=[beres, "ident"], writes=[bres])
                    s.op("act", lambda e, fb=fb, ri=ri, cbl=cbl, bank=bank: e.copy(
                        out=Bblk[fb][:, ri * 512 + cbl * 128:ri * 512 + (cbl + 1) * 128], in_=bank[:, 0:128]),
                         reads=[bres], writes=[f"Bblk{fb}"])
                    k += 1
        s.op("pool", lambda e: e.memset(Cblk[:], 0.0), writes=["Cblk"])
        k = 0
        for fb in range(4):
            for ri, nm in enumerate(("ssm_c_re", "ssm_c_im")):
                c2 = ct2[k % 2]
                c2res = f"ct2_{k % 2}"
                for dup in range(2):
                    s.dma("sp", lambda e, c2=c2, dup=dup, nm=nm, fb=fb: e.dma_start(
                        out=c2[:, dup, :], in_=ssm[nm][fb * 8:(fb + 1) * 8].rearrange("g ho p -> (g ho) p")),
                          writes=[c2res])
                bank, bres = (ps[4], "ps4") if k % 2 == 0 else (ps[5], "ps5")
                s.op("pe", lambda e, c2=c2, bank=bank: e.transpose(
                    out=bank[:, 0:128], in_=c2[:].rearrange("p a b -> p (a b)"), identity=self.ident[:]),
                     reads=[c2res, "ident"], writes=[bres])
                for cbl in range(4):
                    for two in range(2):
                        gl = 2 * cbl + two
                        s.op("act", lambda e, fb=fb, ri=ri, cbl=cbl, two=two, gl=gl, bank=bank: e.activation(
                            out=Cblk[two * 64:(two + 1) * 64, fb, ri, cbl, gl * 16:(gl + 1) * 16],
                            in_=bank[two * 64:(two + 1) * 64, gl * 16:(gl + 1) * 16], func=AF.Copy,
                            scale=(1.0 if ri == 0 else -1.0)), reads=[bres], writes=["Cblk"])
                k += 1
        d4 = self.sb("d4", [4, 128], F32)
        s.dma("sp", lambda e: e.dma_start(out=d4[:], in_=ssm["ssm_d"].rearrange("o (fb q) -> (o fb) q", q=128)), writes=["d4"])
        s.op("pe", lambda e: e.transpose(out=ps[4][:, 0:4], in_=d4[:], identity=self.ident[0:4, 0:4]), reads=["d4", "ident"],
             writes=["ps4"])
        dve(lambda e: e.tensor_copy(out=dcol[:], in_=ps[4][:, 0:4]), ["ps4"], ["dcol"])
        for kc in range(8):
            s.dma("pool", lambda e, kc=kc: e.dma_start(out=win[:, kc, :], in_=w_in[kc * 128:(kc + 1) * 128, 0:512]),
                  writes=["win_u"])
        for fb in range(4):
            s.dma("pool", lambda e, fb=fb: e.dma_start(out=wglu[:, fb, :], in_=ssm["ssm_w_glu"][fb * 128:(fb + 1) * 128, :]),
                  writes=["wglu"])
        dve(lambda e: e.memset(car_r[:], 0.0), [], ["car_r"])
        dve(lambda e: e.memset(car_i[:], 0.0), [], ["car_i"])
        if dbg_out is not None and dbg_out[0] == "tables":
            d = dbg_out[1]
            s.dma("sp", lambda e: e.dma_start(out=d[:, 0:2048], in_=LTr[:].rearrange("p a b -> p (a b)")), reads=["LTr"])
            s.dma("sp", lambda e: e.dma_start(out=d[:, 2048:4096], in_=LTi[:].rearrange("p a b -> p (a b)")), reads=["LTi"])
            s.dma("sp", lambda e: e.dma_start(out=d[:, 4096:6144], in_=LIr[:]), reads=["LIr"])
            s.dma("sp", lambda e: e.dma_start(out=d[:, 6144:8192], in_=LIi[:]), reads=["LIi"])
            self.end_phase()
            return
        if stop_here(0, dcol[:], "dcol", 4):
            return
        if stop_here(-1, wglu[:, 0, 0:64].bitcast(F32), "wglu", 32):
            return
        xt = [self.sb(f"sxt{i}", [128, 1024], F32) for i in range(2)]
        xT = [self.sb(f"sxT{i}", [128, 8, 128], BF16) for i in range(2)]
        uT = self.sb("uT", [128, 4, 128], BF16)
        uTf = self.sb("uTf", [128, 4, 128], F32)
        q4 = [self.sb(f"q4_{i}", [128, 4, 128], F32) for i in range(4)]
        s_r = [self.sb(f"s_r{i}", [128, 4, 128], F32) for i in range(2)]
        s_i = [self.sb(f"s_i{i}", [128, 4, 128], F32) for i in range(2)]
        c4 = [self.sb(f"c4_{i}", [128, 4], F32) for i in range(6)]
        yf = self.sb("yf", [128, 128], F32)
        yw = self.sb("yw", [128, 128], F32)
        ygf = self.sb("ygf", [128, 4, 128], F32)
        ygb = self.sb("ygb", [128, 4, 128], BF16)
        sig = self.sb("sig", [128, 4, 128], F32)
        ysb = [self.sb(f"ysb{i}", [128, 4, 128], BF16) for i in range(2)]
        uT2 = [uT, self.sb("uT_b", [128, 4, 128], BF16)]
        uTf2 = [uTf, self.sb("uTf_b", [128, 4, 128], F32)]
        lcw = self.sb("lcw", [128, 16], F32)
        tW = [[self.sb(f"tW{i}_{w}", [128, 512], BF16) for i in range(4)] for w in range(2)]
        n_chunks = n_pre + n_own
        QUADS = ((0, 2, lc_r, "lc_r", LTr, "LTr"), (1, 3, lc_i, "lc_i", LTi, "LTi"),
                 (2, 2, lc_r, "lc_r", LTi, "LTi"), (3, 3, lc_i, "lc_i", LTr, "LTr"))

        def is_own(c):
            return c >= n_pre

        def row0(c):
            return (c - n_pre) * 128 if is_own(c) else (32 - n_pre + c) * 128

        def P_a(c):
            xsrc = xo if is_own(c) else xp
            r0 = row0(c)
            xtb, xres = xt[c % 2], f"sxt{c % 2}"
            xTb, xTres = xT[c % 2], f"sxT{c % 2}"
            s.dma("sp", lambda e: e.dma_start(out=xtb[:], in_=xsrc[r0:r0 + 128, :]), writes=[xres])
            self.transpose_to_bf(xtb, xres, xTb, xTres, [ps[4], ps[5]], ["ps4", "ps5"], nblk=8)
            for fb in range(4):
                for kc in range(8):
                    s.op("pe", lambda e, fb=fb, kc=kc: e.matmul(
                        ps[6][:, fb * 128:(fb + 1) * 128], lhsT=win[:, kc, fb * 128:(fb + 1) * 128], rhs=xTb[:, kc, :],
                        start=(kc == 0), stop=(kc == 7)), reads=["win_u", xTres], writes=["ps6"])
            s.op("act", lambda e: e.copy(out=uT2[c % 2][:], in_=ps[6][:, :].rearrange("p (a b) -> p a b", a=4)),
                 reads=["ps6"], writes=[f"uT{c % 2}"])
            if is_own(c):
                dve(lambda e: e.tensor_copy(out=uTf2[c % 2][:], in_=ps[6][:, :].rearrange("p (a b) -> p a b", a=4)),
                    ["ps6"], [f"uTf{c % 2}"])

        def P_b(c):
            dve(lambda e: e.tensor_tensor(out=lc_r[:], in0=LTr[:, :, 1], in1=car_r[:], op=ALU.mult), ["LTr", "car_r"], ["lc_r"])
            dve(lambda e: e.tensor_tensor(out=lc_i[:], in0=LTi[:, :, 1], in1=car_i[:], op=ALU.mult), ["LTi", "car_i"], ["lc_i"])
            dve(lambda e: e.tensor_tensor(out=lc_r[:], in0=lc_r[:], in1=lc_i[:], op=ALU.subtract), ["lc_r", "lc_i"], ["lc_r"])
            dve(lambda e: e.tensor_tensor(out=lc_i[:], in0=LTr[:, :, 1], in1=car_i[:], op=ALU.mult), ["LTr", "car_i", "lc_r"], ["lc_i"])
            dve(lambda e: e.tensor_tensor(out=lcw[:], in0=LTi[:, :, 1], in1=car_r[:], op=ALU.mult), ["LTi", "car_r"], ["lcw"])
            dve(lambda e: e.tensor_tensor(out=lc_i[:], in0=lc_i[:], in1=lcw[:], op=ALU.add), ["lc_i", "lcw"], ["lc_i"])

        def S1(c, fb):
            wb = (c * 4 + fb) % 2
            for ri in range(2):
                s.op("pe", lambda e, ri=ri: e.matmul(
                    ps[ri][:, :], lhsT=uT2[c % 2][:, fb, :], rhs=Bblk[fb][:, ri * 512:(ri + 1) * 512], start=True, stop=True),
                     reads=[f"uT{c % 2}", f"Bblk{fb}"], writes=[f"ps{ri}"])
            cs = slice(fb * 512, (fb + 1) * 512)
            tb = tW[wb]
            dve(lambda e: e.tensor_tensor(out=tb[0][:], in0=ps[0][:, :], in1=LIr[:, cs], op=ALU.mult), ["ps0", "LIr"], [f"tW0_{wb}"])
            dve(lambda e: e.scalar_tensor_tensor(out=tb[1][:], in0=ps[1][:, :], scalar=-1.0, in1=LIi[:, cs], op0=ALU.mult,
                                                 op1=ALU.mult), ["ps1", "LIi"], [f"tW1_{wb}"])
            dve(lambda e: e.tensor_tensor(out=tb[2][:], in0=ps[1][:, :], in1=LIr[:, cs], op=ALU.mult), ["ps1", "LIr"], [f"tW2_{wb}"])
            dve(lambda e: e.tensor_tensor(out=tb[3][:], in0=ps[0][:, :], in1=LIi[:, cs], op=ALU.mult), ["ps0", "LIi"], [f"tW3_{wb}"])

        def S2(c, fb):
            wb = (c * 4 + fb) % 2
            tb = tW[wb]
            for ri in range(2):
                for cbl in range(4):
                    for half in range(2):
                        ti = ri * 2 + half
                        s.op("pe", lambda e, ri=ri, cbl=cbl, ti=ti, half=half: e.matmul(
                            ps[2 + ri][:, cbl * 128:(cbl + 1) * 128], lhsT=tb[ti][:, cbl * 128:(cbl + 1) * 128], rhs=Tri[:],
                            start=(half == 0), stop=(half == 1)), reads=[f"tW{ti}_{wb}", "Tri"], writes=[f"ps{2 + ri}"])

        def S3(c, fb):
            wb = (c * 4 + fb) % 2
            cbs = slice(fb * 4, fb * 4 + 4)
            if is_own(c):
                sr, si = s_r[wb], s_i[wb]
                for cbl in range(4):
                    cb = fb * 4 + cbl
                    for (qi, pre, lc, lcres, LT_, LTres) in QUADS:
                        dve(lambda e, qi=qi, pre=pre, lc=lc, LT_=LT_, cb=cb, cbl=cbl: e.scalar_tensor_tensor(
                            out=q4[qi][:, cbl, :], in0=ps[pre][:, cbl * 128:(cbl + 1) * 128], scalar=lc[:, cb:cb + 1],
                            in1=LT_[:, cb, :], op0=ALU.add, op1=ALU.mult), [f"ps{pre}", lcres, LTres], [f"q4_{qi}"])
                s.op("pool", lambda e: e.tensor_tensor(out=sr[:], in0=q4[0][:], in1=q4[1][:], op=ALU.subtract),
                     reads=["q4_0", "q4_1"], writes=[f"s_r{wb}"])
                s.op("pool", lambda e: e.tensor_tensor(out=si[:], in0=q4[2][:], in1=q4[3][:], op=ALU.add),
                     reads=["q4_2", "q4_3"], writes=[f"s_i{wb}"])
                s.op("pool", lambda e: e.tensor_tensor(out=car_r[:, cbs], in0=q4[0][:, :, 127], in1=q4[1][:, :, 127],
                                                       op=ALU.subtract), reads=["q4_0", "q4_1"], writes=["car_r"])
                s.op("pool", lambda e: e.tensor_tensor(out=car_i[:, cbs], in0=q4[2][:, :, 127], in1=q4[3][:, :, 127],
                                                       op=ALU.add), reads=["q4_2", "q4_3"], writes=["car_i"])
            else:
                for (qi, pre, lc, lcres, LT_, LTres) in QUADS:
                    dve(lambda e, qi=qi, pre=pre, lc=lc: e.tensor_tensor(
                        out=c4[qi][:], in0=ps[pre][:, :].rearrange("p (a b) -> p a b", a=4)[:, :, 127], in1=lc[:, cbs],
                        op=ALU.add), [f"ps{pre}", lcres], [f"c4_{qi}"])
                    dve(lambda e, qi=qi, LT_=LT_: e.tensor_tensor(out=c4[qi][:], in0=c4[qi][:], in1=LT_[:, cbs, 127],
                                                                  op=ALU.mult), [f"c4_{qi}", LTres], [f"c4_{qi}"])
                dve(lambda e: e.tensor_tensor(out=car_r[:, cbs], in0=c4[0][:], in1=c4[1][:], op=ALU.subtract),
                    ["c4_0", "c4_1"], ["car_r"])
                dve(lambda e: e.tensor_tensor(out=car_i[:, cbs], in0=c4[2][:], in1=c4[3][:], op=ALU.add),
                    ["c4_2", "c4_3"], ["car_i"])

        def S4(c, fb):
            wb = (c * 4 + fb) % 2
            sr, si = s_r[wb], s_i[wb]
            k2 = 0
            for ri, sx, sxres in ((0, sr, f"s_r{wb}"), (1, si, f"s_i{wb}")):
                for cbl in range(4):
                    s.op("pe", lambda e, ri=ri, cbl=cbl, sx=sx, k2=k2: e.matmul(
                        ps[7][:, 0:128], lhsT=Cblk[:, fb, ri, cbl, :], rhs=sx[:, cbl, :], start=(k2 == 0), stop=(k2 == 7)),
                         reads=["Cblk", sxres], writes=["ps7"])
                    k2 += 1
            dve(lambda e: e.scalar_tensor_tensor(out=yf[:], in0=uTf2[c % 2][:, fb, :], scalar=dcol[:, fb:fb + 1],
                                                 in1=ps[7][:, 0:128], op0=ALU.mult, op1=ALU.add),
                [f"uTf{c % 2}", "dcol", "ps7"], ["yf"])
            s.op("act", lambda e: e.activation(out=ygf[:, fb, :], in_=yf[:], func=AF.Gelu_apprx_tanh), reads=["yf"],
                 writes=["ygf"])
            s.op("act", lambda e: e.activation(out=ygb[:, fb, :], in_=yf[:], func=AF.Gelu_apprx_tanh), reads=["yf"],
                 writes=["ygb"])

        def E(c):
            r0 = row0(c)
            for fo in range(4):
                for fb in range(4):
                    s.op("pe", lambda e, fo=fo, fb=fb: e.matmul(
                        ps[6][:, fo * 128:(fo + 1) * 128], lhsT=wglu[:, fb, fo * 128:(fo + 1) * 128], rhs=ygb[:, fb, :],
                        start=(fb == 0), stop=(fb == 3)), reads=["wglu", "ygb"], writes=["ps6"])
            s.op("act", lambda e: e.activation(out=sig[:], in_=ps[6][:, :].rearrange("p (a b) -> p a b", a=4),
                                               func=AF.Sigmoid), reads=["ps6"], writes=["sig"])
            yb = ysb[c % 2]
            ybres = f"ysb{c % 2}"
            s.op("pool", lambda e: e.tensor_tensor(out=yb[:], in0=ygf[:], in1=sig[:], op=ALU.mult),
                 reads=["ygf", "sig"], writes=[ybres])
            s.dma("sp", lambda e: e.dma_start(
                out=ycT_d[0:512, r0:r0 + 128].rearrange("(fo p) t -> p fo t", p=128), in_=yb[:]),
                  reads=[ybres], writes=["ycT_d"])

        pieces = [(c, fb) for c in range(n_chunks) for fb in range(4)]
        NQ = len(pieces)

        def S1x(q):
            c1, fb1 = pieces[q]
            if fb1 == 0:
                P_a(c1)
            S1(c1, fb1)

        S1x(0)
        if NQ > 1:
            S1x(1)
        S2(*pieces[0])
        for q in range(NQ):
            c, fb = pieces[q]
            if fb == 0:
                P_b(c)
            S3(c, fb)
            if q + 2 < NQ:
                S1x(q + 2)
            if q + 1 < NQ:
                S2(*pieces[q + 1])
            if q >= 1:
                cp, fbp = pieces[q - 1]
                if is_own(cp):
                    S4(cp, fbp)
                    if fbp == 3:
                        E(cp)
        cp, fbp = pieces[NQ - 1]
        if is_own(cp):
            S4(cp, fbp)
            E(cp)
        self.end_phase()

    def build_dev_ssm(self, n_pre=2, n_own=2, tables=False, stop=None):
        xp = self.dram_in("xp", [HALF, D])
        xo = self.dram_in("xo", [HALF, D])
        w_in = self.dram_in("w_in", [D, 2048])
        ssm = {n: self.dram_in(n, shp) for n, shp in (
            ("ssm_a_re", [32, 64]), ("ssm_a_im", [32, 64]), ("ssm_log_dt", [1, 32]), ("ssm_b_re", [32, 64, 16]),
            ("ssm_b_im", [32, 64, 16]), ("ssm_c_re", [32, 16, 64]), ("ssm_c_im", [32, 16, 64]), ("ssm_d", [1, 512]),
            ("ssm_w_glu", [512, 512]))}
        ycT_d = self.dram_out("ycT", [1024, HALF], BF16)
        dbg = self.dram_out("dbg", [128, 8192 + 2048]) if (tables or stop is not None) else None
        self.ps = [self.psum(f"ps{i}") for i in range(8)]
        self.setup_consts()
        self.ssm_phase(xp, xo, w_in, ssm, ycT_d, n_pre=n_pre, n_own=n_own, dbg_out=("tables", dbg) if tables else (("stop", stop, dbg) if stop is not None else None))
        self.finish()
        return self.nc

    def build_dev_tail(self, dbg=99):
        nt = self.n_own_tiles
        ntok = nt * 128
        x = self.dram_in("x", [ntok, D])
        ycat = self.dram_in("ycat", [ntok, D])
        w_out = self.dram_in("w_out", [D, D])
        ln = [self.dram_in(n, [1, D]) for n in ("ln1_g", "ln1_b", "ln2_g", "ln2_b")]
        w_q = self.dram_in("peer_w_q", [D, 2048])
        sub_keys = self.dram_in("peer_sub_keys", [8, 2, 128, 128])
        peer_u = self.dram_in("peer_u", [16384, D])
        peer_v = self.dram_in("peer_v", [16384, D])
        out = self.dram_out("out", [ntok, D])
        self.ps = [self.psum(f"ps{i}") for i in range(8)]
        self.setup_consts()
        ycT_d = self.nc.dram_tensor("ycT_d", [1024, ntok], BF16, kind="Internal").ap()
        uv_d = self.nc.dram_tensor("uv_d", [16384, 2048], BF16, kind="Internal").ap()
        self.prepass_uv(peer_u, peer_v, uv_d)
        self.begin_phase()
        yc = [self.sb(f"yc{i}", [128, 1024], F32) for i in range(2)]
        ycT = [self.sb(f"ycT{i}", [128, 8, 128], BF16) for i in range(2)]
        for it in range(nt):
            b = it % 2
            self.s.dma("sp", lambda e, it=it, b=b: e.dma_start(out=yc[b][:], in_=ycat[it * 128:(it + 1) * 128, :]),
                       writes=[f"yc{b}"])
            self.transpose_to_bf(yc[b], f"yc{b}", ycT[b], f"ycT{b}", [self.ps[4], self.ps[5]], ["ps4", "ps5"], nblk=8)
            self.s.dma("sp", lambda e, it=it, b=b: e.dma_start(
                out=ycT_d[:, it * 128:(it + 1) * 128].rearrange("(kc p) t -> p kc t", p=128), in_=ycT[b][:]),
                       reads=[f"ycT{b}"], writes=["ycT_d"])
        self.end_phase()
        self.begin_phase()
        self.setup_tail(w_out, *ln, w_q, sub_keys)
        self.alloc_tail2()
        self.tail2_all(nt, lambda it: x[it * 128:(it + 1) * 128, :], ycT_d, uv_d, lambda it: out[it * 128:(it + 1) * 128, :])
        self.end_phase()
        self.finish()
        return self.nc


    def build_full(self, with_ssm=True):
        xp = self.dram_in("xp", [HALF, D])
        xo = self.dram_in("xo", [HALF, D])
        gbias = self.dram_in("gbias", [128, 16])
        w_in = self.dram_in("w_in", [D, 2048])
        ssm = {n: self.dram_in(n, shp) for n, shp in (
            ("ssm_a_re", [32, 64]), ("ssm_a_im", [32, 64]), ("ssm_log_dt", [1, 32]), ("ssm_b_re", [32, 64, 16]),
            ("ssm_b_im", [32, 64, 16]), ("ssm_c_re", [32, 16, 64]), ("ssm_c_im", [32, 16, 64]), ("ssm_d", [1, 512]),
            ("ssm_w_glu", [512, 512]))}
        w_out = self.dram_in("w_out", [D, D])
        ln = [self.dram_in(n, [1, D]) for n in ("ln1_g", "ln1_b", "ln2_g", "ln2_b")]
        w_q = self.dram_in("peer_w_q", [D, 2048])
        sub_keys = self.dram_in("peer_sub_keys", [8, 2, 128, 128])
        peer_u = self.dram_in("peer_u", [16384, D])
        peer_v = self.dram_in("peer_v", [16384, D])
        out = self.dram_out("out", [HALF, D])
        ycT_d = self.nc.dram_tensor("ycT_d", [1024, HALF], BF16, kind="Internal").ap()
        self.ps = [self.psum(f"ps{i}") for i in range(8)]
        uv_d = self.nc.dram_tensor("uv_d", [16384, 2048], BF16, kind="Internal").ap()
        self.setup_consts()
        for hg in range(2):
            self.attn_pass(hg, xp, xo, w_in, gbias, ycT_d, n_qb=16,
                           bg_args=(peer_u, peer_v, uv_d) if hg == 0 else None)
        if with_ssm:
            self.ssm_phase(xp, xo, w_in, ssm, ycT_d)
        self.begin_phase()
        self.setup_tail(w_out, *ln, w_q, sub_keys)
        self.alloc_tail2()
        self.tail2_all(32, lambda it: xo[it * 128:(it + 1) * 128, :], ycT_d, uv_d, lambda it: out[it * 128:(it + 1) * 128, :])
        self.end_phase()
        self.finish()
        return self.nc

    def finish(self):
        with self.nc.Block() as block:
            self.s.emit(block)
        self.es.close()


_CACHE = {}


def kernel(**inputs):
    x = np.ascontiguousarray(np.asarray(inputs["x"], dtype=np.float32))
    if "nc" not in _CACHE:
        _CACHE["nc"] = Builder().build_full(with_ssm=hasattr(Builder, "ssm_phase"))
    nc = _CACHE["nc"]
    f = lambda k: np.ascontiguousarray(np.asarray(inputs[k], dtype=np.float32)[0])
    shared = {
        "w_in": f("w_in"), "ssm_a_re": f("ssm_a_re"), "ssm_a_im": f("ssm_a_im"),
        "ssm_log_dt": f("ssm_log_dt").reshape(1, 32), "ssm_b_re": f("ssm_b_re"), "ssm_b_im": f("ssm_b_im"),
        "ssm_c_re": f("ssm_c_re"), "ssm_c_im": f("ssm_c_im"), "ssm_d": f("ssm_d").reshape(1, 512),
        "ssm_w_glu": f("ssm_w_glu"), "w_out": f("w_out"),
        "ln1_g": f("ln1_g").reshape(1, D), "ln1_b": f("ln1_b").reshape(1, D),
        "ln2_g": f("ln2_g").reshape(1, D), "ln2_b": f("ln2_b").reshape(1, D),
        "peer_w_q": f("peer_w_q"), "peer_sub_keys": f("peer_sub_keys"), "peer_u": f("peer_u"), "peer_v": f("peer_v"),
    }
    maps = []
    for c in range(8):
        b, r = c // 2, c % 2
        xo = np.ascontiguousarray(x[b, r * HALF:(r + 1) * HALF])
        if r == 0:
            xp = np.zeros((HALF, D), np.float32)
            gb = np.full((128, 16), NEG, np.float32)
        else:
            xp = np.ascontiguousarray(x[b, 0:HALF])
            gb = np.zeros((128, 16), np.float32)
        m = dict(shared)
        m.update({"xp": xp, "xo": xo, "gbias": gb})
        maps.append(m)
    res = run_bass_kernel_spmd(nc, maps, core_ids=list(range(8)))
    out = np.empty((NB, SEQ, D), np.float32)
    for c in range(8):
        b, r = c // 2, c % 2
        out[b, r * HALF:(r + 1) * HALF] = res.results[c]["out"]
    return out
```

```python
import numpy as np
from contextlib import ExitStack
import ml_dtypes

import concourse.bass as bass
import concourse.mybir as mybir
from concourse.bass_utils import run_bass_kernel_spmd

F32 = mybir.dt.float32
BF16 = mybir.dt.bfloat16
U32 = mybir.dt.uint32
I32 = mybir.dt.int32
AF = mybir.ActivationFunctionType
ALU = mybir.AluOpType
AX = mybir.AxisListType

D = 1024
SEQ = 8192
NB = 4
HALF = 4096
ALPHA = 2.0 ** 0.25
LN_EPS = 1e-5
NEG = -1e30
GELU_C = 1.5957691216057308


class Sched:
    def __init__(self, nc, es, n_lanes=64):
        self.nc = nc
        self.engs = ["pe", "act", "dve", "pool", "sp"]
        self.esem = {e: es.enter_context(nc.semaphore(f"s_{e}")) for e in self.engs}
        self.lanes = [es.enter_context(nc.semaphore(f"l_{i}")) for i in range(n_lanes)]
        self.n_lanes = n_lanes
        self.lane_val = [0] * n_lanes
        n_hw = 24
        self.lane_pool = {"pool": list(range(n_hw, n_lanes))}
        self.lane_pool_default = list(range(0, n_hw))
        self.lane_next = {}
        self.cnt = {e: 0 for e in self.engs}
        self.ops = {e: [] for e in self.engs}
        self.waited = {e: {} for e in self.engs}
        self.res = {}

    def _deps(self, reads, writes):
        deps = []
        for r in reads:
            st = self.res.get(r)
            if st is not None and st[0] is not None:
                deps.append(st[0])
        for w in writes:
            st = self.res.get(w)
            if st is not None:
                if st[0] is not None:
                    deps.append(st[0])
                deps.extend(st[1].items())
        return deps

    def _commit(self, tok, reads, writes):
        key, val = tok
        for r in reads:
            st = self.res.setdefault(r, [None, {}])
            if st[1].get(key, 0) < val:
                st[1][key] = val
        for w in writes:
            self.res[w] = [tok, {}]

    def _filter(self, eng, deps):
        need = {}
        for key, val in deps:
            if eng == "pe" and key == ("e", "pe"):
                continue
            if self.waited[eng].get(key, 0) >= val:
                continue
            if need.get(key, 0) < val:
                need[key] = val
        out = []
        for key, val in need.items():
            self.waited[eng][key] = val
            sem = self.esem[key[1]] if key[0] == "e" else self.lanes[key[1]]
            out.append((sem, val))
        return out

    def op(self, eng, fn, reads=(), writes=()):
        if eng != "pe":
            locks = {r.split("_")[0] + "#lock" for r in list(reads) + list(writes) if r.startswith("ps")}
            if locks:
                writes = list(writes) + sorted(locks)
        deps = self._deps(reads, writes)
        waits = self._filter(eng, deps)
        self.cnt[eng] += 1
        tok = (("e", eng), self.cnt[eng])
        self.ops[eng].append((waits, fn, self.esem[eng], 1))
        self._commit(tok, reads, writes)
        return tok

    def dma(self, eng, fn, reads=(), writes=()):
        deps = self._deps(reads, writes)
        pool_ = self.lane_pool.get(eng, self.lane_pool_default)
        k = self.lane_next.get(eng if eng in self.lane_pool else "hw", 0)
        self.lane_next[eng if eng in self.lane_pool else "hw"] = (k + 1) % len(pool_)
        lane = pool_[k]
        prev = self.lane_val[lane]
        if prev > 0:
            deps.append((("l", lane), prev))
        waits = self._filter(eng, deps)
        self.lane_val[lane] = prev + 16
        tok = (("l", lane), prev + 16)
        self.ops[eng].append((waits, fn, self.lanes[lane], 16))
        self._commit(tok, reads, writes)
        return tok

    def barrier(self):
        snap_e = dict(self.cnt)
        snap_l = list(self.lane_val)
        for eng in self.engs:
            deps = [(("e", e2), v) for e2, v in snap_e.items() if v > 0 and e2 != eng]
            deps += [(("l", i), v) for i, v in enumerate(snap_l) if v > 0]
            if eng != "pe" and snap_e[eng] > 0:
                deps.append((("e", eng), snap_e[eng]))
            waits = self._filter(eng, deps)
            self.cnt[eng] += 1
            self.ops[eng].append((waits, lambda e: e.nop(), self.esem[eng], 1))

    def emit(self, block, final=True):
        sch = self
        ops = self.ops
        self.ops = {e: [] for e in self.engs}

        def run(e, h):
            for waits, fn, sem, inc in ops[e]:
                for s, v in waits:
                    h.wait_ge(s, v)
                fn(h).then_inc(sem, inc)

        @block.tensor
        def _(h):
            run("pe", h)

        @block.scalar
        def _(h):
            run("act", h)

        @block.vector
        def _(h):
            run("dve", h)

        @block.gpsimd
        def _(h):
            run("pool", h)

        @block.sync
        def _(h):
            run("sp", h)
            if not final:
                return
            for i in range(sch.n_lanes):
                if sch.lane_val[i] > 0:
                    h.wait_ge(sch.lanes[i], sch.lane_val[i])
            for e in sch.engs:
                if e != "sp" and sch.cnt[e] > 0:
                    h.wait_ge(sch.esem[e], sch.cnt[e])


class Builder:
    def __init__(self, n_own_tiles=32, dev_tail=False):
        self.n_own_tiles = n_own_tiles
        self.dev_tail = dev_tail
        self.nc = bass.Bass("TRN2", target_bir_lowering=False)
        self.es = ExitStack()
        self.s = Sched(self.nc, self.es)
        self._uid = 0
        self.cur = self.es

    def begin_phase(self):
        self._uid += 1
        self.cur = ExitStack()

    def end_phase(self):
        self.s.barrier()
        with self.nc.Block() as block:
            self.s.emit(block, final=False)
        self.cur.close()
        self.cur = self.es

    def bc_reg(self, e):
        if getattr(self, "_bc", None) is None:
            self._bc = e.to_reg(16383)
        return self._bc

    def sb(self, name, shape, dt):
        return self.cur.enter_context(self.nc.sbuf_tensor(f"{name}_p{self._uid}", list(shape), dt))

    def psum(self, name):
        return self.es.enter_context(self.nc.psum_tensor(name, [128, 512], F32))

    def dram_in(self, name, shape, dt=F32):
        return self.nc.dram_tensor(name, list(shape), dt, kind="ExternalInput").ap()

    def dram_out(self, name, shape, dt=F32):
        return self.nc.dram_tensor(name, list(shape), dt, kind="ExternalOutput").ap()

    def setup_consts(self):
        s = self.s
        self.colidx = self.sb("colidx", [128, 128], F32)
        self.rowidx = self.sb("rowidx", [128, 1], F32)
        self.ident = self.sb("ident", [128, 128], F32)
        self.ident_bf = self.sb("ident_bf", [128, 128], BF16)
        s.op("pool", lambda e: e.iota(self.colidx[:], pattern=[[1, 128]], base=0, channel_multiplier=0,
                                      allow_small_or_imprecise_dtypes=True), writes=["colidx"])
        s.op("pool", lambda e: e.iota(self.rowidx[:], pattern=[[0, 1]], base=0, channel_multiplier=1,
                                      allow_small_or_imprecise_dtypes=True), writes=["rowidx"])
        s.op("dve", lambda e: e.tensor_scalar(out=self.ident[:], in0=self.colidx[:], scalar1=self.rowidx[:, 0:1],
                                              scalar2=None, op0=ALU.is_equal),
             reads=["colidx", "rowidx"], writes=["ident"])
        s.op("dve", lambda e: e.tensor_copy(out=self.ident_bf[:], in_=self.ident[:]), reads=["ident"],
             writes=["ident_bf"])

    def transpose_to_bf(self, src, src_res, dst, dst_res, ps, ps_res, nblk=8, evac=("act", "dve")):
        s = self.s
        for g in range(0, nblk, 4):
            n = min(4, nblk - g)
            bank, bres = ps[(g // 4) % len(ps)], ps_res[(g // 4) % len(ps)]
            for j in range(n):
                jj = g + j
                s.op("pe", lambda e, jj=jj, j=j, bank=bank: e.transpose(
                    out=bank[:, j * 128:(j + 1) * 128], in_=src[:, jj * 128:(jj + 1) * 128], identity=self.ident[:]),
                     reads=[src_res, "ident"], writes=[bres])
            eng = evac[(g // 4) % len(evac)]
            if eng == "act":
                s.op("act", lambda e, g=g, n=n, bank=bank: e.copy(
                    out=dst[:, g:g + n, :], in_=bank[:, 0:n * 128].rearrange("p (j t) -> p j t", j=n)),
                     reads=[bres], writes=[dst_res])
            else:
                s.op(eng, lambda e, g=g, n=n, bank=bank: e.tensor_copy(
                    out=dst[:, g:g + n, :], in_=bank[:, 0:n * 128].rearrange("p (j t) -> p j t", j=n)),
                     reads=[bres], writes=[dst_res])

    def layernorm(self, r, r_res, gain, bias, out, out_res, tag):
        s = self.s
        st = self.ln_stats
        mv = self.ln_mv
        for c in range(2):
            s.op("dve", lambda e, c=c: e.bn_stats(out=st[:, c, :], in_=r[:, c * 512:(c + 1) * 512]),
                 reads=[r_res], writes=["ln_stats"])
        s.op("dve", lambda e: e.bn_aggr(out=mv[:, 0:2], in_=st[:].rearrange("p c k -> p (c k)")),
             reads=["ln_stats"], writes=["ln_mv"])
        s.op("act", lambda e: e.activation(out=mv[:, 2:3], in_=mv[:, 1:2], func=AF.Sqrt, bias=self.eps_t[:, 0:1],
                                           scale=1.0),
             reads=["ln_mv", "eps_t"], writes=["ln_mv2"])
        s.op("dve", lambda e: e.reciprocal(out=mv[:, 3:4], in_=mv[:, 2:3]), reads=["ln_mv2"], writes=["ln_mv3"])
        s.op("dve", lambda e: e.tensor_scalar(out=out[:], in0=r[:], scalar1=mv[:, 0:1], scalar2=mv[:, 3:4],
                                              op0=ALU.subtract, op1=ALU.mult),
             reads=[r_res, "ln_mv", "ln_mv3"], writes=[out_res])
        eng = getattr(self, "ln_affine_eng", "pool")
        s.op(eng, lambda e: e.tensor_tensor(out=out[:], in0=out[:], in1=gain[:], op=ALU.mult),
             reads=[out_res, "lnw"], writes=[out_res])
        s.op(eng, lambda e: e.tensor_tensor(out=out[:], in0=out[:], in1=bias[:], op=ALU.add),
             reads=[out_res, "lnw"], writes=[out_res])

    def setup_tail(self, w_out, ln1_g, ln1_b, ln2_g, ln2_b, w_q, sub_keys):
        s = self.s
        self.wout_bf = self.sb("wout_bf", [128, 8, 1024], BF16)
        self.wq_bf = self.sb("wq_bf", [128, 8, 2048], BF16)
        self.keysT = self.sb("keysT", [128, 16, 128], F32)
        self.S = self.sb("S", [128, 16, 128], F32)
        self.keys_nat = self.S[:].rearrange("p j d -> p (j d)")
        self.lnw = self.sb("lnw", [128, 4, 1024], F32)
        self.eps_t = self.sb("eps_t", [128, 1], F32)
        self.ln_stats = self.sb("ln_stats", [128, 2, 6], F32)
        self.ln_mv = self.sb("ln_mv", [128, 4], F32)
        s.op("dve", lambda e: e.memset(self.eps_t[:], LN_EPS), writes=["eps_t"])
        for kc in range(8):
            s.dma("pool", lambda e, kc=kc: e.dma_start(out=self.wout_bf[:, kc, :], in_=w_out[kc * 128:(kc + 1) * 128, :]),
                  writes=["wout_bf"])
            for hh in range(2):
                s.dma("pool", lambda e, kc=kc, hh=hh: e.dma_start(
                    out=self.wq_bf[:, kc, hh * 1024:(hh + 1) * 1024],
                    in_=w_q[kc * 128:(kc + 1) * 128, hh * 1024:(hh + 1) * 1024]), writes=["wq_bf"])
        for i, v in enumerate([ln1_g, ln1_b, ln2_g, ln2_b]):
            s.dma("sp", lambda e, i=i, v=v: e.dma_start(out=self.lnw[:, i, :], in_=v.partition_broadcast(128)),
                  writes=["lnw"])
        s.dma("sp", lambda e: e.dma_start(out=self.S[:], in_=sub_keys.rearrange("h c n d -> n (h c) d")),
              writes=["keys_nat"])
        self.transpose_to_bf(self.keys_nat, "keys_nat", self.keysT, "keysT", [self.ps[4], self.ps[5]],
                             ["ps4", "ps5"], nblk=16)

    def alloc_tail_bufs(self):
        self.xt = [self.sb(f"xt{i}", [128, 1024], F32) for i in range(2)]
        self.r1 = self.sb("r1", [128, 1024], F32)
        self.h1 = self.sb("h1", [128, 1024], F32)
        self.h1_bf = self.sb("h1_bf", [128, 1024], BF16)
        self.hT = self.sb("hT", [128, 8, 128], BF16)
        self.qT = self.sb("qT", [128, 16, 128], F32)
        self.S2 = self.sb("S2", [128, 1, 128], F32)
        self.m16 = self.sb("m16", [128, 16, 16], F32)
        self.i16 = self.sb("i16", [128, 16, 16], U32)
        self.i16f = self.sb("i16f", [128, 16, 16], F32)
        self.cand = self.sb("cand", [128, 8, 256], F32)
        self.cand2 = self.sb("cand2", [128, 1, 256], F32)
        self.tv = self.sb("tv", [128, 8, 16], F32)
        self.pos = self.sb("pos", [128, 8, 16], U32)
        self.pa = self.sb("pa", [128, 8, 16], U32)
        self.pb = self.sb("pb", [128, 8, 16], U32)
        self.paf = self.sb("paf", [128, 8, 16], F32)
        self.pbf = self.sb("pbf", [128, 8, 16], F32)
        self.oh = self.sb("oh", [128, 8, 16, 16], F32)
        self.i1s = self.sb("i1s", [128, 8, 16], F32)
        self.i2s = self.sb("i2s", [128, 8, 16], F32)
        self.eidx = self.sb("eidx", [128, 128], U32)
        self.eidf = self.sb("eidf", [128, 128], F32)
        self.nmax = self.sb("nmax", [128, 8], F32)
        self.gex = self.sb("gex", [128, 8, 16], F32)
        self.gsum = self.sb("gsum", [128, 8], F32)
        self.act_t = self.sb("act_t", [128, 128], F32)
        self.gl = self.sb("gl", [128, 128], F32)
        self.coef = self.sb("coef", [128, 128], F32)
        self.iota16 = self.sb("iota16", [128, 16], F32)
        NG = 6
        self.NG = NG
        self.ug = [self.sb(f"ug{i}", [128, 1024], BF16) for i in range(NG)]
        self.vg = [self.sb(f"vg{i}", [128, 1024], BF16) for i in range(NG)]
        self.dg = [self.sb(f"dg{i}", [128, 128], BF16) for i in range(NG)]
        self.junk = self.sb("junk", [128, 1024], BF16)
        self.r2 = self.sb("r2", [128, 1024], F32)
        self.o_t = [self.sb(f"o_t{i}", [128, 1024], F32) for i in range(2)]
        self.s.op("pool", lambda e: e.iota(self.iota16[:], pattern=[[1, 16]], base=0, channel_multiplier=0,
                                           allow_small_or_imprecise_dtypes=True), writes=["iota16"])

    def tail_tile(self, it, x_src, ycatT, ycatT_res, peer_u, peer_v, out_dst, dbg=99):
        s = self.s
        if dbg == 0:
            s.dma("sp", lambda e: e.dma_start(out=out_dst[:, 0:128], in_=self.keysT[:, 3, :]), reads=["keysT"])
            s.dma("sp", lambda e: e.dma_start(out=out_dst[:, 128:256], in_=self.lnw[:, 1, 0:128]), reads=["lnw"])
            return
        ps = self.ps
        xt = self.xt[it % 2]
        xres = f"xt{it % 2}"
        s.dma("sp", lambda e: e.dma_start(out=xt[:], in_=x_src), writes=[xres])
        for hh in range(2):
            for kc in range(8):
                s.op("pe", lambda e, hh=hh, kc=kc: e.matmul(
                    ps[hh][:, :], lhsT=ycatT[kc], rhs=self.wout_bf[:, kc, hh * 512:(hh + 1) * 512],
                    start=(kc == 0), stop=(kc == 7)), reads=[ycatT_res[kc], "wout_bf"], writes=[f"ps{hh}"])
        for hh in range(2):
            s.op("dve", lambda e, hh=hh: e.scalar_tensor_tensor(
                out=self.r1[:, hh * 512:(hh + 1) * 512], in0=xt[:, hh * 512:(hh + 1) * 512], scalar=ALPHA,
                in1=ps[hh][:, :], op0=ALU.mult, op1=ALU.add), reads=[xres, f"ps{hh}"], writes=["r1"])
        self.layernorm(self.r1, "r1", self.lnw[:, 0, :], self.lnw[:, 1, :], self.h1, "h1", "ln1")
        s.op("act", lambda e: e.copy(out=self.h1_bf[:], in_=self.h1[:]), reads=["h1"], writes=["h1_bf"])
        if dbg == 1:
            s.dma("sp", lambda e: e.dma_start(out=out_dst, in_=self.h1[:]), reads=["h1"])
            return
        self.transpose_to_bf(self.h1, "h1", self.hT, "hT", [ps[4], ps[5]], ["ps4", "ps5"], nblk=8)
        for j in range(16):
            bank, bres = (ps[6], "ps6") if (j // 4) % 2 == 0 else (ps[7], "ps7")
            for kc in range(8):
                s.op("pe", lambda e, j=j, kc=kc, bank=bank: e.matmul(
                    bank[:, (j % 4) * 128:(j % 4 + 1) * 128], lhsT=self.wq_bf[:, kc, j * 128:(j + 1) * 128],
                    rhs=self.hT[:, kc, :], start=(kc == 0), stop=(kc == 7)),
                     reads=["wq_bf", "hT"], writes=[bres])
            if j % 4 == 3:
                g = j - 3
                eng = "act" if (j // 4) % 2 == 0 else "dve"
                if eng == "act":
                    s.op("act", lambda e, g=g, bank=bank: e.copy(
                        out=self.qT[:, g:g + 4, :], in_=bank[:, :].rearrange("p (j t) -> p j t", j=4)),
                         reads=[bres], writes=["qT"])
                else:
                    s.op("dve", lambda e, g=g, bank=bank: e.tensor_copy(
                        out=self.qT[:, g:g + 4, :], in_=bank[:, :].rearrange("p (j t) -> p j t", j=4)),
                         reads=[bres], writes=["qT"])
        for j in range(16):
            bank, bres = (ps[4], "ps4") if (j // 4) % 2 == 0 else (ps[5], "ps5")
            s.op("pe", lambda e, j=j, bank=bank: e.matmul(
                bank[:, (j % 4) * 128:(j % 4 + 1) * 128], lhsT=self.qT[:, j, :], rhs=self.keysT[:, j, :],
                start=True, stop=True), reads=["qT", "keysT"], writes=[bres])
            if j % 4 == 3:
                g = j - 3
                s.op("act", lambda e, g=g, bank=bank: e.copy(
                    out=self.S[:, g:g + 4, :], in_=bank[:, :].rearrange("p (j t) -> p j t", j=4)),
                     reads=[bres], writes=[f"S{g // 4}"])
        for j in range(16):
            sr = f"S{j // 4}"
            s.op("dve", lambda e, j=j: e.max(out=self.m16[:, j, 0:8], in_=self.S[:, j, :]), reads=[sr], writes=["m16"])
            s.op("dve", lambda e, j=j: e.max_index(out=self.i16[:, j, 0:8], in_max=self.m16[:, j, 0:8],
                                                   in_values=self.S[:, j, :]), reads=[sr, "m16"], writes=["i16"])
            s.op("dve", lambda e, j=j: e.match_replace(out=self.S2[:, 0, :], in_to_replace=self.m16[:, j, 0:8],
                                                       in_values=self.S[:, j, :], imm_value=NEG),
                 reads=[sr, "m16"], writes=["S2"])
            s.op("dve", lambda e, j=j: e.max(out=self.m16[:, j, 8:16], in_=self.S2[:, 0, :]), reads=["S2"],
                 writes=["m16"])
            s.op("dve", lambda e, j=j: e.max_index(out=self.i16[:, j, 8:16], in_max=self.m16[:, j, 8:16],
                                                   in_values=self.S2[:, 0, :]), reads=["S2", "m16"], writes=["i16"])
        s.op("dve", lambda e: e.tensor_copy(out=self.i16f[:], in_=self.i16[:]), reads=["i16"], writes=["i16f"])
        m16v = self.m16[:].rearrange("p (h c) k -> p h c k", c=2)
        s.op("dve", lambda e: e.tensor_tensor(
            out=self.cand[:].rearrange("p h (a b) -> p h a b", a=16),
            in0=m16v[:, :, 0, :].unsqueeze(3).to_broadcast([128, 8, 16, 16]),
            in1=m16v[:, :, 1, :].unsqueeze(2).to_broadcast([128, 8, 16, 16]), op=ALU.add),
             reads=["m16"], writes=["cand"])
        for h in range(8):
            s.op("dve", lambda e, h=h: e.max(out=self.tv[:, h, 0:8], in_=self.cand[:, h, :]), reads=["cand"],
                 writes=["tv"])
            s.op("dve", lambda e, h=h: e.max_index(out=self.pos[:, h, 0:8], in_max=self.tv[:, h, 0:8],
                                                   in_values=self.cand[:, h, :]), reads=["cand", "tv"], writes=["pos"])
            s.op("dve", lambda e, h=h: e.match_replace(out=self.cand2[:, 0, :], in_to_replace=self.tv[:, h, 0:8],
                                                       in_values=self.cand[:, h, :], imm_value=NEG),
                 reads=["cand", "tv"], writes=["cand2"])
            s.op("dve", lambda e, h=h: e.max(out=self.tv[:, h, 8:16], in_=self.cand2[:, 0, :]), reads=["cand2"],
                 writes=["tv"])
            s.op("dve", lambda e, h=h: e.max_index(out=self.pos[:, h, 8:16], in_max=self.tv[:, h, 8:16],
                                                   in_values=self.cand2[:, 0, :]), reads=["cand2", "tv"],
                 writes=["pos"])
        s.op("dve", lambda e: e.tensor_single_scalar(out=self.pa[:], in_=self.pos[:], scalar=4,
                                                     op=ALU.logical_shift_right), reads=["pos"], writes=["pa"])
        s.op("dve", lambda e: e.tensor_single_scalar(out=self.pb[:], in_=self.pos[:], scalar=15,
                                                     op=ALU.bitwise_and), reads=["pos"], writes=["pb"])
        s.op("dve", lambda e: e.tensor_copy(out=self.paf[:], in_=self.pa[:]), reads=["pa"], writes=["paf"])
        s.op("dve", lambda e: e.tensor_copy(out=self.pbf[:], in_=self.pb[:]), reads=["pb"], writes=["pbf"])
        i16v = self.i16f[:].rearrange("p (h c) k -> p h c k", c=2)
        iota_b = self.iota16[:].unsqueeze(1).unsqueeze(1).to_broadcast([128, 8, 16, 16])
        for (pf, pres, c, dst, dres) in ((self.paf, "paf", 0, self.i1s, "i1s"), (self.pbf, "pbf", 1, self.i2s, "i2s")):
            s.op("dve", lambda e, pf=pf: e.tensor_tensor(
                out=self.oh[:], in0=pf[:].unsqueeze(3).to_broadcast([128, 8, 16, 16]), in1=iota_b, op=ALU.is_equal),
                 reads=[pres, "iota16"], writes=["oh"])
            s.op("dve", lambda e, c=c: e.tensor_tensor(
                out=self.oh[:], in0=self.oh[:], in1=i16v[:, :, c, :].unsqueeze(2).to_broadcast([128, 8, 16, 16]),
                op=ALU.mult), reads=["oh", "i16f"], writes=["oh"])
            s.op("dve", lambda e, dst=dst: e.tensor_reduce(out=dst[:], in_=self.oh[:], axis=AX.X, op=ALU.add),
                 reads=["oh"], writes=[dres])
        s.op("dve", lambda e: e.scalar_tensor_tensor(
            out=self.eidf[:].rearrange("p (h k) -> p h k", h=8), in0=self.i1s[:], scalar=128.0, in1=self.i2s[:],
            op0=ALU.mult, op1=ALU.add), reads=["i1s", "i2s"], writes=["eidf"])
        s.op("dve", lambda e: e.tensor_copy(out=self.eidx[:], in_=self.eidf[:]), reads=["eidf"], writes=["eidx"])
        if dbg == 2:
            s.dma("sp", lambda e: e.dma_start(out=out_dst[:, 0:128], in_=self.eidf[:]), reads=["eidf"])
            s.dma("sp", lambda e: e.dma_start(out=out_dst[:, 128:256], in_=self.tv[:].rearrange("p h k -> p (h k)")),
                  reads=["tv"])
            return
        s.op("dve", lambda e: e.tensor_tensor(
            out=self.gex[:], in0=self.tv[:], in1=self.tv[:, :, 0:1].to_broadcast([128, 8, 16]), op=ALU.subtract),
             reads=["tv"], writes=["gex"])
        s.op("act", lambda e: e.activation(out=self.gex[:], in_=self.gex[:], func=AF.Exp), reads=["gex"],
             writes=["gex"])
        s.op("dve", lambda e: e.tensor_reduce(out=self.gsum[:], in_=self.gex[:], axis=AX.X, op=ALU.add),
             reads=["gex"], writes=["gsum"])
        s.op("dve", lambda e: e.reciprocal(out=self.gsum[:], in_=self.gsum[:]), reads=["gsum"], writes=["gsum"])
        s.op("dve", lambda e: e.tensor_tensor(
            out=self.gex[:], in0=self.gex[:], in1=self.gsum[:].unsqueeze(2).to_broadcast([128, 8, 16]), op=ALU.mult),
             reads=["gex", "gsum"], writes=["gex"])
        NG = self.NG
        for sl in range(128):
            b = sl % NG
            s.dma("pool", lambda e, sl=sl, b=b: e.indirect_dma_start(
                out=self.ug[b][:], out_offset=None, in_=peer_u,
                in_offset=bass.IndirectOffsetOnAxis(ap=self.eidx[:, sl:sl + 1], axis=0),
                bounds_check=self.bc_reg(e), oob_is_err=False),
                  reads=["eidx"], writes=[f"ug{b}"])
            s.op("dve", lambda e, sl=sl, b=b: e.scalar_tensor_tensor(
                out=self.junk[:], in0=self.ug[b][:], scalar=1.0, in1=self.h1_bf[:], op0=ALU.mult, op1=ALU.mult,
                accum_out=self.act_t[:, sl:sl + 1]), reads=[f"ug{b}", "h1_bf"], writes=["junk", "act_t"])
        s.op("dve", lambda e: e.tensor_tensor(out=self.gl[:], in0=self.act_t[:], in1=self.act_t[:], op=ALU.mult),
             reads=["act_t"], writes=["gl"])
        s.op("dve", lambda e: e.tensor_scalar(out=self.gl[:], in0=self.gl[:], scalar1=0.044715, scalar2=1.0,
                                              op0=ALU.mult, op1=ALU.add), reads=["gl"], writes=["gl"])
        s.op("dve", lambda e: e.tensor_tensor(out=self.gl[:], in0=self.gl[:], in1=self.act_t[:], op=ALU.mult),
             reads=["gl", "act_t"], writes=["gl"])
        s.op("act", lambda e: e.activation(out=self.gl[:], in_=self.gl[:], func=AF.Sigmoid, scale=GELU_C),
             reads=["gl"], writes=["gl"])
        s.op("dve", lambda e: e.tensor_tensor(out=self.gl[:], in0=self.gl[:], in1=self.act_t[:], op=ALU.mult),
             reads=["gl", "act_t"], writes=["gl"])
        s.op("dve", lambda e: e.tensor_tensor(out=self.coef[:], in0=self.gl[:],
                                              in1=self.gex[:].rearrange("p h k -> p (h k)"), op=ALU.mult),
             reads=["gl", "gex"], writes=["coef"])
        if dbg == 3:
            s.dma("sp", lambda e: e.dma_start(out=out_dst[:, 0:128], in_=self.act_t[:]), reads=["act_t"])
            s.dma("sp", lambda e: e.dma_start(out=out_dst[:, 128:256], in_=self.coef[:]), reads=["coef"])
            return
        for sl in range(128):
            b = sl % NG
            s.dma("pool", lambda e, sl=sl, b=b: e.indirect_dma_start(
                out=self.vg[b][:], out_offset=None, in_=peer_v,
                in_offset=bass.IndirectOffsetOnAxis(ap=self.eidx[:, sl:sl + 1], axis=0),
                bounds_check=self.bc_reg(e), oob_is_err=False),
                  reads=["eidx"], writes=[f"vg{b}"])
            s.op("act", lambda e, sl=sl, b=b: e.activation(
                out=self.dg[b][:], in_=self.ident_bf[:], func=AF.Copy, scale=self.coef[:, sl:sl + 1]),
                 reads=["ident_bf", "coef"], writes=[f"dg{b}"])
            for hh in range(2):
                s.op("pe", lambda e, sl=sl, b=b, hh=hh: e.matmul(
                    ps[2 + hh][:, :], lhsT=self.dg[b][:], rhs=self.vg[b][:, hh * 512:(hh + 1) * 512],
                    start=(sl == 0), stop=(sl == 127)), reads=[f"dg{b}", f"vg{b}"], writes=[f"ps{2 + hh}"])
        for hh in range(2):
            s.op("dve", lambda e, hh=hh: e.scalar_tensor_tensor(
                out=self.r2[:, hh * 512:(hh + 1) * 512], in0=self.h1[:, hh * 512:(hh + 1) * 512], scalar=ALPHA,
                in1=ps[2 + hh][:, :], op0=ALU.mult, op1=ALU.add), reads=["h1", f"ps{2 + hh}"], writes=["r2"])
        ot = self.o_t[it % 2]
        ores = f"o_t{it % 2}"
        self.layernorm(self.r2, "r2", self.lnw[:, 2, :], self.lnw[:, 3, :], ot, ores, "ln2")
        s.dma("sp", lambda e: e.dma_start(out=out_dst, in_=ot[:]), reads=[ores])


    def prepass_uv(self, peer_u, peer_v, uv_d):
        s = self.s
        self.begin_phase()
        NBUF = 8
        bufs = [self.sb(f"uvp{i}", [128, 2048], BF16) for i in range(NBUF)]
        for c in range(128):
            k = c % NBUF
            b = bufs[k]
            s.dma("pool", lambda e, b=b, c=c: e.dma_start(out=b[:, 0:1024], in_=peer_u[c * 128:(c + 1) * 128, :]),
                  writes=[f"uvp{k}_u"])
            s.dma("pool", lambda e, b=b, c=c: e.dma_start(out=b[:, 1024:2048], in_=peer_v[c * 128:(c + 1) * 128, :]),
                  writes=[f"uvp{k}_v"])
            s.dma("sp", lambda e, b=b, c=c: e.dma_start(out=uv_d[c * 128:(c + 1) * 128, :], in_=b[:]),
                  reads=[f"uvp{k}_u", f"uvp{k}_v"], writes=["uv_d"])
        self.end_phase()

    def prepass_bg(self, peer_u, peer_v, uv_d):
        s = self.s
        NBUF = 8
        bufs = [self.sb(f"uvp{i}", [128, 2048], BF16) for i in range(NBUF)]
        items = []
        for c in range(128):
            def f(c=c):
                k = c % NBUF
                b = bufs[k]
                s.dma("pool", lambda e: e.dma_start(out=b[:, 0:1024], in_=peer_u[c * 128:(c + 1) * 128, :]),
                      writes=[f"uvp{k}_u"])
                s.dma("pool", lambda e: e.dma_start(out=b[:, 1024:2048], in_=peer_v[c * 128:(c + 1) * 128, :]),
                      writes=[f"uvp{k}_v"])
                s.dma("sp", lambda e: e.dma_start(out=uv_d[c * 128:(c + 1) * 128, :], in_=b[:]),
                      reads=[f"uvp{k}_u", f"uvp{k}_v"], writes=["uv_d"])
            items.append(f)
        return items

    def alloc_tail2(self):
        s = self.s
        self.xt = [self.sb("xt0", [128, 1024], F32)] * 2
        self.r1 = self.sb("r1", [128, 1024], F32)
        self.h1 = [self.sb(f"h1_{i}", [128, 1024], F32) for i in range(2)]
        self.h1b = [self.sb(f"h1b_{i}", [128, 1024], BF16) for i in range(2)]
        self.hT = self.sb("hT", [128, 8, 128], BF16)
        self.m16 = self.sb("m16", [128, 16, 16], F32)
        self.i16 = self.sb("i16", [128, 16, 16], U32)
        self.i16f = self.sb("i16f", [128, 16, 16], F32)
        self.cand = self.sb("cand", [128, 8, 256], F32)
        self.qT = self.cand[:].rearrange("p h (a b) -> p (h a) b", a=2)
        self.cand2 = self.sb("cand2", [128, 1, 256], F32)
        self.S2 = self.cand2[:, :, 0:128]
        self.tv = self.sb("tv", [128, 8, 16], F32)
        self.pos = self.sb("pos", [128, 8, 16], U32)
        self.pa = self.sb("pa", [128, 8, 16], U32)
        self.pb = self.sb("pb", [128, 8, 16], U32)
        self.paf = self.sb("paf", [128, 8, 16], F32)
        self.pbf = self.sb("pbf", [128, 8, 16], F32)
        self.oh = self.cand[:].rearrange("p h (a b) -> p h a b", a=16)
        self.i1s = self.sb("i1s", [128, 8, 16], F32)
        self.i2s = self.sb("i2s", [128, 8, 16], F32)
        self.eidx2 = [self.sb(f"eidx{i}", [128, 128], U32) for i in range(2)]
        self.eidf = self.sb("eidf", [128, 128], F32)
        self.gex2 = [self.sb(f"gex{i}", [128, 8, 16], F32) for i in range(2)]
        self.gsum = self.sb("gsum", [128, 8], F32)
        self.act_t = self.sb("act_t", [128, 128], F32)
        self.gl = self.sb("gl", [128, 128], F32)
        self.coef = self.sb("coef", [128, 128], F32)
        self.iota16 = self.sb("iota16", [128, 16], F32)
        self.GS = 4
        self.NUV = 20
        self.ln_affine_eng = "dve"
        self.pool_dots = False
        self.r2 = self.r1
        self.uvg = [self.sb(f"uvg{i}", [128, 2048], BF16) for i in range(self.NUV)]
        self.dg = [self.sb(f"dg{i}", [128, 128], BF16) for i in range(8)]
        self.ycT2 = [self.sb("ycT0", [128, 8, 128], BF16)] * 2
        s.op("pool", lambda e: e.iota(self.iota16[:], pattern=[[1, 16]], base=0, channel_multiplier=0,
                                      allow_small_or_imprecise_dtypes=True), writes=["iota16"])

    def tail2_prologue(self, it, x_src, ycT_d):
        s = self.s
        ps = self.ps
        b2 = it % 2
        xt, xres = self.xt[0], "xt0"
        ycT, ycres = self.ycT2[0], "ycT0"
        h1, h1res = self.h1[b2], f"h1_{b2}"
        h1b, h1bres = self.h1b[b2], f"h1b_{b2}"
        eidx, eres = self.eidx2[b2], f"eidx{b2}"
        gex, gres = self.gex2[b2], f"gex{b2}"
        S = self.S

        def a1():
            s.dma("sp", lambda e: e.dma_start(out=xt[:], in_=x_src), writes=[xres])
            s.dma("sp", lambda e: e.dma_start(
                out=ycT[:], in_=ycT_d[:, it * 128:(it + 1) * 128].rearrange("(kc p) t -> p kc t", p=128)),
                  reads=["ycT_d"], writes=[ycres])
            for hh in range(2):
                for kc in range(8):
                    s.op("pe", lambda e, hh=hh, kc=kc: e.matmul(
                        ps[6 + hh][:, :], lhsT=ycT[:, kc, :], rhs=self.wout_bf[:, kc, hh * 512:(hh + 1) * 512],
                        start=(kc == 0), stop=(kc == 7)), reads=[ycres, "wout_bf"], writes=[f"ps{6 + hh}"])
            for hh in range(2):
                s.op("dve", lambda e, hh=hh: e.scalar_tensor_tensor(
                    out=self.r1[:, hh * 512:(hh + 1) * 512], in0=xt[:, hh * 512:(hh + 1) * 512], scalar=ALPHA,
                    in1=ps[6 + hh][:, :], op0=ALU.mult, op1=ALU.add), reads=[xres, f"ps{6 + hh}"], writes=["r1"])

        def a2():
            self.layernorm(self.r1, "r1", self.lnw[:, 0, :], self.lnw[:, 1, :], h1, h1res, "ln1")
            s.op("act", lambda e: e.copy(out=h1b[:], in_=h1[:]), reads=[h1res], writes=[h1bres])

        def b1():
            self.transpose_to_bf(h1, h1res, self.hT, "hT", [ps[4], ps[5]], ["ps4", "ps5"], nblk=8)

        def b2_(j0):
            def f():
                for j in range(j0, j0 + 4):
                    bank, bres = (ps[6], "ps6") if (j // 4) % 2 == 0 else (ps[7], "ps7")
                    for kc in range(8):
                        s.op("pe", lambda e, j=j, kc=kc, bank=bank: e.matmul(
                            bank[:, (j % 4) * 128:(j % 4 + 1) * 128], lhsT=self.wq_bf[:, kc, j * 128:(j + 1) * 128],
                            rhs=self.hT[:, kc, :], start=(kc == 0), stop=(kc == 7)),
                             reads=["wq_bf", "hT"], writes=[bres])
                s.op("act", lambda e, g=j0, bank=bank: e.copy(
                    out=self.qT[:, g:g + 4, :], in_=bank[:, :].rearrange("p (j t) -> p j t", j=4)),
                     reads=[bres], writes=["qT"])
            return f

        def b3_(j0):
            def f():
                for j in range(j0, j0 + 4):
                    bank, bres = (ps[4], "ps4") if (j // 4) % 2 == 0 else (ps[5], "ps5")
                    s.op("pe", lambda e, j=j, bank=bank: e.matmul(
                        bank[:, (j % 4) * 128:(j % 4 + 1) * 128], lhsT=self.qT[:, j, :], rhs=self.keysT[:, j, :],
                        start=True, stop=True), reads=["qT", "keysT"], writes=[bres])
                s.op("act", lambda e, g=j0, bank=bank: e.copy(
                    out=S[:, g:g + 4, :], in_=bank[:, :].rearrange("p (j t) -> p j t", j=4)),
                     reads=[bres], writes=[f"S{j0 // 4}"])
            return f

        def c_(j0, j1):
            def f():
                for j in range(j0, j1):
                    sr = f"S{j // 4}"
                    s.op("dve", lambda e, j=j: e.max(out=self.m16[:, j, 0:8], in_=S[:, j, :]), reads=[sr], writes=["m16"])
                    s.op("dve", lambda e, j=j: e.max_index(out=self.i16[:, j, 0:8], in_max=self.m16[:, j, 0:8],
                                                           in_values=S[:, j, :]), reads=[sr, "m16"], writes=["i16"])
                    s.op("dve", lambda e, j=j: e.match_replace(out=self.S2[:, 0, :], in_to_replace=self.m16[:, j, 0:8],
                                                               in_values=S[:, j, :], imm_value=NEG),
                         reads=[sr, "m16"], writes=["S2"])
                    s.op("dve", lambda e, j=j: e.max(out=self.m16[:, j, 8:16], in_=self.S2[:, 0, :]), reads=["S2"],
                         writes=["m16"])
                    s.op("dve", lambda e, j=j: e.max_index(out=self.i16[:, j, 8:16], in_max=self.m16[:, j, 8:16],
                                                           in_values=self.S2[:, 0, :]), reads=["S2", "m16"], writes=["i16"])
            return f

        def d0():
            s.op("dve", lambda e: e.tensor_copy(out=self.i16f[:], in_=self.i16[:]), reads=["i16"], writes=["i16f"])
            m16v = self.m16[:].rearrange("p (h c) k -> p h c k", c=2)
            s.op("dve", lambda e: e.tensor_tensor(
                out=self.cand[:].rearrange("p h (a b) -> p h a b", a=16),
                in0=m16v[:, :, 0, :].unsqueeze(3).to_broadcast([128, 8, 16, 16]),
                in1=m16v[:, :, 1, :].unsqueeze(2).to_broadcast([128, 8, 16, 16]), op=ALU.add),
                 reads=["m16", "qT"], writes=["cand"])

        def d_(h0, h1_):
            def f():
                for h in range(h0, h1_):
                    s.op("dve", lambda e, h=h: e.max(out=self.tv[:, h, 0:8], in_=self.cand[:, h, :]), reads=["cand"],
                         writes=["tv"])
                    s.op("dve", lambda e, h=h: e.max_index(out=self.pos[:, h, 0:8], in_max=self.tv[:, h, 0:8],
                                                           in_values=self.cand[:, h, :]), reads=["cand", "tv"], writes=["pos"])
                    s.op("dve", lambda e, h=h: e.match_replace(out=self.cand2[:, 0, :], in_to_replace=self.tv[:, h, 0:8],
                                                               in_values=self.cand[:, h, :], imm_value=NEG),
                         reads=["cand", "tv"], writes=["cand2"])
                    s.op("dve", lambda e, h=h: e.max(out=self.tv[:, h, 8:16], in_=self.cand2[:, 0, :]), reads=["cand2"],
                         writes=["tv"])
                    s.op("dve", lambda e, h=h: e.max_index(out=self.pos[:, h, 8:16], in_max=self.tv[:, h, 8:16],
                                                           in_values=self.cand2[:, 0, :]), reads=["cand2", "tv"],
                         writes=["pos"])
            return f

        def e1():
            s.op("dve", lambda e: e.tensor_single_scalar(out=self.pa[:], in_=self.pos[:], scalar=4,
                                                         op=ALU.logical_shift_right), reads=["pos"], writes=["pa"])
            s.op("dve", lambda e: e.tensor_single_scalar(out=self.pb[:], in_=self.pos[:], scalar=15,
                                                         op=ALU.bitwise_and), reads=["pos"], writes=["pb"])
            s.op("dve", lambda e: e.tensor_copy(out=self.paf[:], in_=self.pa[:]), reads=["pa"], writes=["paf"])
            s.op("dve", lambda e: e.tensor_copy(out=self.pbf[:], in_=self.pb[:]), reads=["pb"], writes=["pbf"])
            s.op("dve", lambda e: e.tensor_tensor(
                out=gex[:], in0=self.tv[:], in1=self.tv[:, :, 0:1].to_broadcast([128, 8, 16]), op=ALU.subtract),
                 reads=["tv"], writes=[gres])
            s.op("act", lambda e: e.activation(out=gex[:], in_=gex[:], func=AF.Exp), reads=[gres], writes=[gres])

        def e2_(which):
            def f():
                i16v = self.i16f[:].rearrange("p (h c) k -> p h c k", c=2)
                iota_b = self.iota16[:].unsqueeze(1).unsqueeze(1).to_broadcast([128, 8, 16, 16])
                pf, pres, c, dst, dres = ((self.paf, "paf", 0, self.i1s, "i1s"), (self.pbf, "pbf", 1, self.i2s, "i2s"))[which]
                s.op("dve", lambda e: e.tensor_tensor(
                    out=self.oh[:], in0=pf[:].unsqueeze(3).to_broadcast([128, 8, 16, 16]), in1=iota_b, op=ALU.is_equal),
                     reads=[pres, "iota16", "cand"], writes=["oh"])
                s.op("dve", lambda e: e.tensor_tensor(
                    out=self.oh[:], in0=self.oh[:], in1=i16v[:, :, c, :].unsqueeze(2).to_broadcast([128, 8, 16, 16]),
                    op=ALU.mult), reads=["oh", "i16f"], writes=["oh"])
                s.op("dve", lambda e: e.tensor_reduce(out=dst[:], in_=self.oh[:], axis=AX.X, op=ALU.add),
                     reads=["oh"], writes=[dres])
            return f

        def e3():
            s.op("dve", lambda e: e.scalar_tensor_tensor(
                out=self.eidf[:].rearrange("p (h k) -> p h k", h=8), in0=self.i1s[:], scalar=128.0, in1=self.i2s[:],
                op0=ALU.mult, op1=ALU.add), reads=["i1s", "i2s"], writes=["eidf"])
            s.op("dve", lambda e: e.tensor_copy(out=eidx[:], in_=self.eidf[:]), reads=["eidf"], writes=[eres])
            s.op("dve", lambda e: e.tensor_reduce(out=self.gsum[:], in_=gex[:], axis=AX.X, op=ALU.add),
                 reads=[gres], writes=["gsum"])
            s.op("dve", lambda e: e.reciprocal(out=self.gsum[:], in_=self.gsum[:]), reads=["gsum"], writes=["gsum"])
            s.op("dve", lambda e: e.tensor_tensor(
                out=gex[:], in0=gex[:], in1=self.gsum[:].unsqueeze(2).to_broadcast([128, 8, 16]), op=ALU.mult),
                 reads=[gres, "gsum"], writes=[gres])

        return ([a1, a2, b1] + [b2_(j0) for j0 in (0, 4, 8, 12)] + [b3_(j0) for j0 in (0, 4, 8, 12)]
                + [c_(j0, j0 + 2) for j0 in range(0, 16, 2)] + [d0] + [d_(h0, h0 + 2) for h0 in range(0, 8, 2)]
                + [e1, e2_(0), e2_(1), e3])

    def tail2_gather(self, it, g, uv_d):
        s = self.s
        eidx, eres = self.eidx2[it % 2], f"eidx{it % 2}"
        for k in range(self.GS):
            sl = g * self.GS + k
            bi = (it * 128 + sl) % self.NUV
            s.dma("pool", lambda e, sl=sl, bi=bi: e.indirect_dma_start(
                out=self.uvg[bi][:], out_offset=None, in_=uv_d,
                in_offset=bass.IndirectOffsetOnAxis(ap=eidx[:, sl:sl + 1], axis=0),
                bounds_check=self.bc_reg(e), oob_is_err=False),
                  reads=[eres, "uv_d"], writes=[f"uvg{bi}"])

    def tail2_dots(self, it, g):
        s = self.s
        b2 = it % 2
        h1b, h1bres = self.h1b[b2], f"h1b_{b2}"
        GS = self.GS
        sl0 = g * GS
        for k in range(GS):
            sl = sl0 + k
            bi = (it * 128 + sl) % self.NUV
            if k < GS - 1:
                s.op("dve", lambda e, bi=bi: e.tensor_tensor(out=self.uvg[bi][:, 0:1024], in0=self.uvg[bi][:, 0:1024],
                                                             in1=h1b[:], op=ALU.mult),
                     reads=[f"uvg{bi}", h1bres], writes=[f"uvgp{bi}"])
                s.op("act", lambda e, sl=sl, bi=bi: e.activation(out=self.uvg[bi][:, 0:1024], in_=self.uvg[bi][:, 0:1024],
                                                                 func=AF.Copy, accum_out=self.act_t[:, sl:sl + 1]),
                     reads=[f"uvgp{bi}"], writes=[f"act_{sl}", f"uvgp{bi}"])
            else:
                s.op("dve", lambda e, sl=sl, bi=bi: e.scalar_tensor_tensor(
                    out=self.uvg[bi][:, 0:1024], in0=self.uvg[bi][:, 0:1024], scalar=1.0, in1=h1b[:], op0=ALU.mult,
                    op1=ALU.mult, accum_out=self.act_t[:, sl:sl + 1]),
                     reads=[f"uvg{bi}", h1bres], writes=[f"act_{sl}", f"uvgp{bi}"])

    def tail2_gelu_a(self, it, g):
        s = self.s
        GS = self.GS
        sl0 = g * GS
        cs = slice(sl0, sl0 + GS)
        glr = f"gl{g % 4}"
        s.op("act", lambda e: e.activation(out=self.gl[:, cs], in_=self.act_t[:, cs], func=AF.Gelu_apprx_tanh),
             reads=[f"act_{sl0 + k}" for k in range(GS)], writes=[glr])

    def tail2_finish(self, it, g):
        s = self.s
        ps = self.ps
        b2 = it % 2
        gex, gres = self.gex2[b2], f"gex{b2}"
        fb0 = 2 if b2 == 0 else 0
        GS = self.GS
        sl0 = g * GS
        cs = slice(sl0, sl0 + GS)
        glr = f"gl{g % 4}"
        cr = f"coef{g % 4}"
        s.op("dve", lambda e: e.tensor_tensor(out=self.coef[:, cs], in0=self.gl[:, cs],
                                              in1=gex[:].rearrange("p h k -> p (h k)")[:, cs], op=ALU.mult),
             reads=[glr, gres], writes=[cr])
        for k in range(GS):
            sl = sl0 + k
            bi = (it * 128 + sl) % self.NUV
            di = (it * 128 + sl) % 8
            s.op("act", lambda e, sl=sl, di=di: e.activation(
                out=self.dg[di][:], in_=self.ident_bf[:], func=AF.Copy, scale=self.coef[:, sl:sl + 1]),
                 reads=["ident_bf", cr], writes=[f"dg{di}"])
            for hh in range(2):
                s.op("pe", lambda e, sl=sl, bi=bi, hh=hh, di=di: e.matmul(
                    ps[fb0 + hh][:, :], lhsT=self.dg[di][:], rhs=self.uvg[bi][:, 1024 + hh * 512:1024 + (hh + 1) * 512],
                    start=(sl == 0), stop=(sl == 127)), reads=[f"dg{di}", f"uvg{bi}"], writes=[f"ps{fb0 + hh}"])

    def tail2_epilogue(self, it, out_dst):
        s = self.s
        ps = self.ps
        b2 = it % 2
        h1, h1res = self.h1[b2], f"h1_{b2}"
        fb0 = 2 if b2 == 0 else 0
        for hh in range(2):
            s.op("dve", lambda e, hh=hh: e.scalar_tensor_tensor(
                out=self.r2[:, hh * 512:(hh + 1) * 512], in0=h1[:, hh * 512:(hh + 1) * 512], scalar=ALPHA,
                in1=ps[fb0 + hh][:, :], op0=ALU.mult, op1=ALU.add), reads=[h1res, f"ps{fb0 + hh}"], writes=["r1"])
        ot = self.r2
        ores = "r1"
        self.layernorm(self.r2, "r1", self.lnw[:, 2, :], self.lnw[:, 3, :], ot, ores, "ln2")
        s.dma("sp", lambda e: e.dma_start(out=out_dst, in_=ot[:]), reads=[ores])

    def tail2_all(self, n_tiles, x_rows, ycT_d, uv_d, out_rows):
        NGRP = 128 // self.GS
        for st in self.tail2_prologue(0, x_rows(0), ycT_d):
            st()
        steps = [(it, g) for it in range(n_tiles) for g in range(NGRP)]
        n = len(steps)
        pending = []
        for k in range(-2, n + 1):
            if 0 <= k < n:
                it, g = steps[k]
                if g == 0 and it + 1 < n_tiles:
                    pending = self.tail2_prologue(it + 1, x_rows(it + 1), ycT_d)
            if 0 <= k + 2 < n:
                self.tail2_gather(*steps[k + 2], uv_d)
            if 0 <= k - 1 < n:
                itf, gf = steps[k - 1]
                self.tail2_finish(itf, gf)
                if gf == NGRP - 1:
                    self.tail2_epilogue(itf, out_rows(itf))
            if 0 <= k < n:
                self.tail2_gelu_a(*steps[k])
            if 0 <= k + 1 < n:
                self.tail2_dots(*steps[k + 1])
            if 0 <= k < n:
                it, g = steps[k]
                if pending and g <= NGRP - 4:
                    pending.pop(0)()
                    if g == NGRP - 4:
                        while pending:
                            pending.pop(0)()
        assert not pending

    def attn_pass(self, hg, xp, xo, w_in, gbias, ycT_d, n_qb=16, dbg=99, bg_args=None):
        s = self.s
        ps = self.ps
        self.begin_phase()
        KA = [self.sb(f"KA{i}", [96, 8192], BF16) for i in range(4)]
        QA = [self.sb(f"QA{i}", [96, 4096], BF16) for i in range(4)]
        VA = self.sb("VA", [128, 64, 4, 65], BF16)
        win = self.sb("win_a", [128, 8, 768], BF16)
        xt = [self.sb(f"axt{i}", [128, 1024], F32) for i in range(2)]
        xT4 = [self.sb(f"axT{i}", [128, 8, 512], BF16) for i in range(2)]
        kms = self.sb("kms", [64, 4, 32], F32)
        kmb = self.sb("kmb", [64, 4, 32], BF16)
        GB = self.sb("GB", [128, 16, 32], F32)
        gb_in = self.sb("gb_in", [128, 16], F32)
        Gt = self.sb("Gt", [128, 2, 32], F32)
        m8 = self.sb("m8", [128, 2, 8], F32)
        Gm = [self.sb(f"Gm{i}", [128, 96], F32) for i in range(2)]
        PT = [self.sb(f"PT{i}", [128, 256], BF16) for i in range(4)]
        rs = self.sb("rs", [65, 256], F32)
        rr = self.sb("rr", [64, 256], F32)
        oT = [self.sb(f"oT{i}", [64, 256], BF16) for i in range(2)]
        ones_t = self.sb("ones_t", [65, 64], F32)
        bg = self.prepass_bg(*bg_args) if bg_args is not None else []
        for part, c0 in ((0, 512), (1, 1024), (2, 1536)):
            for kc in range(8):
                s.dma("pool", lambda e, part=part, c0=c0, kc=kc: e.dma_start(
                    out=win[:, kc, part * 256:(part + 1) * 256],
                    in_=w_in[kc * 128:(kc + 1) * 128, c0 + hg * 256:c0 + (hg + 1) * 256]), writes=["win_a"])
        s.dma("sp", lambda e: e.dma_start(out=gb_in[:], in_=gbias), writes=["gb_in"])
        for h in range(4):
            s.op("dve", lambda e, h=h: e.tensor_copy(
                out=KA[h][64:96, :].rearrange("p (n k) -> p n k", k=256),
                in_=self.ident[64:96, 64:96].unsqueeze(2).to_broadcast([32, 32, 256])),
                 reads=["ident"], writes=[f"KAe{h}"])
        s.op("pool", lambda e: e.memset(VA[:, :, :, 64:65], 1.0), writes=["VAone"])
        s.op("pool", lambda e: e.memset(ones_t[:], 1.0), writes=["ones_t"])
        for i in range(2):
            s.op("pool", lambda e, i=i: e.memset(Gm[i][:], 0.0), writes=[f"Gm{i}"])
        s.op("dve", lambda e: e.memset(kms[:], 0.0), writes=["kms"])
        s.op("pool", lambda e: e.memset(GB[:], 0.0), writes=["GB"])
        for j in range(16):
            s.op("pool", lambda e, j=j: e.tensor_copy(out=GB[:, j, 0:16], in_=gb_in[:]), reads=["gb_in"], writes=["GB"])
            s.op("pool", lambda e, j=j: e.memset(GB[:, j, 16 + j:32], NEG), writes=["GB"])
        if dbg == 0:
            s.dma("sp", lambda e: e.dma_start(out=ycT_d[0:32, 0:4096], in_=KA[1][64:96, 0:4096]), reads=["KAe1"])
            s.dma("sp", lambda e: e.dma_start(out=ycT_d[128:256, 0:768], in_=win[:, 3, :]), reads=["win_a"])
            self.end_phase()
            return
        n_groups = 8 + (n_qb * 2 + 3) // 4
        if dbg == 2:
            n_groups = 1
        def T(g):
            own = g >= 8
            xsrc = xo if own else xp
            t0 = (g - 8) * 512 if own else g * 512
            xb = xT4[g % 2]
            xbres = f"axT{g % 2}"
            for ti in range(4):
                tile_i = g * 4 + ti
                xtb = xt[tile_i % 2]
                xres = f"axt{tile_i % 2}"
                s.dma("sp", lambda e, xtb=xtb, r0=t0 + ti * 128: e.dma_start(
                    out=xtb[:], in_=xsrc[r0:r0 + 128, :]), writes=[xres])
                if bg:
                    bg.pop(0)()
                for half in range(2):
                    bank, bres = (ps[4], "ps4") if half == 0 else (ps[5], "ps5")
                    for jj in range(4):
                        kc = half * 4 + jj
                        s.op("pe", lambda e, kc=kc, jj=jj, bank=bank, xtb=xtb: e.transpose(
                            out=bank[:, jj * 128:(jj + 1) * 128], in_=xtb[:, kc * 128:(kc + 1) * 128],
                            identity=self.ident[:]), reads=[xres, "ident"], writes=[bres])
                    if half == 0:
                        s.op("act", lambda e, bank=bank, ti=ti: e.copy(
                            out=xb[:, 0:4, ti * 128:(ti + 1) * 128],
                            in_=bank[:, :].rearrange("p (j t) -> p j t", j=4)), reads=[bres], writes=[xbres])
                    else:
                        s.op("dve", lambda e, bank=bank, ti=ti: e.tensor_copy(
                            out=xb[:, 4:8, ti * 128:(ti + 1) * 128],
                            in_=bank[:, :].rearrange("p (j t) -> p j t", j=4)), reads=[bres], writes=[xbres])

        def M(g):
            own = g >= 8
            t0 = (g - 8) * 512 if own else g * 512
            xb = xT4[g % 2]
            xbres = f"axT{g % 2}"
            kcol0 = g * 512
            for h in range(4):
                bank, bres = (ps[6], "ps6") if h % 2 == 0 else (ps[7], "ps7")
                for kc in range(8):
                    s.op("pe", lambda e, h=h, kc=kc, bank=bank: e.matmul(
                        bank[0:64, :], lhsT=win[:, kc, 256 + h * 64:256 + (h + 1) * 64], rhs=xb[:, kc, :],
                        start=(kc == 0), stop=(kc == 7)), reads=["win_a", xbres], writes=[bres])
                for n in range(2):
                    s.op("act", lambda e, h=h, bank=bank, n=n: e.activation(
                        out=KA[h][0:64, kcol0 + n * 256:kcol0 + (n + 1) * 256], in_=bank[0:64, n * 256:(n + 1) * 256],
                        func=AF.Copy, accum_out=kms[:, h, g * 2 + n:g * 2 + n + 1]),
                         reads=[bres, "kms"], writes=[f"KA{h}", f"kms_{h}_{g * 2 + n}"])
                if own:
                    bank, bres = (ps[2], "ps2") if h % 2 == 0 else (ps[3], "ps3")
                    for kc in range(8):
                        s.op("pe", lambda e, h=h, kc=kc, bank=bank: e.matmul(
                            bank[0:64, :], lhsT=win[:, kc, h * 64:(h + 1) * 64], rhs=xb[:, kc, :],
                            start=(kc == 0), stop=(kc == 7)), reads=["win_a", xbres], writes=[bres])
                    s.op("dve", lambda e, h=h, bank=bank: e.tensor_copy(
                        out=QA[h][0:64, t0:t0 + 512], in_=bank[0:64, :]), reads=[bres], writes=[f"QA{h}"])
            for ti in range(4):
                bank, bres = (ps[0], "ps0") if ti % 2 == 0 else (ps[1], "ps1")
                for kc in range(8):
                    s.op("pe", lambda e, kc=kc, ti=ti, bank=bank: e.matmul(
                        bank[:, 0:256], lhsT=xb[:, kc, ti * 128:(ti + 1) * 128], rhs=win[:, kc, 512:768],
                        start=(kc == 0), stop=(kc == 7)), reads=["win_a", xbres], writes=[bres])
                s.op("act", lambda e, ti=ti, bank=bank: e.copy(
                    out=VA[:, g * 4 + ti, :, 0:64], in_=bank[:, 0:256].rearrange("p (h d) -> p h d", h=4)),
                     reads=[bres], writes=["VA"])

        T(0)
        for g in range(n_groups):
            if g + 1 < n_groups:
                T(g + 1)
            M(g)
        kms_all = [f"kms_{h}_{i}" for h in range(4) for i in range(2 * n_groups)]
        s.op("act", lambda e: e.activation(out=kmb[:], in_=kms[:], func=AF.Copy, scale=1.0 / 256.0),
             reads=["kms"] + kms_all, writes=["kmb"])
        if dbg in (1, 2):
            for h in range(4):
                s.dma("sp", lambda e, h=h: e.dma_start(out=ycT_d[h * 64:(h + 1) * 64, :], in_=KA[h][0:64, 0:4096]),
                      reads=[f"KA{h}"])
                s.dma("sp", lambda e, h=h: e.dma_start(out=ycT_d[256 + h * 64:256 + (h + 1) * 64, :], in_=QA[h][0:64, :]),
                      reads=[f"QA{h}"])
            self.end_phase()
            return
        pairs = [(j, h) for j in range(n_qb) for h in range(4)]
        SB = ((ps[2], "ps2"), (ps[6], "ps6"), (ps[7], "ps7"))

        def mask(p):
            j, h = pairs[p]
            q0 = j * 256
            for qh in range(2):
                s.op("pe", lambda e, qh=qh: e.matmul(
                    ps[4][:, qh * 32:(qh + 1) * 32], lhsT=QA[h][0:64, q0 + qh * 128:q0 + (qh + 1) * 128],
                    rhs=kmb[:, h, :], start=True, stop=True), reads=[f"QA{h}", "kmb"], writes=["ps4"])
            s.op("dve", lambda e: e.tensor_tensor(
                out=Gt[:], in0=ps[4][:, 0:64].rearrange("p (a n) -> p a n", a=2),
                in1=GB[:, j, :].unsqueeze(1).to_broadcast([128, 2, 32]), op=ALU.add),
                 reads=["ps4", "GB"], writes=["Gt"])
            for qh in range(2):
                s.op("dve", lambda e, qh=qh: e.max(out=m8[:, qh, :], in_=Gt[:, qh, :]), reads=["Gt"], writes=["m8"])
            s.op("dve", lambda e: e.tensor_scalar(out=m8[:, :, 2:3], in0=m8[:, :, 2:3], scalar1=-1e29, scalar2=None,
                                                  op0=ALU.max), reads=["m8"], writes=["m8"])
            for qh in range(2):
                s.op("dve", lambda e, qh=qh: e.tensor_scalar(
                    out=Gm[qh][:, 64:96], in0=Gt[:, qh, :], scalar1=m8[:, qh, 2:3], scalar2=1.0,
                    op0=ALU.is_ge, op1=ALU.subtract), reads=["Gt", "m8"], writes=[f"Gm{qh}"])
                s.op("dve", lambda e, qh=qh: e.memset(Gm[qh][:, 64 + 16 + j:64 + 17 + j], 0.0),
                     writes=[f"Gm{qh}"])
                s.op("pe", lambda e, qh=qh: e.transpose(
                    out=ps[5][0:96, qh * 128:(qh + 1) * 128], in_=Gm[qh][:, 0:96], identity=self.ident[:]),
                     reads=[f"Gm{qh}", "ident"], writes=["ps5"])
            s.op("act", lambda e: e.activation(
                out=QA[h][64:96, q0:q0 + 256], in_=ps[5][64:96, 0:256], func=AF.Copy, scale=30000.0),
                 reads=["ps5"], writes=[f"QA{h}"])

        def norm(p):
            j, h = pairs[p]
            hglob = hg * 4 + h
            q0 = j * 256
            obank, ores_ = (ps[0], "ps0") if p % 2 == 0 else (ps[1], "ps1")
            s.op("dve", lambda e: e.tensor_copy(out=rs[64:65, :], in_=obank[64:65, 0:256]), reads=[ores_], writes=["rs"])
            s.op("pe", lambda e: e.matmul(ps[3][0:64, 0:256], lhsT=ones_t[64:65, :], rhs=rs[64:65, :], start=True,
                                          stop=True), reads=["ones_t", "rs"], writes=["ps3"])
            s.op("dve", lambda e: e.reciprocal(out=rr[:], in_=ps[3][0:64, 0:256]), reads=["ps3"], writes=["rr"])
            ob = oT[p % 2]
            obres = f"oT{p % 2}"
            s.op("dve", lambda e: e.tensor_tensor(out=ob[:], in0=obank[0:64, 0:256], in1=rr[:], op=ALU.mult),
                 reads=[ores_, "rr"], writes=[obres])
            s.dma("sp", lambda e: e.dma_start(
                out=ycT_d[512 + hglob * 64:512 + (hglob + 1) * 64, q0:q0 + 256], in_=ob[:]),
                  reads=[obres], writes=["ycT_d"])

        def inner(p):
            j, h = pairs[p]
            q0 = j * 256
            n_kt = 32 + 2 * j + 2
            obank, ores_ = (ps[0], "ps0") if p % 2 == 0 else (ps[1], "ps1")

            def issue_S(kt):
                sbank, sres = SB[kt % 3]
                s.op("pe", lambda e: e.matmul(
                    sbank[:, 0:256], lhsT=KA[h][0:96, kt * 128:(kt + 1) * 128], rhs=QA[h][0:96, q0:q0 + 256],
                    start=True, stop=True), reads=[f"KA{h}", f"KAe{h}", f"QA{h}"], writes=[sres])

            def issue_E(kt):
                sb_i = kt % 4
                sbank, sres = SB[kt % 3]
                s.op("act", lambda e: e.activation(out=PT[sb_i][:], in_=sbank[:, 0:256], func=AF.Exp, scale=0.125),
                     reads=[sres], writes=[f"PT{sb_i}"])
                if kt >= 32 + 2 * j:
                    ktl = kt - (32 + 2 * j)
                    s.op("pool", lambda e: e.affine_select(
                        out=PT[sb_i][:], in_=PT[sb_i][:], pattern=[[1, 256]], compare_op=ALU.is_ge, fill=0.0,
                        base=-ktl * 128, channel_multiplier=-1), reads=[f"PT{sb_i}"], writes=[f"PT{sb_i}"])

            def issue_PV(kt):
                sb_i = kt % 4
                s.op("pe", lambda e: e.matmul(
                    obank[0:65, 0:256], lhsT=VA[:, kt, h, :], rhs=PT[sb_i][:], start=(kt == 0),
                    stop=(kt == n_kt - 1)), reads=["VA", "VAone", f"PT{sb_i}"], writes=[ores_])

            issue_S(0)
            issue_S(1)
            if p > 0:
                norm(p - 1)
            for kt in range(n_kt):
                issue_E(kt)
                if kt + 2 < n_kt:
                    issue_S(kt + 2)
                issue_PV(kt)
                if kt == 12 and p + 1 < len(pairs):
                    mask(p + 1)

        mask(0)
        for p in range(len(pairs)):
            if bg:
                bg.pop(0)()
            inner(p)
        norm(len(pairs) - 1)
        while bg:
            bg.pop(0)()
        self.end_phase()

    def build_dev_attn(self, n_qb=2, hgs=(0,), dbg=99):
        xp = self.dram_in("xp", [HALF, D])
        xo = self.dram_in("xo", [HALF, D])
        w_in = self.dram_in("w_in", [D, 2048])
        gbias = self.dram_in("gbias", [128, 16])
        ycT_d = self.dram_out("ycT", [1024, HALF], BF16)
        self.ps = [self.psum(f"ps{i}") for i in range(8)]
        self.setup_consts()
        for hg in hgs:
            self.attn_pass(hg, xp, xo, w_in, gbias, ycT_d, n_qb=n_qb, dbg=dbg)
        self.finish()
        return self.nc


    def ssm_phase(self, xp, xo, w_in, ssm, ycT_d, n_pre=32, n_own=32, dbg_out=None):
        s = self.s
        ps = self.ps
        self.begin_phase()
        TWO_PI = 6.283185307179586
        MAGIC = 12582912.0
        C1 = 6.28125
        C2 = TWO_PI - C1
        PI_LO = 3.1415925
        LTr = self.sb("LTr", [128, 16, 128], F32)
        LTi = self.sb("LTi", [128, 16, 128], F32)
        LIr = self.sb("LIr", [128, 2048], F32)
        LIi = self.sb("LIi", [128, 2048], F32)
        Bblk = [self.sb(f"Bblk{i}", [128, 1024], BF16) for i in range(4)]
        Cblk = self.sb("Cblk", [128, 4, 2, 4, 128], F32)
        wglu = self.sb("wglu", [128, 4, 512], BF16)
        win = self.sb("win_u", [128, 8, 512], BF16)
        Tri = self.sb("Tri", [128, 128], BF16)
        dcol = self.sb("dcol", [128, 4], F32)
        car_r = self.sb("car_r", [128, 16], F32)
        car_i = self.sb("car_i", [128, 16], F32)
        lc_r = self.sb("lc_r", [128, 16], F32)
        lc_i = self.sb("lc_i", [128, 16], F32)
        prm16 = self.sb("prm16", [16, 3 * 128], F32)
        ldt16 = self.sb("ldt16", [16, 2], F32)
        prm = self.sb("prm", [128, 3, 16], F32)
        dtt = self.sb("dtt", [128, 16], F32)
        adr = self.sb("adr", [128, 16], F32)
        adi = self.sb("adi", [128, 16], F32)
        iot = self.sb("iot", [128, 128], F32)
        big = [self.sb(f"big{i}", [128, 16, 128], F32) for i in range(6)]
        kap = self.sb("kap", [128, 6, 16], F32)
        bc = [self.sb(f"bc{i}", [128, 16, 16], F32) for i in range(2)]
        bb = [self.sb(f"bb{i}", [128, 16, 16], F32) for i in range(2)]
        bt = [self.sb(f"bt{i}", [128, 16, 16], F32) for i in range(2)]
        bexp = [self.sb(f"bexp{i}", [128, 128], F32) for i in range(2)]
        ct2 = [self.sb(f"ct2_{i}", [128, 2, 64], F32) for i in range(2)]

        def dve(fn, r, w):
            s.op("dve", fn, reads=r, writes=w)

        def stop_here(k, tile_ap, res_name, n=128):
            if dbg_out is not None and dbg_out[0] == "stop" and dbg_out[1] == k:
                s.dma("sp", lambda e: e.dma_start(out=dbg_out[2][:, 0:n], in_=tile_ap), reads=[res_name])
                self.end_phase()
                return True
            return False

        s.dma("sp", lambda e: e.dma_start(out=prm16[:, 0:128], in_=ssm["ssm_a_re"].rearrange("(cb two) p -> cb (two p)", two=2)),
              writes=["prm16"])
        s.dma("sp", lambda e: e.dma_start(out=prm16[:, 128:256], in_=ssm["ssm_a_im"].rearrange("(cb two) p -> cb (two p)", two=2)),
              writes=["prm16"])
        s.dma("sp", lambda e: e.dma_start(out=ldt16[:], in_=ssm["ssm_log_dt"].rearrange("o (cb two) -> (o cb) two", two=2)),
              writes=["ldt16"])
        dve(lambda e: e.tensor_copy(out=prm16[:, 256:384].rearrange("p (two q) -> p two q", two=2),
                                    in_=ldt16[:].unsqueeze(2).to_broadcast([16, 2, 64])), ["ldt16"], ["prm16"])
        for i in range(3):
            s.op("pe", lambda e, i=i: e.transpose(out=ps[4][:, i * 16:(i + 1) * 16], in_=prm16[:, i * 128:(i + 1) * 128],
                                                  identity=self.ident[0:16, 0:16]), reads=["prm16", "ident"], writes=["ps4"])
        dve(lambda e: e.tensor_copy(out=prm[:], in_=ps[4][:, 0:48].rearrange("p (i c) -> p i c", i=3)), ["ps4"], ["prm"])
        s.op("act", lambda e: e.activation(out=dtt[:], in_=prm[:, 2, :], func=AF.Exp), reads=["prm"], writes=["dtt"])
        dve(lambda e: e.tensor_tensor(out=adr[:], in0=prm[:, 0, :], in1=dtt[:], op=ALU.mult), ["prm", "dtt"], ["adr"])
        dve(lambda e: e.tensor_tensor(out=adi[:], in0=prm[:, 1, :], in1=dtt[:], op=ALU.mult), ["prm", "dtt"], ["adi"])
        s.op("pool", lambda e: e.iota(iot[:], pattern=[[1, 128]], base=0, channel_multiplier=0,
                                      allow_small_or_imprecise_dtypes=True), writes=["iot"])
        dve(lambda e: e.tensor_scalar(out=Tri[:], in0=self.colidx[:], scalar1=self.rowidx[:, 0:1], scalar2=None,
                                      op0=ALU.is_ge), ["colidx", "rowidx"], ["Tri"])
        iot_b = iot[:].unsqueeze(1).to_broadcast([128, 16, 128])
        ANG, RED, TMP, SN, CS, EX = big
        dve(lambda e: e.tensor_tensor(out=ANG[:], in0=adi[:].unsqueeze(2).to_broadcast([128, 16, 128]), in1=iot_b,
                                      op=ALU.mult), ["adi", "iot"], ["ANG"])

        def reduce_sin(shift, dst, dres):
            if shift != 0.0:
                dve(lambda e: e.tensor_scalar(out=RED[:], in0=ANG[:], scalar1=shift, scalar2=None, op0=ALU.add), ["ANG"], ["RED"])
                src_, sres = RED, "RED"
            else:
                src_, sres = ANG, "ANG"
            dve(lambda e: e.tensor_scalar(out=TMP[:], in0=src_[:], scalar1=1.0 / TWO_PI, scalar2=MAGIC, op0=ALU.mult,
                                          op1=ALU.add), [sres], ["TMP"])
            dve(lambda e: e.tensor_scalar(out=TMP[:], in0=TMP[:], scalar1=-MAGIC, scalar2=None, op0=ALU.add), ["TMP"], ["TMP"])
            dve(lambda e: e.scalar_tensor_tensor(out=RED[:], in0=TMP[:], scalar=-C1, in1=src_[:], op0=ALU.mult, op1=ALU.add),
                ["TMP", sres], ["RED"])
            dve(lambda e: e.scalar_tensor_tensor(out=RED[:], in0=TMP[:], scalar=-C2, in1=RED[:], op0=ALU.mult, op1=ALU.add),
                ["TMP", "RED"], ["RED"])
            dve(lambda e: e.tensor_single_scalar(out=TMP[:], in_=RED[:], scalar=PI_LO, op=ALU.is_gt), ["RED"], ["TMP"])
            dve(lambda e: e.scalar_tensor_tensor(out=RED[:], in0=TMP[:], scalar=-TWO_PI, in1=RED[:], op0=ALU.mult, op1=ALU.add),
                ["TMP", "RED"], ["RED"])
            dve(lambda e: e.tensor_single_scalar(out=TMP[:], in_=RED[:], scalar=-PI_LO, op=ALU.is_lt), ["RED"], ["TMP"])
            dve(lambda e: e.scalar_tensor_tensor(out=RED[:], in0=TMP[:], scalar=TWO_PI, in1=RED[:], op0=ALU.mult, op1=ALU.add),
                ["TMP", "RED"], ["RED"])
            dve(lambda e: e.tensor_scalar(out=RED[:], in0=RED[:], scalar1=-PI_LO, scalar2=PI_LO, op0=ALU.max, op1=ALU.min),
                ["RED"], ["RED"])
            s.op("act", lambda e: e.activation(out=dst[:], in_=RED[:], func=AF.Sin), reads=["RED"], writes=[dres])

        reduce_sin(0.0, SN, "SN")
        reduce_sin(1.5707963267948966, CS, "CS")
        dve(lambda e: e.tensor_tensor(out=TMP[:], in0=adr[:].unsqueeze(2).to_broadcast([128, 16, 128]), in1=iot_b,
                                      op=ALU.mult), ["adr", "iot"], ["TMP"])
        s.op("act", lambda e: e.activation(out=EX[:], in_=TMP[:], func=AF.Exp), reads=["TMP"], writes=["EX"])
        dve(lambda e: e.tensor_tensor(out=LTr[:], in0=EX[:], in1=CS[:], op=ALU.mult), ["EX", "CS"], ["LTr"])
        dve(lambda e: e.tensor_tensor(out=LTi[:], in0=EX[:], in1=SN[:], op=ALU.mult), ["EX", "SN"], ["LTi"])
        s.op("act", lambda e: e.activation(out=EX[:], in_=TMP[:], func=AF.Exp, scale=-1.0), reads=["TMP", "LTr", "LTi"],
             writes=["EX"])
        dve(lambda e: e.tensor_tensor(out=CS[:], in0=EX[:], in1=CS[:], op=ALU.mult), ["EX", "CS"], ["CS"])
        dve(lambda e: e.scalar_tensor_tensor(out=SN[:], in0=SN[:], scalar=-1.0, in1=EX[:], op0=ALU.mult, op1=ALU.mult),
            ["EX", "SN"], ["SN"])
        for (src_, sres, dst, dres) in ((CS, "CS", LIr, "LIr"), (SN, "SN", LIi, "LIi")):
            for g4 in range(4):
                bank, bres = (ps[4], "ps4") if g4 % 2 == 0 else (ps[5], "ps5")
                for j in range(4):
                    cb = g4 * 4 + j
                    s.op("pe", lambda e, src_=src_, cb=cb, j=j, bank=bank: e.transpose(
                        out=bank[:, j * 128:(j + 1) * 128], in_=src_[:, cb, :], identity=self.ident[:]),
                         reads=[sres, "ident"], writes=[bres])
                s.op("act", lambda e, dst=dst, g4=g4, bank=bank: e.copy(out=dst[:, g4 * 512:(g4 + 1) * 512], in_=bank[:, :]),
                     reads=[bres], writes=[dres])
        K_X, K_Y, K_DEN, K_R, K_I, K_T = range(6)
        dve(lambda e: e.tensor_scalar(out=kap[:, K_X, :], in0=LTr[:, :, 1], scalar1=-1.0, scalar2=None, op0=ALU.add), ["LTr"], ["kap"])
        dve(lambda e: e.tensor_tensor(out=kap[:, K_DEN, :], in0=prm[:, 0, :], in1=prm[:, 0, :], op=ALU.mult), ["prm"], ["kap"])
        dve(lambda e: e.tensor_tensor(out=kap[:, K_T, :], in0=prm[:, 1, :], in1=prm[:, 1, :], op=ALU.mult), ["prm"], ["kap"])
        dve(lambda e: e.tensor_tensor(out=kap[:, K_DEN, :], in0=kap[:, K_DEN, :], in1=kap[:, K_T, :], op=ALU.add), ["kap"], ["kap"])
        dve(lambda e: e.reciprocal(out=kap[:, K_DEN, :], in_=kap[:, K_DEN, :]), ["kap"], ["kap"])
        dve(lambda e: e.tensor_tensor(out=kap[:, K_R, :], in0=kap[:, K_X, :], in1=prm[:, 0, :], op=ALU.mult), ["kap", "prm"], ["kap"])
        dve(lambda e: e.tensor_tensor(out=kap[:, K_T, :], in0=LTi[:, :, 1], in1=prm[:, 1, :], op=ALU.mult), ["LTi", "prm"], ["kap"])
        dve(lambda e: e.tensor_tensor(out=kap[:, K_R, :], in0=kap[:, K_R, :], in1=kap[:, K_T, :], op=ALU.add), ["kap"], ["kap"])
        dve(lambda e: e.tensor_tensor(out=kap[:, K_R, :], in0=kap[:, K_R, :], in1=kap[:, K_DEN, :], op=ALU.mult), ["kap"], ["kap"])
        dve(lambda e: e.tensor_tensor(out=kap[:, K_I, :], in0=LTi[:, :, 1], in1=prm[:, 0, :], op=ALU.mult), ["LTi", "prm"], ["kap"])
        dve(lambda e: e.tensor_tensor(out=kap[:, K_T, :], in0=kap[:, K_X, :], in1=prm[:, 1, :], op=ALU.mult), ["kap", "prm"], ["kap"])
        dve(lambda e: e.tensor_tensor(out=kap[:, K_I, :], in0=kap[:, K_I, :], in1=kap[:, K_T, :], op=ALU.subtract), ["kap"], ["kap"])
        dve(lambda e: e.tensor_tensor(out=kap[:, K_I, :], in0=kap[:, K_I, :], in1=kap[:, K_DEN, :], op=ALU.mult), ["kap"], ["kap"])
        for i, nm in enumerate(("ssm_b_re", "ssm_b_im")):
            s.dma("sp", lambda e, i=i, nm=nm: e.dma_start(
                out=bc[i][:], in_=ssm[nm].rearrange("(cb two) p h -> (two p) cb h", two=2)), writes=[f"bc{i}"])
        kr_b = kap[:, K_R, :].unsqueeze(2).to_broadcast([128, 16, 16])
        ki_b = kap[:, K_I, :].unsqueeze(2).to_broadcast([128, 16, 16])
        dve(lambda e: e.tensor_tensor(out=bt[0][:], in0=bc[0][:], in1=kr_b, op=ALU.mult), ["bc0", "kap"], ["bt0"])
        dve(lambda e: e.tensor_tensor(out=bt[1][:], in0=bc[1][:], in1=ki_b, op=ALU.mult), ["bc1", "kap"], ["bt1"])
        dve(lambda e: e.tensor_tensor(out=bb[0][:], in0=bt[0][:], in1=bt[1][:], op=ALU.subtract), ["bt0", "bt1"], ["bb0"])
        dve(lambda e: e.tensor_tensor(out=bt[0][:], in0=bc[1][:], in1=kr_b, op=ALU.mult), ["bc1", "kap", "bb0"], ["bt0"])
        dve(lambda e: e.tensor_tensor(out=bt[1][:], in0=bc[0][:], in1=ki_b, op=ALU.mult), ["bc0", "kap", "bb0"], ["bt1"])
        dve(lambda e: e.tensor_tensor(out=bb[1][:], in0=bt[0][:], in1=bt[1][:], op=ALU.add), ["bt0", "bt1"], ["bb1"])
        k = 0
        for fb in range(4):
            for ri in range(2):
                for cbl in range(4):
                    cb = fb * 4 + cbl
                    be = bexp[k % 2]
                    beres = f"bexp{k % 2}"
                    s.op("pool", lambda e, be=be: e.memset(be[:], 0.0), writes=[beres])
                    for two in range(2):
                        gl = 2 * cbl + two
                        s.op("pool", lambda e, be=be, two=two, gl=gl, ri=ri, cb=cb: e.tensor_copy(
                            out=be[two * 64:(two + 1) * 64, gl * 16:(gl + 1) * 16],
                            in_=bb[ri][two * 64:(two + 1) * 64, cb, :]), reads=[f"bb{ri}"], writes=[beres])
                    bank, bres = (ps[6], "ps6") if k % 2 == 0 else (ps[7], "ps7")
                    s.op("pe", lambda e, be=be, bank=bank: e.matmul(bank[:, 0:128], lhsT=be[:], rhs=self.ident[:], start=True,
                                                                    stop=True), reads=[beres, "ident"], writes=[bres])
                    s.op("act", lambda e, fb=fb, ri=ri, cbl=cbl, bank=bank: e.copy(
                        out=Bblk[fb][:, ri * 512 + cbl * 128:ri * 512 + (cbl + 1) * 128], in_=bank[:, 0:128]),
                         reads=[bres], writes=[f"Bblk{fb}"])
                    k += 1
        s.op("pool", lambda e: e.memset(Cblk[:], 0.0), writes=["Cblk"])
        k = 0
        for fb in range(4):
            for ri, nm in enumerate(("ssm_c_re", "ssm_c_im")):
                c2 = ct2[k % 2]
                c2res = f"ct2_{k % 2}"
                for dup in range(2):
                    s.dma("sp", lambda e, c2=c2, dup=dup, nm=nm, fb=fb: e.dma_start(
                        out=c2[:, dup, :], in_=ssm[nm][fb * 8:(fb + 1) * 8].rearrange("g ho p -> (g ho) p")),
                          writes=[c2res])
                bank, bres = (ps[4], "ps4") if k % 2 == 0 else (ps[5], "ps5")
                s.op("pe", lambda e, c2=c2, bank=bank: e.transpose(
                    out=bank[:, 0:128], in_=c2[:].rearrange("p a b -> p (a b)"), identity=self.ident[:]),
                     reads=[c2res, "ident"], writes=[bres])
                for cbl in range(4):
                    for two in range(2):
                        gl = 2 * cbl + two
                        s.op("act", lambda e, fb=fb, ri=ri, cbl=cbl, two=two, gl=gl, bank=bank: e.activation(
                            out=Cblk[two * 64:(two + 1) * 64, fb, ri, cbl, gl * 16:(gl + 1) * 16],
                            in_=bank[two * 64:(two + 1) * 64, gl * 16:(gl + 1) * 16], func=AF.Copy,
                            scale=(1.0 if ri == 0 else -1.0)), reads=[bres], writes=["Cblk"])
                k += 1
        d4 = self.sb("d4", [4, 128], F32)
        s.dma("sp", lambda e: e.dma_start(out=d4[:], in_=ssm["ssm_d"].rearrange("o (fb q) -> (o fb) q", q=128)), writes=["d4"])
        s.op("pe", lambda e: e.transpose(out=ps[4][:, 0:4], in_=d4[:], identity=self.ident[0:4, 0:4]), reads=["d4", "ident"],
             writes=["ps4"])
        dve(lambda e: e.tensor_copy(out=dcol[:], in_=ps[4][:, 0:4]), ["ps4"], ["dcol"])
        for kc in range(8):
            s.dma("pool", lambda e, kc=kc: e.dma_start(out=win[:, kc, :], in_=w_in[kc * 128:(kc + 1) * 128, 0:512]),
                  writes=["win_u"])
        for fb in range(4):
            s.dma("pool", lambda e, fb=fb: e.dma_start(out=wglu[:, fb, :], in_=ssm["ssm_w_glu"][fb * 128:(fb + 1) * 128, :]),
                  writes=["wglu"])
        dve(lambda e: e.memset(car_r[:], 0.0), [], ["car_r"])
        dve(lambda e: e.memset(car_i[:], 0.0), [], ["car_i"])
        if dbg_out is not None and dbg_out[0] == "tables":
            d = dbg_out[1]
            s.dma("sp", lambda e: e.dma_start(out=d[:, 0:2048], in_=LTr[:].rearrange("p a b -> p (a b)")), reads=["LTr"])
            s.dma("sp", lambda e: e.dma_start(out=d[:, 2048:4096], in_=LTi[:].rearrange("p a b -> p (a b)")), reads=["LTi"])
            s.dma("sp", lambda e: e.dma_start(out=d[:, 4096:6144], in_=LIr[:]), reads=["LIr"])
            s.dma("sp", lambda e: e.dma_start(out=d[:, 6144:8192], in_=LIi[:]), reads=["LIi"])
            self.end_phase()
            return
        if stop_here(0, dcol[:], "dcol", 4):
            return
        if stop_here(-1, wglu[:, 0, 0:64].bitcast(F32), "wglu", 32):
            return
        xt = [self.sb(f"sxt{i}", [128, 1024], F32) for i in range(2)]
        xT = [self.sb(f"sxT{i}", [128, 8, 128], BF16) for i in range(2)]
        uT = self.sb("uT", [128, 4, 128], BF16)
        uTf = self.sb("uTf", [128, 4, 128], F32)
        q4 = [self.sb(f"q4_{i}", [128, 4, 128], F32) for i in range(4)]
        s_r = [self.sb(f"s_r{i}", [128, 4, 128], F32) for i in range(2)]
        s_i = [self.sb(f"s_i{i}", [128, 4, 128], F32) for i in range(2)]
        c4 = [self.sb(f"c4_{i}", [128, 4], F32) for i in range(6)]
        yf = self.sb("yf", [128, 128], F32)
        yw = self.sb("yw", [128, 128], F32)
        ygf = self.sb("ygf", [128, 4, 128], F32)
        ygb = self.sb("ygb", [128, 4, 128], BF16)
        sig = self.sb("sig", [128, 4, 128], F32)
        ysb = [self.sb(f"ysb{i}", [128, 4, 128], BF16) for i in range(2)]
        uT2 = [uT, self.sb("uT_b", [128, 4, 128], BF16)]
        uTf2 = [uTf, self.sb("uTf_b", [128, 4, 128], F32)]
        lcw = self.sb("lcw", [128, 16], F32)
        tW = [[self.sb(f"tW{i}_{w}", [128, 512], BF16) for i in range(4)] for w in range(2)]
        n_chunks = n_pre + n_own
        QUADS = ((0, 2, lc_r, "lc_r", LTr, "LTr"), (1, 3, lc_i, "lc_i", LTi, "LTi"),
                 (2, 2, lc_r, "lc_r", LTi, "LTi"), (3, 3, lc_i, "lc_i", LTr, "LTr"))

        def is_own(c):
            return c >= n_pre

        def row0(c):
            return (c - n_pre) * 128 if is_own(c) else (32 - n_pre + c) * 128

        def P_a(c):
            xsrc = xo if is_own(c) else xp
            r0 = row0(c)
            xtb, xres = xt[c % 2], f"sxt{c % 2}"
            xTb, xTres = xT[c % 2], f"sxT{c % 2}"
            s.dma("sp", lambda e: e.dma_start(out=xtb[:], in_=xsrc[r0:r0 + 128, :]), writes=[xres])
            self.transpose_to_bf(xtb, xres, xTb, xTres, [ps[4], ps[5]], ["ps4", "ps5"], nblk=8, evac=("act", "act"))
            for fb in range(4):
                for kc in range(8):
                    s.op("pe", lambda e, fb=fb, kc=kc: e.matmul(
                        ps[6][:, fb * 128:(fb + 1) * 128], lhsT=win[:, kc, fb * 128:(fb + 1) * 128], rhs=xTb[:, kc, :],
                        start=(kc == 0), stop=(kc == 7)), reads=["win_u", xTres], writes=["ps6"])
            s.op("act", lambda e: e.copy(out=uT2[c % 2][:], in_=ps[6][:, :].rearrange("p (a b) -> p a b", a=4)),
                 reads=["ps6"], writes=[f"uT{c % 2}"])
            if is_own(c):
                s.op("act", lambda e: e.copy(out=uTf2[c % 2][:], in_=ps[6][:, :].rearrange("p (a b) -> p a b", a=4)),
                     reads=["ps6"], writes=[f"uTf{c % 2}"])

        def P_b(c):
            dve(lambda e: e.tensor_tensor(out=lc_r[:], in0=LTr[:, :, 1], in1=car_r[:], op=ALU.mult), ["LTr", "car_r"], ["lc_r"])
            dve(lambda e: e.tensor_tensor(out=lc_i[:], in0=LTi[:, :, 1], in1=car_i[:], op=ALU.mult), ["LTi", "car_i"], ["lc_i"])
            dve(lambda e: e.tensor_tensor(out=lc_r[:], in0=lc_r[:], in1=lc_i[:], op=ALU.subtract), ["lc_r", "lc_i"], ["lc_r"])
            dve(lambda e: e.tensor_tensor(out=lc_i[:], in0=LTr[:, :, 1], in1=car_i[:], op=ALU.mult), ["LTr", "car_i", "lc_r"], ["lc_i"])
            dve(lambda e: e.tensor_tensor(out=lcw[:], in0=LTi[:, :, 1], in1=car_r[:], op=ALU.mult), ["LTi", "car_r"], ["lcw"])
            dve(lambda e: e.tensor_tensor(out=lc_i[:], in0=lc_i[:], in1=lcw[:], op=ALU.add), ["lc_i", "lcw"], ["lc_i"])

        def S1(c, fb):
            wb = (c * 4 + fb) % 2
            for ri in range(2):
                s.op("pe", lambda e, ri=ri: e.matmul(
                    ps[ri][:, :], lhsT=uT2[c % 2][:, fb, :], rhs=Bblk[fb][:, ri * 512:(ri + 1) * 512], start=True, stop=True),
                     reads=[f"uT{c % 2}", f"Bblk{fb}"], writes=[f"ps{ri}"])
            cs = slice(fb * 512, (fb + 1) * 512)
            tb = tW[wb]
            dve(lambda e: e.tensor_tensor(out=tb[0][:], in0=ps[0][:, :], in1=LIr[:, cs], op=ALU.mult), ["ps0", "LIr"], [f"tW0_{wb}"])
            dve(lambda e: e.scalar_tensor_tensor(out=tb[1][:], in0=ps[1][:, :], scalar=-1.0, in1=LIi[:, cs], op0=ALU.mult,
                                                 op1=ALU.mult), ["ps1", "LIi"], [f"tW1_{wb}"])
            dve(lambda e: e.tensor_tensor(out=tb[2][:], in0=ps[1][:, :], in1=LIr[:, cs], op=ALU.mult), ["ps1", "LIr"], [f"tW2_{wb}"])
            dve(lambda e: e.tensor_tensor(out=tb[3][:], in0=ps[0][:, :], in1=LIi[:, cs], op=ALU.mult), ["ps0", "LIi"], [f"tW3_{wb}"])

        def S2(c, fb):
            wb = (c * 4 + fb) % 2
            tb = tW[wb]
            for ri in range(2):
                for cbl in range(4):
                    for half in range(2):
                        ti = ri * 2 + half
                        s.op("pe", lambda e, ri=ri, cbl=cbl, ti=ti, half=half: e.matmul(
                            ps[2 + ri][:, cbl * 128:(cbl + 1) * 128], lhsT=tb[ti][:, cbl * 128:(cbl + 1) * 128], rhs=Tri[:],
                            start=(half == 0), stop=(half == 1)), reads=[f"tW{ti}_{wb}", "Tri"], writes=[f"ps{2 + ri}"])

        def S3(c, fb):
            wb = (c * 4 + fb) % 2
            cbs = slice(fb * 4, fb * 4 + 4)
            if is_own(c):
                sr, si = s_r[wb], s_i[wb]
                for cbl in range(4):
                    cb = fb * 4 + cbl
                    for (qi, pre, lc, lcres, LT_, LTres) in QUADS:
                        dve(lambda e, qi=qi, pre=pre, lc=lc, LT_=LT_, cb=cb, cbl=cbl: e.scalar_tensor_tensor(
                            out=q4[qi][:, cbl, :], in0=ps[pre][:, cbl * 128:(cbl + 1) * 128], scalar=lc[:, cb:cb + 1],
                            in1=LT_[:, cb, :], op0=ALU.add, op1=ALU.mult), [f"ps{pre}", lcres, LTres], [f"q4_{qi}"])
                s.op("pool", lambda e: e.tensor_tensor(out=sr[:], in0=q4[0][:], in1=q4[1][:], op=ALU.subtract),
                     reads=["q4_0", "q4_1"], writes=[f"s_r{wb}"])
                s.op("pool", lambda e: e.tensor_tensor(out=si[:], in0=q4[2][:], in1=q4[3][:], op=ALU.add),
                     reads=["q4_2", "q4_3"], writes=[f"s_i{wb}"])
                s.op("pool", lambda e: e.tensor_tensor(out=car_r[:, cbs], in0=q4[0][:, :, 127], in1=q4[1][:, :, 127],
                                                       op=ALU.subtract), reads=["q4_0", "q4_1"], writes=["car_r"])
                s.op("pool", lambda e: e.tensor_tensor(out=car_i[:, cbs], in0=q4[2][:, :, 127], in1=q4[3][:, :, 127],
                                                       op=ALU.add), reads=["q4_2", "q4_3"], writes=["car_i"])
            else:
                for (qi, pre, lc, lcres, LT_, LTres) in QUADS:
                    dve(lambda e, qi=qi, pre=pre, lc=lc: e.tensor_tensor(
                        out=c4[qi][:], in0=ps[pre][:, :].rearrange("p (a b) -> p a b", a=4)[:, :, 127], in1=lc[:, cbs],
                        op=ALU.add), [f"ps{pre}", lcres], [f"c4_{qi}"])
                    dve(lambda e, qi=qi, LT_=LT_: e.tensor_tensor(out=c4[qi][:], in0=c4[qi][:], in1=LT_[:, cbs, 127],
                                                                  op=ALU.mult), [f"c4_{qi}", LTres], [f"c4_{qi}"])
                dve(lambda e: e.tensor_tensor(out=car_r[:, cbs], in0=c4[0][:], in1=c4[1][:], op=ALU.subtract),
                    ["c4_0", "c4_1"], ["car_r"])
                dve(lambda e: e.tensor_tensor(out=car_i[:, cbs], in0=c4[2][:], in1=c4[3][:], op=ALU.add),
                    ["c4_2", "c4_3"], ["car_i"])

        def S4(c, fb):
            wb = (c * 4 + fb) % 2
            sr, si = s_r[wb], s_i[wb]
            k2 = 0
            for ri, sx, sxres in ((0, sr, f"s_r{wb}"), (1, si, f"s_i{wb}")):
                for cbl in range(4):
                    s.op("pe", lambda e, ri=ri, cbl=cbl, sx=sx, k2=k2: e.matmul(
                        ps[7][:, 0:128], lhsT=Cblk[:, fb, ri, cbl, :], rhs=sx[:, cbl, :], start=(k2 == 0), stop=(k2 == 7)),
                         reads=["Cblk", sxres], writes=["ps7"])
                    k2 += 1
            dve(lambda e: e.scalar_tensor_tensor(out=yf[:], in0=uTf2[c % 2][:, fb, :], scalar=dcol[:, fb:fb + 1],
                                                 in1=ps[7][:, 0:128], op0=ALU.mult, op1=ALU.add),
                [f"uTf{c % 2}", "dcol", "ps7"], ["yf"])
            s.op("act", lambda e: e.activation(out=ygf[:, fb, :], in_=yf[:], func=AF.Gelu_apprx_tanh), reads=["yf"],
                 writes=["ygf"])
            s.op("act", lambda e: e.activation(out=ygb[:, fb, :], in_=yf[:], func=AF.Gelu_apprx_tanh), reads=["yf"],
                 writes=["ygb"])

        def E(c):
            r0 = row0(c)
            for fo in range(4):
                for fb in range(4):
                    s.op("pe", lambda e, fo=fo, fb=fb: e.matmul(
                        ps[6][:, fo * 128:(fo + 1) * 128], lhsT=wglu[:, fb, fo * 128:(fo + 1) * 128], rhs=ygb[:, fb, :],
                        start=(fb == 0), stop=(fb == 3)), reads=["wglu", "ygb"], writes=["ps6"])
            s.op("act", lambda e: e.activation(out=sig[:], in_=ps[6][:, :].rearrange("p (a b) -> p a b", a=4),
                                               func=AF.Sigmoid), reads=["ps6"], writes=["sig"])
            yb = ysb[c % 2]
            ybres = f"ysb{c % 2}"
            s.op("pool", lambda e: e.tensor_tensor(out=yb[:], in0=ygf[:], in1=sig[:], op=ALU.mult),
                 reads=["ygf", "sig"], writes=[ybres])
            s.dma("sp", lambda e: e.dma_start(
                out=ycT_d[0:512, r0:r0 + 128].rearrange("(fo p) t -> p fo t", p=128), in_=yb[:]),
                  reads=[ybres], writes=["ycT_d"])

        pieces = [(c, fb) for c in range(n_chunks) for fb in range(4)]
        NQ = len(pieces)

        def S1x(q):
            c1, fb1 = pieces[q]
            if fb1 == 0:
                P_a(c1)
            S1(c1, fb1)

        S1x(0)
        if NQ > 1:
            S1x(1)
        S2(*pieces[0])
        for q in range(NQ):
            c, fb = pieces[q]
            if fb == 0:
                P_b(c)
            S3(c, fb)
            if q + 2 < NQ:
                S1x(q + 2)
            if q + 1 < NQ:
                S2(*pieces[q + 1])
            if q >= 1:
                cp, fbp = pieces[q - 1]
                if is_own(cp):
                    S4(cp, fbp)
                    if fbp == 3:
                        E(cp)
        cp, fbp = pieces[NQ - 1]
        if is_own(cp):
            S4(cp, fbp)
            E(cp)
        self.end_phase()

    def build_dev_ssm(self, n_pre=2, n_own=2, tables=False, stop=None):
        xp = self.dram_in("xp", [HALF, D])
        xo = self.dram_in("xo", [HALF, D])
        w_in = self.dram_in("w_in", [D, 2048])
        ssm = {n: self.dram_in(n, shp) for n, shp in (
            ("ssm_a_re", [32, 64]), ("ssm_a_im", [32, 64]), ("ssm_log_dt", [1, 32]), ("ssm_b_re", [32, 64, 16]),
            ("ssm_b_im", [32, 64, 16]), ("ssm_c_re", [32, 16, 64]), ("ssm_c_im", [32, 16, 64]), ("ssm_d", [1, 512]),
            ("ssm_w_glu", [512, 512]))}
        ycT_d = self.dram_out("ycT", [1024, HALF], BF16)
        dbg = self.dram_out("dbg", [128, 8192 + 2048]) if (tables or stop is not None) else None
        self.ps = [self.psum(f"ps{i}") for i in range(8)]
        self.setup_consts()
        self.ssm_phase(xp, xo, w_in, ssm, ycT_d, n_pre=n_pre, n_own=n_own, dbg_out=("tables", dbg) if tables else (("stop", stop, dbg) if stop is not None else None))
        self.finish()
        return self.nc

    def build_dev_tail(self, dbg=99):
        nt = self.n_own_tiles
        ntok = nt * 128
        x = self.dram_in("x", [ntok, D])
        ycat = self.dram_in("ycat", [ntok, D])
        w_out = self.dram_in("w_out", [D, D])
        ln = [self.dram_in(n, [1, D]) for n in ("ln1_g", "ln1_b", "ln2_g", "ln2_b")]
        w_q = self.dram_in("peer_w_q", [D, 2048])
        sub_keys = self.dram_in("peer_sub_keys", [8, 2, 128, 128])
        peer_u = self.dram_in("peer_u", [16384, D])
        peer_v = self.dram_in("peer_v", [16384, D])
        out = self.dram_out("out", [ntok, D])
        self.ps = [self.psum(f"ps{i}") for i in range(8)]
        self.setup_consts()
        ycT_d = self.nc.dram_tensor("ycT_d", [1024, ntok], BF16, kind="Internal").ap()
        uv_d = self.nc.dram_tensor("uv_d", [16384, 2048], BF16, kind="Internal").ap()
        self.prepass_uv(peer_u, peer_v, uv_d)
        self.begin_phase()
        yc = [self.sb(f"yc{i}", [128, 1024], F32) for i in range(2)]
        ycT = [self.sb(f"ycT{i}", [128, 8, 128], BF16) for i in range(2)]
        for it in range(nt):
            b = it % 2
            self.s.dma("sp", lambda e, it=it, b=b: e.dma_start(out=yc[b][:], in_=ycat[it * 128:(it + 1) * 128, :]),
                       writes=[f"yc{b}"])
            self.transpose_to_bf(yc[b], f"yc{b}", ycT[b], f"ycT{b}", [self.ps[4], self.ps[5]], ["ps4", "ps5"], nblk=8)
            self.s.dma("sp", lambda e, it=it, b=b: e.dma_start(
                out=ycT_d[:, it * 128:(it + 1) * 128].rearrange("(kc p) t -> p kc t", p=128), in_=ycT[b][:]),
                       reads=[f"ycT{b}"], writes=["ycT_d"])
        self.end_phase()
        self.begin_phase()
        self.setup_tail(w_out, *ln, w_q, sub_keys)
        self.alloc_tail2()
        self.tail2_all(nt, lambda it: x[it * 128:(it + 1) * 128, :], ycT_d, uv_d, lambda it: out[it * 128:(it + 1) * 128, :])
        self.end_phase()
        self.finish()
        return self.nc


    def build_full(self, with_ssm=True):
        xp = self.dram_in("xp", [HALF, D])
        xo = self.dram_in("xo", [HALF, D])
        gbias = self.dram_in("gbias", [128, 16])
        w_in = self.dram_in("w_in", [D, 2048])
        ssm = {n: self.dram_in(n, shp) for n, shp in (
            ("ssm_a_re", [32, 64]), ("ssm_a_im", [32, 64]), ("ssm_log_dt", [1, 32]), ("ssm_b_re", [32, 64, 16]),
            ("ssm_b_im", [32, 64, 16]), ("ssm_c_re", [32, 16, 64]), ("ssm_c_im", [32, 16, 64]), ("ssm_d", [1, 512]),
            ("ssm_w_glu", [512, 512]))}
        w_out = self.dram_in("w_out", [D, D])
        ln = [self.dram_in(n, [1, D]) for n in ("ln1_g", "ln1_b", "ln2_g", "ln2_b")]
        w_q = self.dram_in("peer_w_q", [D, 2048])
        sub_keys = self.dram_in("peer_sub_keys", [8, 2, 128, 128])
        peer_u = self.dram_in("peer_u", [16384, D])
        peer_v = self.dram_in("peer_v", [16384, D])
        out = self.dram_out("out", [HALF, D])
        ycT_d = self.nc.dram_tensor("ycT_d", [1024, HALF], BF16, kind="Internal").ap()
        self.ps = [self.psum(f"ps{i}") for i in range(8)]
        uv_d = self.nc.dram_tensor("uv_d", [16384, 2048], BF16, kind="Internal").ap()
        self.setup_consts()
        for hg in range(2):
            self.attn_pass(hg, xp, xo, w_in, gbias, ycT_d, n_qb=16,
                           bg_args=(peer_u, peer_v, uv_d) if hg == 0 else None)
        if with_ssm:
            self.ssm_phase(xp, xo, w_in, ssm, ycT_d)
        self.begin_phase()
        self.setup_tail(w_out, *ln, w_q, sub_keys)
        self.alloc_tail2()
        self.tail2_all(32, lambda it: xo[it * 128:(it + 1) * 128, :], ycT_d, uv_d, lambda it: out[it * 128:(it + 1) * 128, :])
        self.end_phase()
        self.finish()
        return self.nc

    def finish(self):
        with self.nc.Block() as block:
            self.s.emit(block)
        self.es.close()


_CACHE = {}


def kernel(**inputs):
    x = np.ascontiguousarray(np.asarray(inputs["x"], dtype=np.float32))
    if "nc" not in _CACHE:
        _CACHE["nc"] = Builder().build_full(with_ssm=hasattr(Builder, "ssm_phase"))
    nc = _CACHE["nc"]
    f = lambda k: np.ascontiguousarray(np.asarray(inputs[k], dtype=np.float32)[0])
    shared = {
        "w_in": f("w_in"), "ssm_a_re": f("ssm_a_re"), "ssm_a_im": f("ssm_a_im"),
        "ssm_log_dt": f("ssm_log_dt").reshape(1, 32), "ssm_b_re": f("ssm_b_re"), "ssm_b_im": f("ssm_b_im"),
        "ssm_c_re": f("ssm_c_re"), "ssm_c_im": f("ssm_c_im"), "ssm_d": f("ssm_d").reshape(1, 512),
        "ssm_w_glu": f("ssm_w_glu"), "w_out": f("w_out"),
        "ln1_g": f("ln1_g").reshape(1, D), "ln1_b": f("ln1_b").reshape(1, D),
        "ln2_g": f("ln2_g").reshape(1, D), "ln2_b": f("ln2_b").reshape(1, D),
        "peer_w_q": f("peer_w_q"), "peer_sub_keys": f("peer_sub_keys"), "peer_u": f("peer_u"), "peer_v": f("peer_v"),
    }
    maps = []
    for c in range(8):
        b, r = c // 2, c % 2
        xo = np.ascontiguousarray(x[b, r * HALF:(r + 1) * HALF])
        if r == 0:
            xp = np.zeros((HALF, D), np.float32)
            gb = np.full((128, 16), NEG, np.float32)
        else:
            xp = np.ascontiguousarray(x[b, 0:HALF])
            gb = np.zeros((128, 16), np.float32)
        m = dict(shared)
        m.update({"xp": xp, "xo": xo, "gbias": gb})
        maps.append(m)
    res = run_bass_kernel_spmd(nc, maps, core_ids=list(range(8)))
    out = np.empty((NB, SEQ, D), np.float32)
    for c in range(8):
        b, r = c // 2, c % 2
        out[b, r * HALF:(r + 1) * HALF] = res.results[c]["out"]
    return out
```
